# Optimizing a Trainium2 kernel written in Bass

```python
import math
import jax
import jax.numpy as jnp
from jax import lax
import numpy as np

D_MODEL = 1024
BATCH = 4
SEQ = 4096
DEPTH = 4
DEC_BATCH = 32
DEC_SEQ = 8
PAST_LEN = 8192
PAGE_SIZE = 128

N_A_LAYERS = DEPTH // 2
N_B_LAYERS = DEPTH - N_A_LAYERS
CONV_W = 3
D_FF = 2816
N_HEADS = 16
HEAD_DIM = D_MODEL // N_HEADS
N_KV = 4
HPG = N_HEADS // N_KV
ROPE_DIM = HEAD_DIM // 4
ROPE_THETA = 500000.0
L_CMP = 32
L_SEL = 64
N_SEL = 16
WINDOW = 512
CMP_HIDDEN = 4 * HEAD_DIM
Q_BLOCK = 64
N_KV_ENTRIES = 4
N_WIN_ENTRIES = 2
EPS = 1e-6
NEG = -1e30
TINY = 1e-30
FORCE_SCORE = 1e4
POS_PAD = -(1 << 30)

kernel_name = "yoco_shortconv_nsa_macaron_step"


def rmsnorm(x, g):
    xf = x.astype(jnp.float32)
    y = xf * lax.rsqrt(jnp.mean(xf * xf, axis=-1, keepdims=True) + EPS)
    return (y * g.astype(jnp.float32)).astype(x.dtype)


def swiglu(x, w_in, w_out):
    g, u = jnp.split(x @ w_in, 2, axis=-1)
    return (jax.nn.silu(g) * u) @ w_out


def partial_rope(x, pos):
    half = ROPE_DIM // 2
    inv_freq = ROPE_THETA ** (-jnp.arange(half, dtype=jnp.float32) / half)
    ang = pos.astype(jnp.float32)[:, None] * inv_freq[None, :]
    cos = jnp.cos(ang)[:, None, :]
    sin = jnp.sin(ang)[:, None, :]
    xr = x[..., :ROPE_DIM].astype(jnp.float32)
    x1, x2 = xr[..., :half], xr[..., half:]
    rot = jnp.concatenate([x1 * cos - x2 * sin, x2 * cos + x1 * sin], axis=-1).astype(x.dtype)
    return jnp.concatenate([rot, x[..., ROPE_DIM:]], axis=-1)


def masked_softmax(s, mask):
    m = jnp.max(jnp.where(mask, s, NEG), axis=-1, keepdims=True)
    e = jnp.exp(jnp.where(mask, s - m, NEG))
    return e / jnp.maximum(jnp.sum(e, axis=-1, keepdims=True), TINY)


def short_conv(hn, prev, w_in, w_conv, w_out):
    t = hn.shape[1]
    b_gate, c_gate, xh = jnp.split(hn @ w_in, 3, axis=-1)
    u = c_gate * xh
    up = jnp.concatenate([prev.astype(u.dtype), u], axis=1)
    conv = w_conv[0] * up[:, 0:t]
    for j in range(1, CONV_W):
        conv = conv + w_conv[j] * up[:, j:j + t]
    return (b_gate * conv) @ w_out, up[:, t:]


def shared_kv_rows(h, pos, kv_norm, w_kv, k_norm):
    n, t, _ = h.shape
    p = (rmsnorm(h, kv_norm) @ w_kv).reshape(n, t, 6, N_KV, HEAD_DIM)
    k_sel = partial_rope(rmsnorm(p[:, :, 2], k_norm[1]), pos)
    k_win = partial_rope(rmsnorm(p[:, :, 4], k_norm[2]), pos)
    kv_rows = jnp.stack([p[:, :, 0], p[:, :, 1], k_sel, p[:, :, 3]], axis=2)
    win_rows = jnp.stack([k_win, p[:, :, 5]], axis=2)
    return kv_rows, win_rows


def compress_blocks(kv_full, k_norm_c, cmp_pe, cmp_w1, cmp_w2):
    n, t_pad = kv_full.shape[:2]
    nbc = t_pad // L_CMP
    rows = kv_full[:, :, :2].reshape(n, nbc, L_CMP, 2, N_KV, HEAD_DIM)
    rows = rows + jnp.swapaxes(cmp_pe, 0, 1)[:, :, None, :].astype(rows.dtype)
    hid = jax.nn.gelu(jnp.einsum('nclegd,eldf->ncegf', rows, cmp_w1))
    out = jnp.einsum('ncegf,efd->ncegd', hid, cmp_w2)
    return rmsnorm(out[:, :, 0], k_norm_c), out[:, :, 1]


def build_shared(h, pos, kv_past, win_prev, win_prev_pos, w):
    n, t, _ = h.shape
    kv_rows, win_rows = shared_kv_rows(h, pos, w['kv_norm'], w['w_kv'], w['k_norm'])
    total = kv_past.shape[1] + t
    t_pad = -(-total // L_SEL) * L_SEL
    kv_full = jnp.concatenate([kv_past.astype(kv_rows.dtype), kv_rows,
                               jnp.zeros((n, t_pad - total) + kv_rows.shape[2:], kv_rows.dtype)], axis=1)
    k_c, v_c = compress_blocks(kv_full, w['k_norm'][0], w['cmp_pe'], w['cmp_w1'], w['cmp_w2'])
    pad = WINDOW - win_prev.shape[1]
    win_all = jnp.concatenate([jnp.zeros((n, pad) + win_rows.shape[2:], win_rows.dtype),
                               win_prev.astype(win_rows.dtype), win_rows], axis=1)
    win_pos = jnp.concatenate([jnp.full((pad,), POS_PAD, jnp.int32), win_prev_pos, pos])
    new_win = win_all[:, win_all.shape[1] - min(WINDOW, total):]
    shared = (k_c, v_c, kv_full[:, :, 2], kv_full[:, :, 3], win_all[:, :, 0], win_all[:, :, 1], win_pos)
    return shared, kv_rows, new_win


def nsa_mixer(hn, q_pos, shared, w_qg, q_norm, w_o):
    k_c, v_c, k_sel, v_sel, win_k, win_v, win_pos = shared
    n, tq, _ = hn.shape
    hd = N_HEADS * HEAD_DIM
    qg = hn @ w_qg
    q = rmsnorm(qg[..., :hd].reshape(n, tq, N_HEADS, HEAD_DIM), q_norm)
    q_rot = partial_rope(q, q_pos)
    gates = jax.nn.sigmoid(qg[..., hd:].astype(jnp.float32)).reshape(n, tq, N_KV, HPG, 3)
    qb = math.gcd(Q_BLOCK, tq)
    nqb = tq // qb
    nbc = k_c.shape[1]
    nbs = k_sel.shape[1] // L_SEL
    n_sel = min(N_SEL, nbs)
    scale = HEAD_DIM ** -0.5

    def blockify(a):
        return jnp.swapaxes(a.reshape((n, nqb, qb) + a.shape[2:]), 0, 1)

    xs = (blockify(q.reshape(n, tq, N_KV, HPG, HEAD_DIM)), blockify(q_rot.reshape(n, tq, N_KV, HPG, HEAD_DIM)),
          blockify(gates), q_pos.reshape(nqb, qb), jnp.arange(nqb, dtype=jnp.int32) * qb)
    n_idx = jnp.arange(n)[:, None, None, None]
    g_idx = jnp.arange(N_KV)[None, None, :, None]
    cmp_end = (jnp.arange(nbc) + 1) * L_CMP - 1
    blk = jnp.arange(nbs)
    sel_off = jnp.arange(L_SEL)

    def block(args):
        qc, qr, gt, pos, j0 = args
        s = jnp.einsum('nqghd,ncgd->nqghc', qc, k_c).astype(jnp.float32) * scale
        vis = (cmp_end[None, :] <= pos[:, None])[None, :, None, None, :]
        p_c = masked_softmax(s, vis)
        o_c = jnp.einsum('nqghc,ncgd->nqghd', p_c.astype(v_c.dtype), v_c)
        imp = p_c.sum(axis=3).reshape(n, qb, N_KV, nbs, L_SEL // L_CMP).sum(-1)
        cur = (pos // L_SEL)[:, None]
        forced = ((blk == 0) | (blk == cur) | (blk == cur - 1))[None, :, None, :]
        valid = (blk * L_SEL <= pos[:, None])[None, :, None, :]
        score = jnp.where(valid, jnp.where(forced, FORCE_SCORE, imp), NEG)
        _, idx = lax.top_k(score, n_sel)
        tok = (idx[..., None] * L_SEL + sel_off).reshape(n, qb, N_KV, n_sel * L_SEL)
        ks = k_sel[n_idx, tok, g_idx]
        vs = v_sel[n_idx, tok, g_idx]
        s = jnp.einsum('nqghd,nqgsd->nqghs', qr, ks).astype(jnp.float32) * scale
        vis = (tok <= pos[None, :, None, None])[:, :, :, None, :]
        o_s = jnp.einsum('nqghs,nqgsd->nqghd', masked_softmax(s, vis).astype(vs.dtype), vs)
        kw = lax.dynamic_slice_in_dim(win_k, j0, WINDOW + qb, axis=1)
        vw = lax.dynamic_slice_in_dim(win_v, j0, WINDOW + qb, axis=1)
        pw = lax.dynamic_slice_in_dim(win_pos, j0, WINDOW + qb)
        d = pos[:, None] - pw[None, :]
        vis = ((d >= 0) & (d <= WINDOW))[None, :, None, None, :]
        s = jnp.einsum('nqghd,nkgd->nqghk', qr, kw).astype(jnp.float32) * scale
        o_w = jnp.einsum('nqghk,nkgd->nqghd', masked_softmax(s, vis).astype(vw.dtype), vw)
        o = gt[..., 0:1] * o_c.astype(jnp.float32) + gt[..., 1:2] * o_s.astype(jnp.float32) \
            + gt[..., 2:3] * o_w.astype(jnp.float32)
        return o.astype(qc.dtype)

    o = lax.map(block, xs)
    o = jnp.swapaxes(o, 0, 1).reshape(n, tq, hd)
    return o @ w_o


def trunk(x, pos, conv_prev, kv_past, win_prev, win_prev_pos, w):
    h = x
    conv_states = []
    shared = None
    kv_rows = None
    new_win = None
    for layer in range(DEPTH):
        h = h + 0.5 * swiglu(rmsnorm(h, w['ffn_a_norm'][layer]), w['ffn_a_w_in'][layer], w['ffn_a_w_out'][layer])
        hn = rmsnorm(h, w['mix_norm'][layer])
        if layer < N_A_LAYERS:
            y, st = short_conv(hn, conv_prev[layer], w['conv_w_in'][layer], w['conv_w'][layer], w['conv_w_out'][layer])
            conv_states.append(st)
        else:
            b = layer - N_A_LAYERS
            y = nsa_mixer(hn, pos, shared, w['nsa_w_qg'][b], w['nsa_q_norm'][b], w['nsa_w_o'][b])
        h = h + y
        h = h + 0.5 * swiglu(rmsnorm(h, w['ffn_b_norm'][layer]), w['ffn_b_w_in'][layer], w['ffn_b_w_out'][layer])
        if layer == N_A_LAYERS - 1:
            shared, kv_rows, new_win = build_shared(h, pos, kv_past, win_prev, win_prev_pos, w)
    return h, kv_rows, new_win, jnp.stack(conv_states)


def setup_inputs(seed: int = 0) -> dict:
    key = jax.random.key(seed)
    ks = jax.random.split(key, 32)
    f32 = jnp.float32

    def nrm(k, shape, scale):
        return jax.random.normal(k, shape, f32) * scale

    def gain(k, shape):
        return 1.0 + 0.01 * jax.random.normal(k, shape, f32)

    n_pages = PAST_LEN // PAGE_SIZE
    n_pool = (DEC_BATCH * n_pages * 5) // 4
    wb = min(WINDOW, PAST_LEN)
    hd = N_HEADS * HEAD_DIM
    page_table = jax.random.permutation(ks[5], n_pool)[:DEC_BATCH * n_pages].reshape(DEC_BATCH, n_pages).astype(jnp.int32)
    return {
        'x_prompt': nrm(ks[0], (BATCH, SEQ, D_MODEL), 1.0),
        'x_sample': nrm(ks[1], (DEC_BATCH, DEC_SEQ, D_MODEL), 1.0),
        'cache_kv': nrm(ks[2], (n_pool, PAGE_SIZE, N_KV_ENTRIES, N_KV, HEAD_DIM), 1.0),
        'cache_win': nrm(ks[3], (DEC_BATCH, wb, N_WIN_ENTRIES, N_KV, HEAD_DIM), 1.0),
        'state_conv': nrm(ks[4], (N_A_LAYERS, DEC_BATCH, CONV_W - 1, D_MODEL), 1.0),
        'page_table': page_table,
        'ffn_a_norm': gain(ks[6], (DEPTH, D_MODEL)),
        'ffn_a_w_in': nrm(ks[7], (DEPTH, D_MODEL, 2 * D_FF), D_MODEL ** -0.5),
        'ffn_a_w_out': nrm(ks[8], (DEPTH, D_FF, D_MODEL), D_FF ** -0.5),
        'mix_norm': gain(ks[9], (DEPTH, D_MODEL)),
        'ffn_b_norm': gain(ks[10], (DEPTH, D_MODEL)),
        'ffn_b_w_in': nrm(ks[11], (DEPTH, D_MODEL, 2 * D_FF), D_MODEL ** -0.5),
        'ffn_b_w_out': nrm(ks[12], (DEPTH, D_FF, D_MODEL), D_FF ** -0.5),
        'conv_w_in': nrm(ks[13], (N_A_LAYERS, D_MODEL, 3 * D_MODEL), D_MODEL ** -0.5),
        'conv_w': nrm(ks[14], (N_A_LAYERS, CONV_W, D_MODEL), CONV_W ** -0.5),
        'conv_w_out': nrm(ks[15], (N_A_LAYERS, D_MODEL, D_MODEL), D_MODEL ** -0.5),
        'kv_norm': gain(ks[16], (D_MODEL,)),
        'w_kv': nrm(ks[17], (D_MODEL, 6 * N_KV * HEAD_DIM), D_MODEL ** -0.5),
        'k_norm': gain(ks[18], (3, HEAD_DIM)),
        'cmp_pe': nrm(ks[19], (2, L_CMP, HEAD_DIM), 0.1),
        'cmp_w1': nrm(ks[20], (2, L_CMP, HEAD_DIM, CMP_HIDDEN), (L_CMP * HEAD_DIM) ** -0.5),
        'cmp_w2': nrm(ks[21], (2, CMP_HIDDEN, HEAD_DIM), CMP_HIDDEN ** -0.5),
        'nsa_w_qg': nrm(ks[22], (N_B_LAYERS, D_MODEL, hd + 3 * N_HEADS), D_MODEL ** -0.5),
        'nsa_q_norm': gain(ks[23], (N_B_LAYERS, HEAD_DIM)),
        'nsa_w_o': nrm(ks[24], (N_B_LAYERS, hd, D_MODEL), hd ** -0.5),
    }


def reference(x_prompt, x_sample, cache_kv, cache_win, state_conv, page_table,
              ffn_a_norm, ffn_a_w_in, ffn_a_w_out, mix_norm, ffn_b_norm, ffn_b_w_in, ffn_b_w_out,
              conv_w_in, conv_w, conv_w_out, kv_norm, w_kv, k_norm, cmp_pe, cmp_w1, cmp_w2,
              nsa_w_qg, nsa_q_norm, nsa_w_o):
    w = dict(ffn_a_norm=ffn_a_norm, ffn_a_w_in=ffn_a_w_in, ffn_a_w_out=ffn_a_w_out, mix_norm=mix_norm,
             ffn_b_norm=ffn_b_norm, ffn_b_w_in=ffn_b_w_in, ffn_b_w_out=ffn_b_w_out,
             conv_w_in=conv_w_in, conv_w=conv_w, conv_w_out=conv_w_out, kv_norm=kv_norm, w_kv=w_kv,
             k_norm=k_norm, cmp_pe=cmp_pe, cmp_w1=cmp_w1, cmp_w2=cmp_w2,
             nsa_w_qg=nsa_w_qg, nsa_q_norm=nsa_q_norm, nsa_w_o=nsa_w_o)
    bp, tp, _ = x_prompt.shape
    bs, ts, _ = x_sample.shape
    dt = x_prompt.dtype
    past_len = page_table.shape[1] * cache_kv.shape[1]
    y_prompt, kv_prompt, win_prompt, conv_prompt = trunk(
        x_prompt, jnp.arange(tp, dtype=jnp.int32),
        jnp.zeros((N_A_LAYERS, bp, CONV_W - 1, D_MODEL), dt),
        jnp.zeros((bp, 0, N_KV_ENTRIES, N_KV, HEAD_DIM), dt),
        jnp.zeros((bp, 0, N_WIN_ENTRIES, N_KV, HEAD_DIM), dt),
        jnp.zeros((0,), jnp.int32), w)
    kv_past = cache_kv[page_table].reshape(bs, past_len, N_KV_ENTRIES, N_KV, HEAD_DIM)
    wb = cache_win.shape[1]
    y_sample, kv_sample, win_sample, conv_sample = trunk(
        x_sample, past_len + jnp.arange(ts, dtype=jnp.int32), state_conv, kv_past, cache_win,
        past_len - wb + jnp.arange(wb, dtype=jnp.int32), w)
    return (y_prompt, y_sample, kv_prompt, kv_sample, win_prompt, win_sample, conv_prompt, conv_sample)
```

```python
import numpy as np
import concourse.bass as bass
import concourse.mybir as mybir
from concourse.bass_utils import run_bass_kernel_spmd

F32, BF16, I32 = mybir.dt.float32, mybir.dt.bfloat16, mybir.dt.int32
AF = mybir.ActivationFunctionType
ALU = mybir.AluOpType
AX = mybir.AxisListType

D = 1024
KC = 8
DFF = 2816
FC = 22
HD = 64
NKV = 4
NH = 16
EPS = 1e-6
ROPE_THETA = 500000.0
WSLOT = 4096
NWSLOT = 3
HT_ALL = tuple(("hT", c) for c in range(KC))
HN_ALL = tuple(("hn", c) for c in range(KC))


class Eng:
    def __init__(self, nc, name, e):
        self.name = name
        self.e = e
        self.sem = nc.alloc_semaphore("s_" + name)
        self.cnt = 0
        self.seen = {}


class KB:
    def __init__(self, nc):
        self.nc = nc
        self.pe = Eng(nc, "pe", nc.tensor)
        self.act = Eng(nc, "act", nc.scalar)
        self.dve = Eng(nc, "dve", nc.vector)
        self.pool = Eng(nc, "pool", nc.gpsimd)
        self.sp = Eng(nc, "sp", nc.sync)
        self.lw = {}
        self.rd = {}
        self.dsems = {}
        self.sems = {}
        self.ps_i = 0
        self.w_i = 0
        self.n_inst = 0

    def _deps(self, reads, writes):
        evs = []
        for r in reads:
            if r in self.lw:
                evs.append(self.lw[r])
        for w in writes:
            if w in self.lw:
                evs.append(self.lw[w])
            evs.extend(self.rd.get(w, {}).values())
        return evs

    def _wait(self, E, evs, skip_self=False):
        best = {}
        for sem, val in evs:
            k = id(sem)
            if k not in best or best[k][1] < val:
                best[k] = (sem, val)
        for k, (sem, val) in best.items():
            if skip_self and sem is E.sem:
                continue
            if E.seen.get(k, 0) >= val:
                continue
            E.e.wait_ge(sem, val)
            self.n_inst += 1
            E.seen[k] = val

    def _commit(self, ev, reads, writes):
        for w in writes:
            self.lw[w] = ev
            self.rd[w] = {}
        for r in reads:
            d = self.rd.setdefault(r, {})
            k = id(ev[0])
            if k not in d or d[k][1] < ev[1]:
                d[k] = ev

    def op(self, E, fn, reads=(), writes=(), skip_self=False):
        self._wait(E, self._deps(reads, writes), skip_self)
        inst = fn()
        self.n_inst += 1
        E.cnt += 1
        inst.then_inc(E.sem, 1)
        self._commit((E.sem, E.cnt), reads, writes)
        return inst

    def mm(self, out, pairs, reads, writes, start=True, stop=True):
        E = self.pe
        self._wait(E, self._deps(reads, writes), skip_self=True)
        n = len(pairs)
        inst = None
        for i, (l, r) in enumerate(pairs):
            inst = self.nc.tensor.matmul(out, lhsT=l, rhs=r, start=(start and i == 0), stop=(stop and i == n - 1))
            self.n_inst += 1
        E.cnt += 1
        inst.then_inc(E.sem, 1)
        self._commit((E.sem, E.cnt), reads, writes)

    def tr(self, outs_ins, ident, reads, writes):
        E = self.pe
        self._wait(E, self._deps(reads, writes), skip_self=True)
        inst = None
        for o, i_ in outs_ins:
            inst = self.nc.tensor.transpose(o, i_, ident)
            self.n_inst += 1
        E.cnt += 1
        inst.then_inc(E.sem, 1)
        self._commit((E.sem, E.cnt), reads, writes)

    def dma(self, Q, out, in_, reads, writes, semname, **kw):
        self._wait(Q, self._deps(reads, writes))
        if semname not in self.dsems:
            self.dsems[semname] = [self.nc.alloc_semaphore("d_" + semname), 0]
        ent = self.dsems[semname]
        inst = Q.e.dma_start(out=out, in_=in_, **kw)
        self.n_inst += 1
        ent[1] += 16
        inst.then_inc(ent[0], 16)
        self._commit((ent[0], ent[1]), reads, writes)

    def dma_multi(self, Q, parts, reads, writes, semname):
        self._wait(Q, self._deps(reads, writes))
        if semname not in self.dsems:
            self.dsems[semname] = [self.nc.alloc_semaphore("d_" + semname), 0]
        ent = self.dsems[semname]
        for out, in_ in parts:
            inst = Q.e.dma_start(out=out, in_=in_)
            self.n_inst += 1
            ent[1] += 16
            inst.then_inc(ent[0], 16)
        self._commit((ent[0], ent[1]), reads, writes)

    def idma(self, out, in_, idx_ap, reads, writes, semname):
        Q = self.pool
        self._wait(Q, self._deps(reads, writes))
        if semname not in self.dsems:
            self.dsems[semname] = [self.nc.alloc_semaphore("d_" + semname), 0]
        ent = self.dsems[semname]
        inst = self.nc.gpsimd.indirect_dma_start(out=out, out_offset=None, in_=in_,
                                                 in_offset=bass.IndirectOffsetOnAxis(ap=idx_ap, axis=0))
        self.n_inst += 1
        ent[1] += 16
        inst.then_inc(ent[0], 16)
        self._commit((ent[0], ent[1]), reads, writes)

    def finish(self):
        for name, (sem, val) in self.dsems.items():
            if val:
                self.nc.sync.wait_ge(sem, val)


def sl(ap_t, *idx):
    return ap_t[idx]


class Builder:
    def __init__(self, cfg):
        self.cfg = cfg
        nc = bass.Bass("TRN2", target_bir_lowering=False)
        self.nc = nc
        self.k = KB(nc)
        self.NT = cfg["NT"]
        self.NTILES = cfg["NTILES"]
        self.OWN0 = cfg["OWN0"]
        self.NSEQ = cfg["NSEQ"]
        self.TS = cfg["TS"]
        self.NS = self.NSEQ * self.TS
        self.nslots = self.NTILES * self.NT
        self.nown = (self.NTILES - self.OWN0) * self.NT
        self.NCH = self.nslots // 128
        self.NCB = self.nslots // 32
        self.NBLK = self.nslots // 64
        self.NOT = self.NTILES - self.OWN0
        self.declare_io()
        self.alloc()

    def declare_io(self):
        nc = self.nc
        NT, NS = self.NT, self.NS

        def din(name, shape, dt=F32):
            return nc.dram_tensor(name, list(shape), dt, kind="ExternalInput").ap()

        def dout(name, shape, dt=F32):
            return nc.dram_tensor(name, list(shape), dt, kind="ExternalOutput").ap()

        self.xT = din("xT", [D, self.nslots])
        self.xsT = din("xsT", [D, NS])
        self.convst = din("convst", [128, 2, KC, self.NSEQ, 2])
        self.gains = din("gains", [128, 13, KC])
        self.convw = din("convw", [128, 2, 3, KC])
        self.knorm = din("knorm", [128, 3, HD])
        self.rope_p = din("rope_p", [128, self.nslots // 128, 16])
        self.rope_s = din("rope_s", [NS, 16])
        self.w = {}
        self.w["ffn_a_w_in"] = din("ffn_a_w_in", [4, D, 2 * DFF])
        self.w["ffn_a_w_out"] = din("ffn_a_w_out", [4, DFF, D])
        self.w["ffn_b_w_in"] = din("ffn_b_w_in", [4, D, 2 * DFF])
        self.w["ffn_b_w_out"] = din("ffn_b_w_out", [4, DFF, D])
        self.w["conv_w_in"] = din("conv_w_in", [2, D, 3 * D])
        self.w["conv_w_out"] = din("conv_w_out", [2, D, D])
        self.w["w_kv"] = din("w_kv", [D, 1536])
        if self.cfg.get("stage", 9) > -3:
            self.w["nsa_w_qg"] = din("nsa_w_qg", [2, D, 1072])
            self.w["nsa_w_o"] = din("nsa_w_o", [2, D, D])
            self.w["cmp_w1"] = din("cmp_w1", [2, 32, HD, 256])
            self.w["cmp_w2"] = din("cmp_w2", [2, 256, HD])
        if self.cfg.get("stage", 9) > -3:
            self.d_ident = din("ident", [128, 128])
            self.d_tri = din("tri", [128, 2, 128])
            self.d_etab = din("etab", [64, self.nslots])
            self.d_flags = din("flags", [128, self.NCH + 1])
            self.d_qnorm = din("qnorm", [128, 2, HD])
            self.d_kn0 = din("kn0col", [64, 1])
            self.d_pe = din("peT", [64, 2, 128])
            self.d_cmq = din("cmq", [self.NOT, 128, 4, self.NCB])
            self.d_cmt = din("cmt", [self.NOT, 128, 4, 128])
            self.d_mulc = din("mulc", [self.NOT, 128, 4, self.NBLK])
            self.d_addc = din("addc", [self.NOT, 128, 4, self.NBLK])
        self.PAST = self.cfg["PAST"]
        self.NPG = self.PAST // 128
        self.NCBS = self.PAST // 32
        self.NBS = self.PAST // 64 + 1
        self.d_cache = din("cache_kv", [self.cfg["NPOOL"] * 256, 512])
        self.d_pt = din("page_table", [1, self.NSEQ * self.NPG], I32)
        self.d_pidx = din("pidx", [128, 1])
        self.d_wm = din("wmasks", [128, 3, 32])
        self.d_bmat = din("bmat", [8, 32])
        self.d_smc = din("smulc", [8, 136])
        self.d_sac = din("saddc", [8, 136])
        self.o_yT = dout("o_yT", [D, self.nown])
        self.o_ysT = dout("o_ysT", [D, NS])
        self.o_kv = dout("o_kv", [self.nown, 1024])
        self.o_kvs = dout("o_kvs", [NS, 1024])
        self.o_win = dout("o_win", [512, 512])
        self.o_wins = dout("o_wins", [self.NSEQ, 512, 512])
        self.cwin = din("cwin", [self.NSEQ, 512, 512])
        self.o_conv = dout("o_conv", [2, 128, KC, 2])
        self.o_convs = dout("o_convs", [2, 128, KC, self.NSEQ, 2])

    def alloc(self):
        nc = self.nc
        NT = self.NT
        def A(name, shape, dt):
            return nc.alloc_sbuf_tensor("sb_" + name, shape, dt)
        self.hT = A("hT", [128, KC, NT], F32)
        self.hn = A("hn", [128, KC, NT], BF16)
        self.aTf = A("aTf", [128, FC * NT // 2], F32)
        a_aT = nc.lookup_mloc("sb_aTf").addr
        self.aT = nc.alloc_sbuf_tensor_at("sb_aT", [128, FC, NT], BF16, offset=a_aT)
        self.rinv = A("rinv", [128, NT], F32)
        self.tmpa = A("tmpa", [128, NT], F32)
        self.tmpb = A("tmpb", [128, NT], F32)
        self.sg = A("sg", [128, 2, NT], BF16)
        self.uext = A("uext", [128, KC, NT + 2 * max(1, self.NSEQ)], F32)
        a_ue = nc.lookup_mloc("sb_uext").addr
        self.qcb = nc.alloc_sbuf_tensor_at("sb_qcb", [128, 4, 16, HD], BF16, offset=a_ue)
        self.qrb = nc.alloc_sbuf_tensor_at("sb_qrb", [128, 4, 16, HD], BF16, offset=a_ue + 8192)
        self.carry = A("carry", [128, 2, KC, 2], F32)
        self.kvrow = self.aTf[:, 0:(NT // 128) * 1024].rearrange("p (s f) -> p s f", f=1024)
        uf = self.uext[:, :, :].rearrange("p c n -> p (c n)")
        self.winrow = uf[:, 0:(NT // 128) * 512].rearrange("p (s f) -> p s f", f=512)
        self.rawT = nc.alloc_sbuf_tensor_at("sb_rawT", [128, 8, NT], BF16, offset=a_ue + 8192)
        self.ones = A("ones", [128, 128], BF16)
        self.g32 = A("g32", [128, 13, KC], F32)
        self.cw = A("cw", [128, 2, 3, KC], F32)
        self.kn = A("kn", [128, 3, HD], F32)
        self.ropep = A("ropep", [128, self.nslots // 128, 16], F32)
        self.ropes = A("ropes", [128, 16], F32)
        self.epsb_t = A("epsb", [128, 2], F32)
        self.epsb = {float(D * EPS): self.epsb_t[:, 0:1], float(HD * EPS): self.epsb_t[:, 1:2]}
        self.wslots = [A(f"w{i}", [128, WSLOT], BF16) for i in range(NWSLOT)]
        NCH, NCB, NBLK = self.NCH, self.NCB, self.NBLK
        self.dummy = A("dummy_t", [128, 1], F32)
        self.ones32 = A("ones32", [128, 1], F32)
        self.identb = A("identb", [128, 128], BF16)
        self.identf = A("identf", [128, 128], F32)
        self.tri = A("tri", [128, 2, 128], F32)
        self.Kaug = A("Kaug", [128, 4, self.nslots], BF16)
        self.V1s = A("V1s", [128, NCH, 4, 66], BF16)
        self.kwT = A("kwT", [128, 4, 1024], BF16)
        self.V1w = A("V1w", [128, 8, 4, 66], BF16)
        self.kcT = A("kcT", [128, 4, NCB], BF16)
        self.vcT = A("vcT", [128, 4, NCB], BF16)
        self.V1c = A("V1c", [128, 4, 66], BF16)
        self.flags = A("flags", [128, NCH + 1], F32)
        self.rowb = A("rowb", [128, NT // 128, 512], BF16)
        self.kb2 = A("kb2", [128, 2, 256], BF16)
        self.hidT = A("hidT", [128, 256], BF16)
        self.gl = A("gl", [128, 3, 256], F32)
        self.w2sb = A("w2sb", [128, 2, 2, HD], BF16)
        self.peT = A("peT", [128, 2, 128], F32)
        self.kn0 = A("kn0", [128, 1], F32)
        self.qnrm = A("qnrm", [128, 2, HD], F32)
        self.qf = A("qf", [128, 512], F32)
        self.sq8 = A("sq8", [128, 512], F32)
        self.xn8 = A("xn8", [128, 512], F32)
        self.ss16 = A("ss16", [128, 16], F32)
        self.rt8 = A("rt8", [128, 4, 8, 8], F32)
        self.gates = A("gates", [128, 4, 48], F32)
        self.QcT = A("QcT", [128, 512], BF16)
        self.Qaug = A("Qaug", [128, 512], BF16)
        self.Pb = A("Pb", [128, 4, 512], BF16)
        self.ec = A("ec", [128, 4 * NCB], F32)
        self.ep = A("ep", [128, 4, NBLK], F32)
        self.sm = A("sm", [128, 64], F32)
        self.imp = A("imp", [128, 3, NBLK], F32)
        self.m8 = A("m8", [128, 16], F32)
        self.selpad = A("selpad", [128, 128], BF16)
        self.Oev = A("Oev", [128, 3, 512], F32)
        self.oacc = A("oacc", [128, 2, 256], F32)
        self.otok = A("otok", [128, 1024], BF16)
        self.cmq = A("cmq", [128, 4, NCB], F32)
        self.cmt = A("cmt", [128, 4, 128], F32)
        self.mulc = A("mulc", [128, 4, NBLK], F32)
        self.addc = A("addc", [128, 4, NBLK], F32)
        NSEQ, NPG, NCBS = self.NSEQ, self.NPG, self.NCBS
        CCH = (NCBS + 127) // 128
        self.CCH = CCH
        need = 60 * 1024
        a0 = nc.lookup_mloc("sb_Kaug").addr
        a1 = nc.lookup_mloc("sb_V1w").addr + 8 * 4 * 66 * 2
        self.sa_base = a0
        self.sa_off = 0
        use_alias = (a1 - a0 >= need)

        def SA(name, shape, dt):
            if not use_alias:
                return A(name, shape, dt)
            nbytes = int(np.prod(shape[1:])) * (2 if dt == BF16 else 4)
            t_ = nc.alloc_sbuf_tensor_at("sb_" + name, shape, dt, offset=self.sa_base + self.sa_off)
            self.sa_off += (nbytes + 31) // 32 * 32
            assert self.sa_off <= a1 - a0, self.sa_off
            return t_
        self.kcTs = SA("kcTs", [128, NSEQ, 4, NCBS], BF16)
        self.vcTs = SA("vcTs", [128, NSEQ, 4, NCBS], BF16)
        self.V1cs = SA("V1cs", [128, NSEQ, CCH, 4, 66], BF16)
        self.kvpage = SA("kvpage", [128, 4, 512], BF16)
        self.KTp = SA("KTp", [128, 3, 4, 128], BF16)
        self.V1p = SA("V1p", [128, 3, 4, 66], BF16)
        self.newsel = SA("newsel", [128, NSEQ, 512], BF16)
        self.newwin = SA("newwin", [128, NSEQ, 512], BF16)
        self.pt_i = SA("pt_i", [128, NSEQ * NPG], I32)
        self.idxf = SA("idxf", [128, NSEQ * NPG], F32)
        self.idx_i = SA("idx_i", [128, 2, NSEQ * NPG], I32)
        self.ec_s = SA("ec_s", [128, 4, NCBS], F32)
        self.ep_s = SA("ep_s", [128, 4, NCBS // 2], F32)
        self.imp_s = SA("imp_s", [128, 3, 136], F32)
        self.biasx = SA("biasx", [128, 4, 136], BF16)
        self.Oev_s = SA("Oev_s", [128, 3, 128], F32)
        self.Ps = SA("Ps", [128, 3, 128], BF16)
        self.QT_s = SA("QT_s", [128, 2, 16, 8], BF16)
        self.wm = SA("wm", [128, 3, 32], F32)
        self.bmat = SA("bmat", [128, 32], BF16)
        self.smc = SA("smc", [128, 136], F32)
        self.sac = SA("sac", [128, 136], F32)
        self.pidx = SA("pidx", [128, 1], F32)
        self.otok_s = SA("otok_s", [128, 1024], BF16)
        self.rawT2 = nc.alloc_sbuf_tensor_at("sb_rawT2", [128, 8, 2 * NT], BF16, offset=a_ue)
        aQ = nc.lookup_mloc("sb_QcT").addr
        aQe = nc.lookup_mloc("sb_ep").addr + 4 * NBLK * 4
        if aQe - aQ >= 7168:
            self.gl2 = nc.alloc_sbuf_tensor_at("sb_gl2", [128, 3, 512], F32, offset=aQ)
            self.hidT2 = nc.alloc_sbuf_tensor_at("sb_hidT2", [128, 512], BF16, offset=aQ + 6144)
        else:
            self.gl2 = A("gl2", [128, 3, 512], F32)
            self.hidT2 = A("hidT2", [128, 512], BF16)
        aO = nc.lookup_mloc("sb_Oev").addr
        aT_ = nc.lookup_mloc("sb_otok").addr
        aE = nc.lookup_mloc("sb_addc").addr + 4 * NBLK * 4
        self.rawTB = None
        if aT_ - aO >= 4096:
            self.rowbB = nc.alloc_sbuf_tensor_at("sb_rowbB", [128, NT // 128, 512], BF16, offset=aO)
        else:
            self.rowbB = A("rowbB", [128, NT // 128, 512], BF16)
        self.biasE = SA("biasE", [128, 3, 4, 128], BF16)
        print("sbuf bytes remaining", nc.sbuf_bytes_remaining() if callable(nc.sbuf_bytes_remaining) else nc.sbuf_bytes_remaining)
        self.ps = [nc.alloc_psum_tensor(f"ps{i}", [128, 512], F32) if i not in (4, 5) else
                   nc.alloc_psum_tensor(f"ps{i}", [128, 1024], BF16) for i in range(8)]
        self.psb_i = 0
        self.gen_list = [0, 1, 2, 3, 6, 7]

    def psum(self):
        lst = self.gen_list
        i = lst[self.k.ps_i % len(lst)]
        self.k.ps_i += 1
        return self.ps[i], ("ps", i)

    def psum_b(self):
        i = 4 + self.psb_i % 2
        self.psb_i += 1
        return self.ps[i], ("ps", i)

    def wload(self, src_ap, shape, nparts=128):
        k = self.k
        i = k.w_i % NWSLOT
        k.w_i += 1
        n = int(np.prod(shape))
        assert n <= WSLOT
        t = self.wslots[i]
        if len(shape) == 2:
            view = t[0:nparts, 0:n].rearrange("p (a b) -> p a b", a=shape[0])
        elif len(shape) == 3:
            view = t[0:nparts, 0:n].rearrange("p (a b c) -> p a b c", a=shape[0], b=shape[1])
        else:
            raise ValueError
        if isinstance(src_ap, (list, tuple)):
            k.dma_multi(k.pool, [(view[:, :, j, :], sa) for j, sa in enumerate(src_ap)], reads=(), writes=(("w", i),), semname=f"w{i}")
        else:
            k.dma(k.pool, view, src_ap, reads=(), writes=(("w", i),), semname=f"w{i}")
        return view, ("w", i)

    def setup(self):
        k, nc = self.k, self.nc
        k.op(k.dve, lambda: nc.vector.memset(self.ones[:], 1.0), writes=("ones",))
        k.dma(k.sp, self.g32[:], self.gains, (), ("g32",), "c0")
        k.dma(k.sp, self.cw[:], self.convw, (), ("cw",), "c1")
        k.dma(k.sp, self.kn[:], self.knorm, (), ("kn",), "c2")
        k.dma(k.sp, self.ropep[:], self.rope_p, (), ("ropep",), "c3")
        k.dma(k.sp, self.ropes[0:self.NS, :], self.rope_s, (), ("ropes",), "c4")
        k.op(k.dve, lambda: nc.vector.tensor_scalar(self.g32[:], self.g32[:], 32.0, None, ALU.mult),
             reads=("g32",), writes=("g32",))
        k.op(k.dve, lambda: nc.vector.memset(self.carry[:], 0.0), writes=("carry",))
        k.op(k.dve, lambda: nc.vector.memset(self.ones32[:], 1.0), writes=("ones32",))
        k.op(k.dve, lambda: nc.vector.memset(self.epsb_t[:, 0:1], float(D * EPS)), writes=("epsb",))
        k.op(k.dve, lambda: nc.vector.memset(self.epsb_t[:, 1:2], float(HD * EPS)), writes=("epsb",))
        if self.cfg.get('stage', 9) >= 0:
            self.setup2()

    def setup2(self):
        k, nc = self.k, self.nc
        k.dma(k.pool, self.identb[:, :], self.d_ident, (), ("identb",), "c5")
        k.dma(k.sp, self.identf[:, :], self.d_ident, (), ("identf",), "c6")
        k.dma(k.sp, self.tri[:, :, :], self.d_tri, (), ("tri",), "c7")
        k.dma(k.sp, self.flags[:, :], self.d_flags, (), ("flags",), "c8")
        k.dma(k.sp, self.qnrm[:, :, :], self.d_qnorm, (), ("qnrm",), "c9")
        k.dma(k.sp, self.kn0[0:HD, :], self.d_kn0, (), ("kn0",), "c10")
        k.dma(k.sp, self.peT[0:HD, :, :], self.d_pe, (), ("peT",), "c11")
        for e in range(2):
            k.dma(k.pool, self.w2sb[:, e, :, :], self.w["cmp_w2"][e].rearrange("(fc p) d -> p fc d", p=128), (), ("w2sb",), "c12")
        k.op(k.dve, lambda: nc.vector.memset(self.Kaug[:, :, :], 0.0), writes=("Kaug",))
        for g in range(4):
            k.dma(k.pool, self.Kaug[HD:128, g, :], self.d_etab, (), ("Kaug",), "c13")
        for t_, kname in ((self.V1s, "V1s"), (self.V1w, "V1w"), (self.V1c, "V1c"), (self.kcT, "kcT"), (self.vcT, "vcT"),
                          (self.kwT, "kwT"), (self.selpad, "selpad")):
            k.op(k.dve, lambda t_=t_: nc.vector.memset(t_[:], 0.0), writes=(kname,))
        k.op(k.dve, lambda: nc.vector.tensor_copy(out=self.V1c[:, :, HD:HD + 1], in_=self.bc_mid(self.flags[:, self.NCH:self.NCH + 1], 4)),
             reads=("flags",), writes=("V1c",))

    def rsqrt(self, out, in_, addc, rkeys, wkeys):
        k, nc = self.k, self.nc
        np_ = out.shape[0]
        k.op(k.act, lambda: nc.scalar.activation(out=out, in_=in_, func=AF.Sqrt, bias=self.epsb[addc][0:np_, :], scale=1.0),
             reads=tuple(rkeys) + ("epsb",), writes=tuple(wkeys))
        k.op(k.dve, lambda: nc.vector.reciprocal(out=out, in_=out), reads=tuple(wkeys), writes=tuple(wkeys))

    def rmsnorm_T(self, N, gi):
        k, nc = self.k, self.nc
        hT, hn = self.hT, self.hn
        k.op(k.act, lambda: nc.scalar.activation(out=hn[:, :, 0:N], in_=hT[:, :, 0:N], func=AF.Square),
             reads=HT_ALL, writes=HN_ALL)
        ps, pk = self.psum()
        k.mm(ps[:, 0:N], [(self.ones[:, :], hn[:, c, 0:N]) for c in range(KC)],
             reads=("ones",) + HN_ALL, writes=(pk,))
        self.rsqrt(self.rinv[:, 0:N], ps[:, 0:N], float(D * EPS), (pk,), ("rinv",))
        for c in range(KC):
            k.op(k.dve, lambda c=c: nc.vector.scalar_tensor_tensor(
                out=hn[:, c, 0:N], in0=hT[:, c, 0:N], scalar=self.g32[:, gi, c:c + 1], in1=self.rinv[:, 0:N],
                op0=ALU.mult, op1=ALU.mult),
                reads=(("hT", c), "rinv", "g32"), writes=(("hn", c),))

    def ffn(self, N, layer, which):
        k, nc = self.k, self.nc
        gi = (0 if which == "a" else 8) + layer
        w_in = self.w[f"ffn_{which}_w_in"]
        w_out = self.w[f"ffn_{which}_w_out"]
        self.rmsnorm_T(N, gi)
        hn, aT = self.hn, self.aT
        for b in range(FC // 2):
            src4 = w_in[layer].rearrange("(c p) (two f) -> p c two f", p=128, two=2)
            wv, wk = self.wload([src4[:, :, j, b * 256:(b + 1) * 256] for j in range(2)], (KC, 2, 256))
            for cc in range(2):
                c = 2 * b + cc
                pg, pgk = self.psum()
                pu, puk = self.psum()
                k.mm(pg[:, 0:N], [(wv[:, kk, 0, cc * 128:(cc + 1) * 128], hn[:, kk, 0:N]) for kk in range(KC)],
                     reads=(wk,) + HN_ALL, writes=(pgk,))
                k.mm(pu[:, 0:N], [(wv[:, kk, 1, cc * 128:(cc + 1) * 128], hn[:, kk, 0:N]) for kk in range(KC)],
                     reads=(wk,) + HN_ALL, writes=(puk,))
                sgk = ("sg", c % 2)
                k.op(k.act, lambda: nc.scalar.activation(out=self.sg[:, c % 2, 0:N], in_=pg[:, 0:N], func=AF.Silu),
                     reads=(pgk,), writes=(sgk,))
                k.op(k.dve, lambda: nc.vector.tensor_tensor(out=aT[:, c, 0:N], in0=pu[:, 0:N],
                                                            in1=self.sg[:, c % 2, 0:N], op=ALU.mult),
                     reads=(puk, sgk), writes=(("aT", c),))
        for dch in range(KC):
            src = w_out[layer].rearrange("(c p) d -> p c d", p=128)[:, :, dch * 128:(dch + 1) * 128]
            wv, wk = self.wload(src, (FC, 128))
            ps, pk = self.psum()
            k.mm(ps[:, 0:N], [(wv[:, c, :], aT[:, c, 0:N]) for c in range(FC)],
                 reads=(wk,) + tuple(("aT", c) for c in range(FC)), writes=(pk,))
            k.op(k.dve, lambda: nc.vector.scalar_tensor_tensor(
                out=self.hT[:, dch, 0:N], in0=ps[:, 0:N], scalar=0.5, in1=self.hT[:, dch, 0:N],
                op0=ALU.mult, op1=ALU.add),
                reads=(pk, ("hT", dch)), writes=(("hT", dch),))

    def conv_mixer(self, N, layer, nseq, T):
        k, nc = self.k, self.nc
        self.rmsnorm_T(N, 4 + layer)
        hn, uext = self.hn, self.uext
        W = T + 2
        ue4 = uext[:, :, 0:nseq * W].rearrange("p c (s w) -> p c s w", s=nseq)
        vT = self.aT
        w_in = self.w["conv_w_in"][layer].rearrange("(c p) (three f) -> p c three f", p=128, three=3)
        for m in range(KC):
            wv, wk = self.wload([w_in[:, :, j, m * 128:(m + 1) * 128] for j in range(3)], (KC, 3, 128))
            pb, pbk = self.psum()
            pc, pck = self.psum()
            px, pxk = self.psum()
            for j, (pp, ppk) in enumerate(((pb, pbk), (pc, pck), (px, pxk))):
                k.mm(pp[:, 0:N], [(wv[:, kk, j, :], hn[:, kk, 0:N]) for kk in range(KC)],
                     reads=(wk,) + HN_ALL, writes=(ppk,))
            k.op(k.act, lambda: nc.scalar.copy(out=self.tmpa[:, 0:N], in_=pc[:, 0:N]),
                 reads=(pck,), writes=("tmpa",))
            k.op(k.dve, lambda: nc.vector.tensor_tensor(
                out=ue4[:, m, :, 2:W], in0=px[:, 0:N].rearrange("p (s t) -> p s t", s=nseq),
                in1=self.tmpa[:, 0:N].rearrange("p (s t) -> p s t", s=nseq), op=ALU.mult),
                reads=(pxk, "tmpa"), writes=("uext",))
            tb = self.tmpb[:, 0:N].rearrange("p (s t) -> p s t", s=nseq)
            k.op(k.dve, lambda: nc.vector.tensor_scalar(tb, ue4[:, m, :, 0:T], self.cw[:, layer, 0, m:m + 1], None,
                                                        ALU.mult),
                 reads=("uext", "cw"), writes=("tmpb",))
            for j in (1, 2):
                k.op(k.dve, lambda j=j: nc.vector.scalar_tensor_tensor(
                    out=tb, in0=ue4[:, m, :, j:j + T], scalar=self.cw[:, layer, j, m:m + 1], in1=tb,
                    op0=ALU.mult, op1=ALU.add),
                    reads=("uext", "cw", "tmpb"), writes=("tmpb",))
            k.op(k.dve, lambda: nc.vector.tensor_tensor(out=vT[:, m, 0:N], in0=pb[:, 0:N], in1=self.tmpb[:, 0:N],
                                                        op=ALU.mult),
                 reads=(pbk, "tmpb"), writes=(("aT", m),))
        w_out = self.w["conv_w_out"][layer].rearrange("(c p) d -> p c d", p=128)
        for half in range(2):
            wv, wk = self.wload(w_out[:, :, half * 512:(half + 1) * 512], (KC, 512))
            for dd in range(4):
                dch = half * 4 + dd
                ps, pk = self.psum()
                k.mm(ps[:, 0:N], [(wv[:, kk, dd * 128:(dd + 1) * 128], vT[:, kk, 0:N]) for kk in range(KC)],
                     reads=(wk,) + tuple(("aT", c) for c in range(KC)), writes=(pk,))
                k.op(k.dve, lambda: nc.vector.tensor_tensor(out=self.hT[:, dch, 0:N], in0=ps[:, 0:N],
                                                            in1=self.hT[:, dch, 0:N], op=ALU.add),
                     reads=(pk, ("hT", dch)), writes=(("hT", dch),))
        return ue4

    def barrier(self, keys):
        k, nc = self.k, self.nc
        if self.cfg.get("stage", 9) <= -4:
            return
        k.op(k.dve, lambda: nc.vector.memset(self.dummy[:, :], 0.0), reads=(), writes=tuple(keys))

    ALIAS_KEYS = tuple(("aT", c) for c in range(FC)) + ("uext", "rowdst", "rawT", "rawT2", "qcb", "qrb")

    def bc_mid(self, ap2, n):
        return bass.AP(ap2.tensor, ap2.offset, [list(ap2.ap[0]), [0, n], list(ap2.ap[1])])

    def bc_last(self, ap2, n):
        return bass.AP(ap2.tensor, ap2.offset, [list(ap2.ap[0]), list(ap2.ap[1]), [0, n]])

    def headnorm_rope(self, src, pk, M, nh, gain_ap, scale, rope_ap, work, wkey, mid=None):
        k, nc = self.k, self.nc
        W = nh * HD
        sq, ss, xn, rt = self.sq8, self.ss16, self.xn8, self.rt8
        k.op(k.act, lambda: nc.scalar.activation(out=sq[0:M, 0:W], in_=src, func=AF.Square),
             reads=(pk,), writes=("sq8",))
        k.op(k.dve, lambda: nc.vector.tensor_reduce(out=ss[0:M, 0:nh], in_=sq[0:M, 0:W].rearrange("p (g d) -> p g d", g=nh),
                                                    axis=AX.X, op=ALU.add),
             reads=("sq8",), writes=("ss16",))
        self.rsqrt(ss[0:M, 8:8 + nh], ss[0:M, 0:nh], float(HD * EPS), ("ss16",), ("ss16",))
        rb = self.bc_last(ss[0:M, 8:8 + nh], HD)
        k.op(k.dve, lambda: nc.vector.tensor_tensor(out=xn[0:M, 0:W].rearrange("p (g d) -> p g d", g=nh),
                                                    in0=src.rearrange("p (g d) -> p g d", g=nh), in1=rb, op=ALU.mult),
             reads=("ss16", pk), writes=("xn8",))
        knb = self.bc_mid(gain_ap, nh)
        w3 = work.rearrange("p (g d) -> p g d", g=nh)
        k.op(k.dve, lambda: nc.vector.scalar_tensor_tensor(
            out=w3, in0=xn[0:M, 0:W].rearrange("p (g d) -> p g d", g=nh),
            scalar=float(scale), in1=knb, op0=ALU.mult, op1=ALU.mult),
            reads=("xn8", "kn", "qnrm"), writes=(wkey,))
        if mid is not None:
            mid()
        x1, x2 = w3[:, :, 0:8], w3[:, :, 8:16]
        cosb = self.bc_mid(rope_ap[:, 0:8], nh)
        sinb = self.bc_mid(rope_ap[:, 8:16], nh)
        T_ = rt[0:M, :, 0:nh, :]
        k.op(k.dve, lambda: nc.vector.tensor_tensor(out=T_[:, 0], in0=x1, in1=cosb, op=ALU.mult),
             reads=(wkey, "ropep", "ropes"), writes=("rt8",))
        k.op(k.dve, lambda: nc.vector.tensor_tensor(out=T_[:, 1], in0=x2, in1=sinb, op=ALU.mult),
             reads=(wkey,), writes=("rt8",))
        k.op(k.dve, lambda: nc.vector.tensor_tensor(out=T_[:, 2], in0=x2, in1=cosb, op=ALU.mult),
             reads=(wkey,), writes=("rt8",))
        k.op(k.dve, lambda: nc.vector.tensor_tensor(out=T_[:, 3], in0=x1, in1=sinb, op=ALU.mult),
             reads=(wkey,), writes=("rt8",))
        k.op(k.dve, lambda: nc.vector.tensor_tensor(out=x1, in0=T_[:, 0], in1=T_[:, 1], op=ALU.subtract),
             reads=("rt8",), writes=(wkey,))
        k.op(k.dve, lambda: nc.vector.tensor_tensor(out=x2, in0=T_[:, 2], in1=T_[:, 3], op=ALU.add),
             reads=("rt8",), writes=(wkey,))

    def psb(self, ps):
        return ps[:, :].bitcast(BF16)

    def kv_rows(self, N, tile_idx, sample):
        k, nc = self.k, self.nc
        self.rmsnorm_T(N, 12)
        self.barrier(self.ALIAS_KEYS)
        hn = self.hn
        wkv = self.w["w_kv"].rearrange("(c p) f -> p c f", p=128)
        nsub = max(1, N // 128)
        M = min(128, N)
        for blk in range(3):
            wv, wk = self.wload(wkv[:, :, blk * 512:(blk + 1) * 512], (KC, 512))
            for sub in range(nsub):
                ch = tile_idx * nsub + sub
                ps, pk = self.psum()
                k.mm(ps[0:M, :], [(hn[:, kk, sub * 128:sub * 128 + M], wv[:, kk, :]) for kk in range(KC)],
                     reads=(wk,) + HN_ALL, writes=(pk,))
                rope_ap = self.ropes[0:M, :] if sample else self.ropep[0:M, ch, :]
                if blk == 0:
                    k.op(k.act, lambda: nc.scalar.copy(out=self.kvrow[0:M, sub, 0:512], in_=ps[0:M, :]),
                         reads=(pk,), writes=("rowdst",))
                    if not sample:
                        k.op(k.dve, lambda: nc.vector.tensor_copy(out=self.rowb[0:M, sub, :], in_=self.kvrow[0:M, sub, 0:512]),
                             reads=("rowdst",), writes=("rowb",))
                    continue
                dstrow = self.kvrow[0:M, sub, 512:768] if blk == 1 else self.winrow[0:M, sub, 0:256]
                vdst = self.kvrow[0:M, sub, 768:1024] if blk == 1 else self.winrow[0:M, sub, 256:512]
                k.op(k.act, lambda: nc.scalar.copy(out=vdst, in_=ps[0:M, 256:512]), reads=(pk,), writes=("rowdst",))
                self.headnorm_rope(ps[0:M, 0:256], pk, M, 4, self.kn[0:M, blk, :], 8.0, rope_ap, dstrow, "rowdst")
                if sample or self.cfg.get("stage", 9) < 1:
                    continue
                kb = self.kb2[:, blk - 1, :]
                k.op(k.dve, lambda: nc.vector.tensor_copy(out=kb, in_=dstrow), reads=("rowdst",), writes=(("kb2", blk),))
                if blk == 1:
                    Vt, vkey, vi = self.V1s, "V1s", ch
                else:
                    Vt, vkey, vi = self.V1w, "V1w", ch % 8
                k.op(k.act, lambda: nc.scalar.copy(out=Vt[:, vi, :, 0:HD], in_=ps[:, 256:512].rearrange("p (g d) -> p g d", g=4)),
                     reads=(pk,), writes=(vkey,))
                fl = self.flags[:, ch:ch + 1]
                k.op(k.dve, lambda: nc.vector.tensor_copy(out=Vt[:, vi, :, HD:HD + 1], in_=self.bc_mid(fl, 4)),
                     reads=("flags",), writes=(vkey,))
                pt, ptk = self.psum_b()
                ptb = pt
                k.tr([(ptb[0:HD, g * 128:(g + 1) * 128], kb[:, g * HD:(g + 1) * HD]) for g in range(4)], self.identb[:, :],
                     reads=(("kb2", blk), "identb"), writes=(ptk,))
                if blk == 1:
                    dstk, kkey = self.Kaug[0:HD, :, ch * 128:(ch + 1) * 128], "Kaug"
                else:
                    dstk, kkey = self.kwT[0:HD, :, (ch % 8) * 128:(ch % 8 + 1) * 128], "kwT"
                k.op(k.act, lambda: nc.scalar.copy(out=dstk, in_=ptb[0:HD, 0:512].rearrange("p (g n) -> p g n", g=4)),
                     reads=(ptk,), writes=(kkey,))

    def compress(self, t, kc_dst=None, vc_dst=None, kkey="kcT", vkey="vcT", alt=False, dbl=False):
        k, nc = self.k, self.nc
        NT, NCB = self.NT, self.NCB
        nsub = NT // 128
        nb = NT // 32
        c0 = (t or 0) * nb
        if kc_dst is None:
            kc_dst = self.kcT[0:HD, :, c0:c0 + nb]
            vc_dst = self.vcT[0:HD, :, c0:c0 + nb]
        rawT = self.rawTB if alt else self.rawT
        rawk = "rawTB" if alt else "rawT"
        gl, hidT = self.gl, self.hidT
        if dbl:
            nsub, nb = 2 * nsub, 2 * nb
            rawT, rawk, gl, hidT = self.rawT2, "rawT2", self.gl2, self.hidT2
        for sub in range(nsub):
            pt, ptk = self.psum_b()
            ptb = pt
            useB = alt or (dbl and sub >= 4)
            rowb = self.rowbB if useB else self.rowb
            rsub = sub % 4
            k.tr([(ptb[0:HD, j * 128:(j + 1) * 128], rowb[:, rsub, j * HD:(j + 1) * HD]) for j in range(8)],
                 self.identb[:, :], reads=((("rowbB", rsub),) if useB else ("rowb", ("rowb", rsub))) + ("identb",), writes=(ptk,))
            for e in range(2):
                k.op(k.dve, lambda e=e: nc.vector.tensor_tensor(
                    out=rawT[0:HD, e * 4:(e + 1) * 4, sub * 128:(sub + 1) * 128],
                    in0=ptb[0:HD, e * 512:(e + 1) * 512].rearrange("p (g n) -> p g n", g=4),
                    in1=self.bc_mid(self.peT[0:HD, e, :], 4), op=ALU.add),
                    reads=(ptk, "peT"), writes=(rawk,))
        hps, hpk = self.psum()
        w1 = self.w["cmp_w1"]
        for e in range(2):
            wvs = []
            for lb in range(2):
                wv, wk = self.wload(w1[e, lb * 16:(lb + 1) * 16, :, :].rearrange("l d f -> d l f"), (16, 256), nparts=HD)
                wvs.append((wv, wk))
            for g in range(4):
                rv = rawT[0:HD, e * 4 + g, :].rearrange("p (c l) -> p c l", l=32)
                for fc in range(2):
                    col0 = ((e * 4 + g) * 2 + fc) * nb
                    k.mm(hps[:, col0:col0 + nb],
                         [(wvs[l // 16][0][0:HD, l % 16, fc * 128:(fc + 1) * 128], rv[:, :, l]) for l in range(32)],
                         reads=(wvs[0][1], wvs[1][1], rawk), writes=(hpk,))
        ncol = 16 * nb
        x_, a_, b_ = gl[:, 0, 0:ncol], gl[:, 1, 0:ncol], gl[:, 2, 0:ncol]
        k.op(k.act, lambda: nc.scalar.copy(out=x_, in_=hps[:, 0:ncol]), reads=(hpk,), writes=("gl0",))
        k.op(k.dve, lambda: nc.vector.tensor_tensor(out=a_, in0=x_, in1=x_, op=ALU.mult), reads=("gl0",), writes=("gl1",))
        k.op(k.dve, lambda: nc.vector.tensor_scalar(a_, a_, 0.044715, 1.0, ALU.mult, ALU.add), reads=("gl1",), writes=("gl1",))
        k.op(k.dve, lambda: nc.vector.tensor_tensor(out=a_, in0=a_, in1=x_, op=ALU.mult), reads=("gl0", "gl1"), writes=("gl1",))
        k.op(k.act, lambda: nc.scalar.activation(out=b_, in_=a_, func=AF.Sigmoid, scale=1.5957691216057308),
             reads=("gl1",), writes=("gl2",))
        k.op(k.dve, lambda: nc.vector.tensor_tensor(out=hidT[:, 0:ncol], in0=x_, in1=b_, op=ALU.mult),
             reads=("gl0", "gl2"), writes=("hidT",))
        po, pok = self.psum()
        for e in range(2):
            for g in range(4):
                cols = [((e * 4 + g) * 2 + fc) * nb for fc in range(2)]
                k.mm(po[0:HD, (e * 4 + g) * nb:(e * 4 + g + 1) * nb],
                     [(self.w2sb[:, e, fc, :], hidT[:, cols[fc]:cols[fc] + nb]) for fc in range(2)],
                     reads=("w2sb", "hidT"), writes=(pok,))
        nk = 4 * nb
        sqk = hidT[0:HD, 0:nk]
        k.op(k.act, lambda: nc.scalar.activation(out=sqk, in_=po[0:HD, 0:nk], func=AF.Square), reads=(pok,), writes=("hidT",))
        pss, pssk = self.psum()
        k.mm(pss[0:HD, 0:nk], [(self.ones[0:HD, 0:HD], sqk)], reads=("ones", "hidT"), writes=(pssk,))
        self.rsqrt(gl[0:HD, 0, 0:nk], pss[0:HD, 0:nk], float(HD * EPS), (pssk,), ("gl0",))
        k.op(k.dve, lambda: nc.vector.tensor_tensor(out=gl[0:HD, 1, 0:nk], in0=po[0:HD, 0:nk], in1=gl[0:HD, 0, 0:nk], op=ALU.mult),
             reads=(pok, "gl0"), writes=("gl1",))
        k.op(k.dve, lambda: nc.vector.tensor_scalar(kc_dst, gl[0:HD, 1, 0:nk].rearrange("p (g c) -> p g c", g=4),
                                                    self.kn0[0:HD, 0:1], 8.0, ALU.mult, ALU.mult),
             reads=("gl1", "kn0"), writes=(kkey,))
        k.op(k.act, lambda: nc.scalar.copy(out=vc_dst, in_=po[0:HD, nk:2 * nk].rearrange("p (g c) -> p g c", g=4)),
             reads=(pok,), writes=(vkey,))
        if t is None:
            return
        pv, pvk = self.psum_b()
        pvb = pv
        k.tr([(pvb[0:NCB, g * HD:(g + 1) * HD], self.vcT[0:HD, g, 0:NCB]) for g in range(4)], self.identb[0:HD, 0:HD],
             reads=("vcT", "identb"), writes=(pvk,))
        k.op(k.act, lambda: nc.scalar.copy(out=self.V1c[0:NCB, :, 0:HD], in_=pvb[0:NCB, 0:256].rearrange("p (g d) -> p g d", g=4)),
             reads=(pvk,), writes=("V1c",))

    def attn_group(self, t, sub, g, diag, lb):
        k, nc = self.k, self.nc
        NCB, NBLK = self.NCB, self.NBLK
        QcT, Qaug = self.QcT, self.Qaug
        pq, pqk = self.psum_b()
        pqb = pq
        k.tr([(pqb[0:HD, h * 128:(h + 1) * 128], self.qcb[:, sub, 4 * g + h, :]) for h in range(4)] +
             [(pqb[0:HD, 512 + h * 128:512 + (h + 1) * 128], self.qrb[:, sub, 4 * g + h, :]) for h in range(4)],
             self.identb[:, :], reads=("qcb", "qrb", "identb"), writes=(pqk,))
        k.op(k.act, lambda: nc.scalar.copy(out=QcT[0:HD, :], in_=pqb[0:HD, 0:512]), reads=(pqk,), writes=("QcT",))
        k.op(k.act, lambda: nc.scalar.copy(out=Qaug[0:HD, :], in_=pqb[0:HD, 512:1024]), reads=(pqk,), writes=("Qaug_q",))
        pc, pck = self.psum()
        for h in range(4):
            k.mm(pc[:, h * NCB:(h + 1) * NCB], [(QcT[0:HD, h * 128:(h + 1) * 128], self.kcT[0:HD, g, 0:NCB])],
                 reads=("QcT", "kcT"), writes=(pck,))
        ec, ep, sm, imp = self.ec, self.ep, self.sm, self.imp
        k.op(k.act, lambda: nc.scalar.activation(out=ec[:, :], in_=pc[:, 0:4 * NCB], func=AF.Exp), reads=(pck,), writes=("ec",))
        k.op(k.dve, lambda: nc.vector.tensor_tensor(out=ec[:, :].rearrange("p (h c) -> p h c", h=4),
                                                    in0=ec[:, :].rearrange("p (h c) -> p h c", h=4),
                                                    in1=self.bc_mid(self.cmq[:, sub, :], 4), op=ALU.mult),
             reads=("ec", "cmq"), writes=("ec",))
        k.op(k.dve, lambda: nc.vector.tensor_reduce(out=ep[:, :, :].rearrange("p h b -> p (h b)"),
                                                    in_=ec[:, :].rearrange("p (x two) -> p x two", two=2), axis=AX.X, op=ALU.add),
             reads=("ec",), writes=("ep",))
        k.op(k.dve, lambda: nc.vector.tensor_reduce(out=sm[:, 0:4], in_=ep[:, :, :], axis=AX.X, op=ALU.add),
             reads=("ep",), writes=("sm",))
        k.op(k.dve, lambda: nc.vector.tensor_scalar(sm[:, 0:4], sm[:, 0:4], 1e-30, None, ALU.max), reads=("sm",), writes=("sm",))
        k.op(k.dve, lambda: nc.vector.reciprocal(out=sm[:, 4:8], in_=sm[:, 0:4]), reads=("sm",), writes=("sm",))
        k.op(k.dve, lambda: nc.vector.tensor_scalar(imp[:, 0, :], ep[:, 0, :], sm[:, 4:5], None, ALU.mult),
             reads=("ep", "sm"), writes=("imp",))
        for h in range(1, 4):
            k.op(k.dve, lambda h=h: nc.vector.scalar_tensor_tensor(out=imp[:, 0, :], in0=ep[:, h, :], scalar=sm[:, 4 + h:5 + h],
                                                                   in1=imp[:, 0, :], op0=ALU.mult, op1=ALU.add),
                 reads=("ep", "sm", "imp"), writes=("imp",))
        k.op(k.dve, lambda: nc.vector.tensor_tensor(out=imp[:, 1, :], in0=imp[:, 0, :], in1=self.mulc[:, sub, :], op=ALU.mult),
             reads=("imp", "mulc"), writes=("imp",))
        k.op(k.dve, lambda: nc.vector.tensor_tensor(out=imp[:, 1, :], in0=imp[:, 1, :], in1=self.addc[:, sub, :], op=ALU.add),
             reads=("imp", "addc"), writes=("imp",))
        m8 = self.m8
        k.op(k.dve, lambda: nc.vector.max(out=m8[:, 0:8], in_=imp[:, 1, :]), reads=("imp",), writes=("m8",))
        k.op(k.dve, lambda: nc.vector.match_replace(out=imp[:, 2, :], in_to_replace=m8[:, 0:8], in_values=imp[:, 1, :], imm_value=-3e38),
             reads=("imp", "m8"), writes=("imp",))
        k.op(k.dve, lambda: nc.vector.max(out=m8[:, 8:16], in_=imp[:, 2, :]), reads=("imp",), writes=("m8",))
        k.op(k.dve, lambda: nc.vector.tensor_scalar(self.selpad[:, 64:64 + NBLK], imp[:, 1, :], m8[:, 15:16], 30000.0, ALU.is_ge, ALU.mult),
             reads=("imp", "m8"), writes=("selpad",))
        pt, ptk = self.psum_b()
        ptb = pt
        k.tr([(ptb[:, 0:128], self.selpad[:, :])], self.identb[:, :], reads=("selpad", "identb"), writes=(ptk,))
        src_b = ptb[64:128, 0:128]
        k.op(k.dve, lambda: nc.vector.tensor_scalar(Qaug[64:128, :].rearrange("p (h q) -> p h q", h=4), self.bc_mid(src_b, 4),
                                                    -30000.0, None, ALU.add),
             reads=(ptk,), writes=("Qaug_b",))
        self.pcount = getattr(self, "pcount", 0)

        def pbuf():
            i = self.pcount % 4
            self.pcount += 1
            return self.Pb[:, i, :], ("Pb", i)

        tri_lo = self.bc_mid(self.tri[:, 0, :], 4)
        tri_up = self.bc_mid(self.tri[:, 1, :], 4)
        Os, Osk = self.ps[6], ("ps", 6)
        Ow, Owk = self.ps[7], ("ps", 7)
        wj = [j for j in range(diag - 4, diag + 1) if j >= 0]
        jobs = [("s", j) for j in range(diag + 1)] + [("w", j) for j in wj] + [("c", 0)]
        oc_box = []

        def qk(job):
            kind, j = job
            ps, pk = self.psum()
            P, Pk = pbuf()
            if kind == "s":
                k.mm(ps[:, :], [(self.Kaug[:, g, j * 128:(j + 1) * 128], Qaug[:, :])], reads=("Kaug", "Qaug_q", "Qaug_b"), writes=(pk,))
                k.op(k.act, lambda: nc.scalar.activation(out=P, in_=ps[:, :], func=AF.Exp), reads=(pk,), writes=(Pk,))
                m_ = tri_lo if j == diag else None
            elif kind == "w":
                r = j % 8
                k.mm(ps[:, :], [(self.kwT[0:HD, g, r * 128:(r + 1) * 128], Qaug[0:HD, :])], reads=("kwT", "Qaug_q"), writes=(pk,))
                k.op(k.act, lambda: nc.scalar.activation(out=P, in_=ps[:, :], func=AF.Exp), reads=(pk,), writes=(Pk,))
                m_ = tri_lo if j == diag else (tri_up if j == diag - 4 else None)
            else:
                k.mm(ps[0:NCB, :], [(self.kcT[0:HD, g, 0:NCB], QcT[0:HD, :])], reads=("kcT", "QcT"), writes=(pk,))
                k.op(k.act, lambda: nc.scalar.activation(out=P[0:NCB, :], in_=ps[0:NCB, :], func=AF.Exp), reads=(pk,), writes=(Pk,))
                k.op(k.dve, lambda: nc.vector.tensor_tensor(out=P[0:NCB, :].rearrange("p (h q) -> p h q", h=4),
                                                            in0=P[0:NCB, :].rearrange("p (h q) -> p h q", h=4),
                                                            in1=self.bc_mid(self.cmt[0:NCB, sub, :], 4), op=ALU.mult),
                     reads=(Pk, "cmt"), writes=(Pk,))
                m_ = None
            if m_ is not None:
                k.op(k.dve, lambda: nc.vector.tensor_tensor(out=P.rearrange("p (h q) -> p h q", h=4),
                                                            in0=P.rearrange("p (h q) -> p h q", h=4), in1=m_, op=ALU.mult),
                     reads=(Pk, "tri"), writes=(Pk,))
            return P, Pk

        def pv(job, P, Pk):
            kind, j = job
            if kind == "s":
                k.mm(Os[0:HD + 2, :], [(self.V1s[:, j, g, :], P)], reads=("V1s", Pk), writes=(Osk,), start=(j == 0), stop=(j == diag))
            elif kind == "w":
                k.mm(Ow[0:HD + 2, :], [(self.V1w[:, j % 8, g, :], P)], reads=("V1w", Pk), writes=(Owk,), start=(j == wj[0]), stop=(j == wj[-1]))
            else:
                Oc_, Ock_ = self.psum()
                oc_box.append((Oc_, Ock_))
                k.mm(Oc_[0:HD + 2, :], [(self.V1c[0:NCB, g, :], P[0:NCB, :])], reads=("V1c", Pk), writes=(Ock_,))

        pend = []
        for job in jobs:
            pend.append((job,) + qk(job))
            if len(pend) > 2:
                pv(*pend.pop(0))
        while pend:
            pv(*pend.pop(0))
        Oc, Ock = oc_box[0]
        Oev = self.Oev
        for b, (O_, Ok_) in enumerate(((Oc, Ock), (Os, Osk), (Ow, Owk))):
            k.op(k.act, lambda: nc.scalar.copy(out=Oev[0:HD + 1, b, :], in_=O_[0:HD + 1, :]), reads=(Ok_,), writes=(("Oev", b),))
        self.combine_g(128, lambda b, h: Oev[0:HD + 1, b, h * 128:(h + 1) * 128], tuple(("Oev", b) for b in range(3)),
                       self.gates[:, sub, g * 12:(g + 1) * 12], self.otok[:, g * 256:(g + 1) * 256])

    def combine_g(self, M, src, srckeys, gates_ap, out_ap):
        k, nc = self.k, self.nc
        sm, oacc = self.sm, self.oacc
        gt = gates_ap.rearrange("p (h b) -> p h b", b=3)
        for b in range(3):
            po, pok = self.psum()
            k.tr([(po[0:M, h * 65:(h + 1) * 65], src(b, h)) for h in range(4)],
                 self.identf[0:HD + 1, 0:HD + 1], reads=tuple(srckeys) + ("identf",), writes=(pok,))
            po3 = po[0:M, 0:260].rearrange("p (h d) -> p h d", h=4)
            k.op(k.dve, lambda: nc.vector.tensor_scalar(sm[0:M, 8:12], po3[:, :, 64], 1e-30, None, ALU.max), reads=(pok,), writes=("sm2",))
            k.op(k.dve, lambda: nc.vector.reciprocal(out=sm[0:M, 8:12], in_=sm[0:M, 8:12]), reads=("sm2",), writes=("sm2",))
            k.op(k.dve, lambda: nc.vector.tensor_tensor(out=sm[0:M, 12:16], in0=sm[0:M, 8:12], in1=gt[:, :, b], op=ALU.mult),
                 reads=("sm2", "gates"), writes=("sm2",))
            fb = self.bc_last(sm[0:M, 12:16], HD)
            if b == 0:
                k.op(k.dve, lambda: nc.vector.tensor_tensor(out=oacc[0:M, 0, :].rearrange("p (h d) -> p h d", h=4), in0=po3[:, :, 0:HD],
                                                            in1=fb, op=ALU.mult), reads=(pok, "sm2"), writes=("oacc0",))
            else:
                k.op(k.dve, lambda: nc.vector.tensor_tensor(out=oacc[0:M, 1, :].rearrange("p (h d) -> p h d", h=4), in0=po3[:, :, 0:HD],
                                                            in1=fb, op=ALU.mult), reads=(pok, "sm2"), writes=("oacc1",))
                dst = oacc[0:M, 0, :] if b == 1 else out_ap
                k.op(k.dve, lambda: nc.vector.tensor_tensor(out=dst, in0=oacc[0:M, 0, :], in1=oacc[0:M, 1, :], op=ALU.add),
                     reads=("oacc0", "oacc1"), writes=(("oacc0",) if b == 1 else ("otok",)))

    def nsa(self, N, t, lb):
        k, nc = self.k, self.nc
        layer = 2 + lb
        nsub = N // 128
        self.rmsnorm_T(N, 4 + layer)
        hn = self.hn
        wqg = self.w["nsa_w_qg"][lb].rearrange("(c p) f -> p c f", p=128)
        for blk in range(3):
            ncol = 512 if blk < 2 else 48
            wv, wk = self.wload(wqg[:, :, blk * 512:blk * 512 + ncol], (KC, ncol))
            for sub in range(nsub):
                ch = t * nsub + sub
                ps, pk = self.psum()
                k.mm(ps[:, 0:ncol], [(hn[:, kk, sub * 128:(sub + 1) * 128], wv[:, kk, :]) for kk in range(KC)],
                     reads=(wk,) + HN_ALL, writes=(pk,))
                if blk < 2:
                    h0 = blk * 8

                    def mid(sub=sub, h0=h0):
                        k.op(k.act, lambda: nc.scalar.copy(out=self.qcb[:, sub, h0:h0 + 8, :],
                                                           in_=self.qf[:, :].rearrange("p (h d) -> p h d", h=8)),
                             reads=("qf",), writes=("qcb",))
                    self.headnorm_rope(ps[:, 0:512], pk, 128, 8, self.qnrm[:, lb, :], 1.0, self.ropep[:, ch, :], self.qf[:, :], "qf", mid=mid)
                    k.op(k.act, lambda: nc.scalar.copy(out=self.qrb[:, sub, h0:h0 + 8, :],
                                                       in_=self.qf[:, :].rearrange("p (h d) -> p h d", h=8)),
                         reads=("qf",), writes=("qrb",))
                else:
                    k.op(k.act, lambda: nc.scalar.activation(out=self.gates[:, sub, :], in_=ps[:, 0:48], func=AF.Sigmoid),
                         reads=(pk,), writes=("gates",))
        oT = self.aT
        self.gen_list = [0, 1, 2, 3]
        for sub in range(nsub):
            diag = t * nsub + sub
            for g in range(4):
                if self.cfg.get("stage", 9) >= 4:
                    self.attn_group(t, sub, g, diag, lb)
            pt, ptk = self.psum_b()
            ptb = pt
            k.tr([(ptb[:, kk * 128:(kk + 1) * 128], self.otok[:, kk * 128:(kk + 1) * 128]) for kk in range(KC)], self.identb[:, :],
                 reads=("otok", "identb"), writes=(ptk,))
            k.op(k.act, lambda: nc.scalar.copy(out=oT[:, 0:KC, sub * 128:(sub + 1) * 128],
                                               in_=ptb[:, 0:1024].rearrange("p (c n) -> p c n", c=KC)),
                 reads=(ptk,), writes=tuple(("aT", c) for c in range(KC)))
        self.gen_list = [0, 1, 2, 3, 6, 7]
        w_o = self.w["nsa_w_o"][lb].rearrange("(c p) d -> p c d", p=128)
        for half in range(2):
            wv, wk = self.wload(w_o[:, :, half * 512:(half + 1) * 512], (KC, 512))
            for dd in range(4):
                dch = half * 4 + dd
                ps, pk = self.psum()
                k.mm(ps[:, 0:N], [(wv[:, kk, dd * 128:(dd + 1) * 128], oT[:, kk, 0:N]) for kk in range(KC)],
                     reads=(wk,) + tuple(("aT", c) for c in range(KC)), writes=(pk,))
                k.op(k.dve, lambda: nc.vector.tensor_tensor(out=self.hT[:, dch, 0:N], in0=ps[:, 0:N],
                                                            in1=self.hT[:, dch, 0:N], op=ALU.add),
                     reads=(pk, ("hT", dch)), writes=(("hT", dch),))

    SKEYS = ("kcTs", "vcTs", "V1cs", "kvpage", "KTp", "V1p", "newsel", "newwin", "idx", "ec_s", "ep_s", "imp_s", "biasx",
             "Oev_s0", "Oev_s1", "Oev_s2", "Ps0", "Ps1", "QT_s", "sc_pidx", "sc_wm", "sc_smc", "sc_sac", "sc_bmat", "otok")

    def sample_prep(self):
        k, nc = self.k, self.nc
        NSEQ, NPG, NCBS, TS, CCH = self.NSEQ, self.NPG, self.NCBS, self.TS, self.CCH
        self.barrier(("Kaug", "V1s", "kwT", "V1w", "otok", "cmq", "cmt", "mulc", "addc", "rawTB", "QcT", "Qaug_q", "Qaug_b", "ec", "ep",
                      "gl0", "gl1", "gl2", "hidT", "Pb", ("Pb", 0), ("Pb", 1), ("Pb", 2), ("Pb", 3)) + tuple(("Oev", b) for b in range(3))
                     + tuple(("rowbB", i) for i in range(4)) + self.SKEYS + self.ALIAS_KEYS[-6:])
        n = NSEQ * NPG
        ptb = bass.AP(self.d_pt.tensor, 0, [[0, 128], [1, n]])
        k.dma(k.sp, self.pt_i[:, :], ptb, (), ("idx",), "s0")
        k.dma(k.sp, self.pidx[:, :], self.d_pidx, (), ("sc_pidx",), "s1a")
        k.dma(k.sp, self.wm[:, :, :], self.d_wm, (), ("sc_wm",), "s1b")
        k.dma(k.sp, self.smc[0:8, :], self.d_smc, (), ("sc_smc",), "s1c")
        k.dma(k.sp, self.sac[0:8, :], self.d_sac, (), ("sc_sac",), "s1d")
        k.dma(k.pool, self.bmat[0:8, :], self.d_bmat, (), ("sc_bmat",), "s2")
        k.op(k.dve, lambda: nc.vector.tensor_copy(out=self.idxf[:, :], in_=self.pt_i[:, :]), reads=("idx",), writes=("idxf",))
        k.op(k.dve, lambda: nc.vector.tensor_scalar(self.idxf[:, :], self.idxf[:, :], 256.0, self.pidx[:, 0:1], ALU.mult, ALU.add),
             reads=("idxf", "sc_pidx", "sc_wm", "sc_smc", "sc_sac", "sc_bmat"), writes=("idxf",))
        k.op(k.dve, lambda: nc.vector.tensor_copy(out=self.idx_i[:, 0, :], in_=self.idxf[:, :]), reads=("idxf",), writes=("idx",))
        k.op(k.dve, lambda: nc.vector.tensor_scalar(self.idxf[:, :], self.idxf[:, :], 1.0, None, ALU.add), reads=("idxf",), writes=("idxf",))
        k.op(k.dve, lambda: nc.vector.tensor_copy(out=self.idx_i[:, 1, :], in_=self.idxf[:, :]), reads=("idxf",), writes=("idx",))
        for t_, kn_ in ((self.newsel, tuple(("newsel", i) for i in range(NSEQ))), (self.newwin, tuple(("newwin", i) for i in range(NSEQ))),
                        (self.V1cs, ("V1cs",)), (self.imp_s, ("imp_s",)), (self.biasx, ("biasx",))):
            k.op(k.dve, lambda t_=t_: nc.vector.memset(t_[:], 0.0), writes=kn_)
        k.op(k.dve, lambda: nc.vector.memset(self.V1cs[:, :, :, :, HD:HD + 1], 1.0), writes=("V1cs",))
        k.op(k.dve, lambda: nc.vector.memset(self.V1p[:], 0.0), writes=(("V1p", 0), ("V1p", 1), ("V1p", 2)))
        k.op(k.dve, lambda: nc.vector.memset(self.Ps[:], 0.0), writes=(("Ps", 0), ("Ps", 1), ("Ps", 2)))
        for s_ in range(NSEQ):
            k.dma(k.pool, self.newsel[0:TS, s_, :], self.kvrow[s_ * TS:(s_ + 1) * TS, 0, 512:1024], ("rowdst",), (("newsel", s_),), f"s3a{s_}")
            k.dma(k.pool, self.newwin[0:TS, s_, :], self.winrow[s_ * TS:(s_ + 1) * TS, 0, :], ("rowdst",), (("newwin", s_),), f"s3b{s_}")
        self.barrier(("uext", "rowdst", "rawT", "rawT2"))
        for s_ in range(NSEQ):
            for pr in range(self.PAST // 1024):
                for sub in range(8):
                    j = s_ * NPG + pr * 8 + sub
                    if sub >= 4:
                        k.idma(self.rowbB[:, sub - 4, :], self.d_cache, self.idx_i[:, 0, j:j + 1], ("idx",), (("rowbB", sub - 4),), f"s5{sub - 4}")
                    else:
                        k.idma(self.rowb[:, sub, :], self.d_cache, self.idx_i[:, 0, j:j + 1], ("idx",), (("rowb", sub),), f"s4{sub}")
                self.compress(None, kc_dst=self.kcTs[0:HD, s_, :, pr * 32:(pr + 1) * 32],
                              vc_dst=self.vcTs[0:HD, s_, :, pr * 32:(pr + 1) * 32], kkey="kcTs", vkey="vcTs", dbl=True)
            for cch in range(CCH):
                w = min(128, NCBS - cch * 128)
                pv, pvk = self.psum_b()
                k.tr([(pv[0:w, g * HD:(g + 1) * HD], self.vcTs[0:HD, s_, g, cch * 128:cch * 128 + w]) for g in range(4)],
                     self.identb[0:HD, 0:HD], reads=("vcTs", "identb"), writes=(pvk,))
                k.op(k.act, lambda: nc.scalar.copy(out=self.V1cs[0:w, s_, cch, :, 0:HD],
                                                   in_=pv[0:w, 0:256].rearrange("p (g d) -> p g d", g=4)),
                     reads=(pvk,), writes=("V1cs",))

    def s_chunk(self, kv, kvkey, acc, first, last, QT, flag_ap, mask_ap=None, bias_j=None):
        k, nc = self.k, self.nc
        self.sc_i = getattr(self, "sc_i", 0) + 1
        par = self.sc_i % 3
        pt, ptk = self.psum_b()
        k.tr([(pt[0:HD, g * 128:(g + 1) * 128], kv[:, g * HD:(g + 1) * HD]) for g in range(4)], self.identb[:, :],
             reads=(kvkey, "identb"), writes=(ptk,))
        KT, KTk = self.KTp[0:HD, par, :, :], ("KTp", par)
        k.op(k.act, lambda: nc.scalar.copy(out=KT, in_=pt[0:HD, 0:512].rearrange("p (g n) -> p g n", g=4)), reads=(ptk,), writes=(KTk,))
        V1, V1k = self.V1p[:, par, :, :], ("V1p", par)
        k.op(k.dve, lambda: nc.vector.tensor_copy(out=V1[:, :, 0:HD], in_=kv[:, 256:512].rearrange("p (g d) -> p g d", g=4)),
             reads=(kvkey,), writes=(V1k,))
        k.op(k.dve, lambda: nc.vector.tensor_copy(out=V1[:, :, HD:HD + 1], in_=self.bc_mid(flag_ap, 4)), reads=("sc_pidx", "sc_wm", "sc_smc", "sc_sac", "sc_bmat", "ones32"), writes=(V1k,))
        if bias_j is not None:
            bx = self.biasx[0:8, 0, 2 * bias_j:2 * bias_j + 2]
            bsrc = bass.AP(bx.tensor, bx.offset, [list(bx.ap[0]), [136, 4], [1, 2], [0, 64]])
            k.op(k.dve, lambda: nc.vector.tensor_copy(out=self.biasE[0:8, par, :, :].rearrange("p g (r t) -> p g r t", r=2), in_=bsrc),
                 reads=("biasx",), writes=(("biasE", par),))
        ps, pk = self.psum()
        for g in range(4):
            pairs = [(KT[:, g, :], QT[0:HD, 4 * g:4 * g + 4, :])]
            if bias_j is not None:
                pairs.append((self.biasE[0:8, par, g, :], self.bmat[0:8, :]))
            k.mm(ps[:, g * 32:(g + 1) * 32], pairs, reads=(KTk, "QT_s", ("biasE", par), "sc_bmat"), writes=(pk,))
        P, Pk = self.Ps[:, par, :], ("Ps", par)
        k.op(k.act, lambda: nc.scalar.activation(out=P, in_=ps[:, 0:128], func=AF.Exp), reads=(pk,), writes=(Pk,))
        if mask_ap is not None:
            k.op(k.dve, lambda: nc.vector.tensor_tensor(out=P.rearrange("p (g n) -> p g n", g=4), in0=P.rearrange("p (g n) -> p g n", g=4),
                                                        in1=self.bc_mid(mask_ap, 4), op=ALU.mult), reads=(Pk, "sc_pidx", "sc_wm", "sc_smc", "sc_sac", "sc_bmat"), writes=(Pk,))
        return (acc, first, last, V1, V1k, P, Pk)

    def s_chunk_b(self, ctx):
        k = self.k
        acc, first, last, V1, V1k, P, Pk = ctx
        for g in range(4):
            a_, ak_ = acc[g]
            k.mm(a_[0:HD + 2, 0:32], [(V1[:, g, :], P[:, g * 32:(g + 1) * 32])], reads=(V1k, Pk), writes=(ak_,), start=first, stop=last)

    def nsa_sample(self, lb):
        k, nc = self.k, self.nc
        NSEQ, NPG, NCBS, TS, CCH, NBS = self.NSEQ, self.NPG, self.NCBS, self.TS, self.CCH, self.NBS
        NS = self.NS
        layer = 2 + lb
        self.rmsnorm_T(NS, 4 + layer)
        hn = self.hn
        wqg = self.w["nsa_w_qg"][lb].rearrange("(c p) f -> p c f", p=128)
        for blk in range(3):
            ncol = 512 if blk < 2 else 48
            wv, wk = self.wload(wqg[:, :, blk * 512:blk * 512 + ncol], (KC, ncol))
            for s_ in range(NSEQ):
                ps, pk = self.psum()
                k.mm(ps[0:TS, 0:ncol], [(hn[:, kk, s_ * TS:(s_ + 1) * TS], wv[:, kk, :]) for kk in range(KC)],
                     reads=(wk,) + HN_ALL, writes=(pk,))
                if blk < 2:
                    h0 = blk * 8

                    def mid(s_=s_, h0=h0):
                        k.op(k.act, lambda: nc.scalar.copy(out=self.qcb[0:TS, s_, h0:h0 + 8, :],
                                                           in_=self.qf[0:TS, :].rearrange("p (h d) -> p h d", h=8)),
                             reads=("qf",), writes=("qcb",))
                    self.headnorm_rope(ps[0:TS, 0:512], pk, TS, 8, self.qnrm[0:TS, lb, :], 1.0, self.ropes[0:TS, :], self.qf[0:TS, :], "qf", mid=mid)
                    k.op(k.act, lambda: nc.scalar.copy(out=self.qrb[0:TS, s_, h0:h0 + 8, :],
                                                       in_=self.qf[0:TS, :].rearrange("p (h d) -> p h d", h=8)),
                         reads=("qf",), writes=("qrb",))
                else:
                    k.op(k.act, lambda: nc.scalar.activation(out=self.gates[0:TS, s_, :], in_=ps[0:TS, 0:48], func=AF.Sigmoid),
                         reads=(pk,), writes=("gates",))
        oT = self.aT
        acc = [(self.ps[6], ("ps", 6)), (self.ps[7], ("ps", 7)), (self.ps[2], ("ps", 2)), (self.ps[3], ("ps", 3))]
        self.gen_list = [0, 1]
        ec, ep, imp, sm, m8 = self.ec_s, self.ep_s, self.imp_s, self.sm, self.m8
        nbp = NCBS // 2
        for s_ in range(NSEQ):
            pq, pqk = self.psum_b()
            k.tr([(pq[0:HD, (c * 16 + h) * 8:(c * 16 + h + 1) * 8], (self.qcb if c == 0 else self.qrb)[0:TS, s_, h, :])
                  for c in range(2) for h in range(16)], self.identb[0:TS, 0:TS], reads=("qcb", "qrb", "identb"), writes=(pqk,))
            k.op(k.act, lambda: nc.scalar.copy(out=self.QT_s[0:HD, :, :, :].rearrange("p c h q -> p (c h q)"), in_=pq[0:HD, 0:256]),
                 reads=(pqk,), writes=("QT_s",))
            QTc, QTr = self.QT_s[:, 0, :, :], self.QT_s[:, 1, :, :]
            for g in range(4):
                for hp in range(2):
                    pc, pck = self.psum()
                    for hh in range(2):
                        h = hp * 2 + hh
                        k.mm(pc[0:TS, hh * NCBS:(hh + 1) * NCBS], [(QTc[0:HD, 4 * g + h, :], self.kcTs[0:HD, s_, g, :])],
                             reads=("QT_s", "kcTs"), writes=(pck,))
                    k.op(k.act, lambda: nc.scalar.activation(out=ec[0:TS, hp * 2:hp * 2 + 2, :].rearrange("p h c -> p (h c)"),
                                                             in_=pc[0:TS, 0:2 * NCBS], func=AF.Exp), reads=(pck,), writes=("ec_s",))
                k.op(k.dve, lambda: nc.vector.tensor_reduce(out=ep[0:TS, :, :].rearrange("p h b -> p (h b)"),
                                                            in_=ec[0:TS, :, :].rearrange("p h (x two) -> p (h x) two", two=2),
                                                            axis=AX.X, op=ALU.add), reads=("ec_s",), writes=("ep_s",))
                k.op(k.dve, lambda: nc.vector.tensor_reduce(out=sm[0:TS, 0:4], in_=ep[0:TS, :, :], axis=AX.X, op=ALU.add),
                     reads=("ep_s",), writes=("sm",))
                k.op(k.dve, lambda: nc.vector.tensor_scalar(sm[0:TS, 0:4], sm[0:TS, 0:4], 1e-30, None, ALU.max), reads=("sm",), writes=("sm",))
                k.op(k.dve, lambda: nc.vector.reciprocal(out=sm[0:TS, 4:8], in_=sm[0:TS, 0:4]), reads=("sm",), writes=("sm",))
                k.op(k.dve, lambda: nc.vector.tensor_scalar(imp[0:TS, 0, 0:nbp], ep[0:TS, 0, :], sm[0:TS, 4:5], None, ALU.mult),
                     reads=("ep_s", "sm"), writes=("imp_s",))
                for h in range(1, 4):
                    k.op(k.dve, lambda h=h: nc.vector.scalar_tensor_tensor(out=imp[0:TS, 0, 0:nbp], in0=ep[0:TS, h, :], scalar=sm[0:TS, 4 + h:5 + h],
                                                                           in1=imp[0:TS, 0, 0:nbp], op0=ALU.mult, op1=ALU.add),
                         reads=("ep_s", "sm", "imp_s"), writes=("imp_s",))
                k.op(k.dve, lambda: nc.vector.tensor_tensor(out=imp[0:TS, 1, 0:NBS], in0=imp[0:TS, 0, 0:NBS], in1=self.smc[0:TS, 0:NBS], op=ALU.mult),
                     reads=("imp_s", "sc_pidx", "sc_wm", "sc_smc", "sc_sac", "sc_bmat"), writes=("imp_s",))
                k.op(k.dve, lambda: nc.vector.tensor_tensor(out=imp[0:TS, 1, 0:NBS], in0=imp[0:TS, 1, 0:NBS], in1=self.sac[0:TS, 0:NBS], op=ALU.add),
                     reads=("imp_s", "sc_pidx", "sc_wm", "sc_smc", "sc_sac", "sc_bmat"), writes=("imp_s",))
                k.op(k.dve, lambda: nc.vector.max(out=m8[0:TS, 0:8], in_=imp[0:TS, 1, 0:NBS]), reads=("imp_s",), writes=("m8",))
                k.op(k.dve, lambda: nc.vector.match_replace(out=imp[0:TS, 2, 0:NBS], in_to_replace=m8[0:TS, 0:8], in_values=imp[0:TS, 1, 0:NBS],
                                                            imm_value=-3e38), reads=("imp_s", "m8"), writes=("imp_s",))
                k.op(k.dve, lambda: nc.vector.max(out=m8[0:TS, 8:16], in_=imp[0:TS, 2, 0:NBS]), reads=("imp_s",), writes=("m8",))
                k.op(k.dve, lambda: nc.vector.tensor_scalar(imp[0:TS, 2, 0:NBS], imp[0:TS, 1, 0:NBS], m8[0:TS, 15:16], 30000.0, ALU.is_ge, ALU.mult),
                     reads=("imp_s", "m8"), writes=("imp_s",))
                k.op(k.dve, lambda: nc.vector.tensor_scalar(self.biasx[0:TS, g, 0:NBS], imp[0:TS, 2, 0:NBS], -30000.0, None, ALU.add),
                     reads=("imp_s",), writes=("biasx",))
            Oc, Ock = acc[0]
            for g in range(4):
                for cch in range(CCH):
                    w = min(128, NCBS - cch * 128)
                    ps, pk = self.psum()
                    k.mm(ps[0:w, 0:32], [(self.kcTs[0:HD, s_, g, cch * 128:cch * 128 + w], QTc[0:HD, 4 * g:4 * g + 4, :])],
                         reads=("kcTs", "QT_s"), writes=(pk,))
                    self.sc_i = getattr(self, "sc_i", 0) + 1
                    par = self.sc_i % 3
                    P, Pk = self.Ps[:, par, 0:32], ("Ps", par)
                    k.op(k.act, lambda: nc.scalar.activation(out=P[0:w, :], in_=ps[0:w, 0:32], func=AF.Exp), reads=(pk,), writes=(Pk,))
                    k.mm(Oc[0:HD + 2, g * 32:(g + 1) * 32], [(self.V1cs[0:w, s_, cch, g, :], P[0:w, :])], reads=("V1cs", Pk), writes=(Ock,),
                         start=(cch == 0), stop=(cch == CCH - 1))
            k.op(k.act, lambda: nc.scalar.copy(out=self.Oev_s[0:HD + 1, 0, :], in_=Oc[0:HD + 1, 0:128]), reads=(Ock,), writes=("Oev_s0",))
            pend = []
            for ch in range(5):
                if ch < 4:
                    self.sc_j = getattr(self, "sc_j", 0) + 1
                    pj = self.sc_j % 4
                    kv, kvk = self.kvpage[:, pj, :], ("kvpage", pj)
                    k.dma(k.pool, kv, self.cwin[s_, ch * 128:(ch + 1) * 128, :], (), (kvk,), f"pg{pj}")
                    ctx = self.s_chunk(kv, kvk, acc, ch == 0, False, QTr, self.ones32[:, 0:1], mask_ap=(self.wm[:, 0, :] if ch == 0 else None))
                else:
                    ctx = self.s_chunk(self.newwin[:, s_, :], ("newwin", s_), acc, False, True, QTr, self.wm[:, 2, 0:1], mask_ap=self.wm[:, 1, :])
                pend.append(ctx)
                if len(pend) > 2:
                    self.s_chunk_b(pend.pop(0))
            while pend:
                self.s_chunk_b(pend.pop(0))
            for g in range(4):
                k.op(k.act, lambda g=g: nc.scalar.copy(out=self.Oev_s[0:HD + 1, 2, g * 32:(g + 1) * 32], in_=acc[g][0][0:HD + 1, 0:32]),
                     reads=(acc[g][1],), writes=("Oev_s2",))
            pend = []
            for j in range(NPG + 1):
                if j < NPG:
                    self.sc_j = getattr(self, "sc_j", 0) + 1
                    pj = self.sc_j % 4
                    kv, kvk = self.kvpage[:, pj, :], ("kvpage", pj)
                    k.idma(kv, self.d_cache, self.idx_i[:, 1, s_ * NPG + j:s_ * NPG + j + 1], ("idx",), (kvk,), f"pg{pj}")
                    ctx = self.s_chunk(kv, kvk, acc, j == 0, False, QTr, self.ones32[:, 0:1], bias_j=j)
                else:
                    ctx = self.s_chunk(self.newsel[:, s_, :], ("newsel", s_), acc, False, True, QTr, self.wm[:, 2, 0:1], mask_ap=self.wm[:, 1, :])
                pend.append(ctx)
                if len(pend) > 2:
                    self.s_chunk_b(pend.pop(0))
            while pend:
                self.s_chunk_b(pend.pop(0))
            for g in range(4):
                k.op(k.act, lambda g=g: nc.scalar.copy(out=self.Oev_s[0:HD + 1, 1, g * 32:(g + 1) * 32], in_=acc[g][0][0:HD + 1, 0:32]),
                     reads=(acc[g][1],), writes=("Oev_s1",))
            for g in range(4):
                self.combine_g(TS, lambda b, h, g=g: self.Oev_s[0:HD + 1, b, (4 * g + h) * 8:(4 * g + h + 1) * 8],
                               ("Oev_s0", "Oev_s1", "Oev_s2"), self.gates[0:TS, s_, g * 12:(g + 1) * 12],
                               self.otok_s[0:TS, g * 256:(g + 1) * 256])
            pt, ptk = self.psum_b()
            k.tr([(pt[:, kk * 8:(kk + 1) * 8], self.otok_s[0:TS, kk * 128:(kk + 1) * 128]) for kk in range(KC)], self.identb[0:TS, 0:TS],
                 reads=("otok", "identb"), writes=(ptk,))
            k.op(k.act, lambda: nc.scalar.copy(out=oT[:, 0:KC, s_ * TS:(s_ + 1) * TS], in_=pt[:, 0:KC * 8].rearrange("p (c n) -> p c n", c=KC)),
                 reads=(ptk,), writes=tuple(("aT", c) for c in range(KC)))
        self.gen_list = [0, 1, 2, 3, 6, 7]
        w_o = self.w["nsa_w_o"][lb].rearrange("(c p) d -> p c d", p=128)
        for half in range(2):
            wv, wk = self.wload(w_o[:, :, half * 512:(half + 1) * 512], (KC, 512))
            for dd in range(4):
                dch = half * 4 + dd
                ps, pk = self.psum()
                k.mm(ps[:, 0:NS], [(wv[:, kk, dd * 128:(dd + 1) * 128], oT[:, kk, 0:NS]) for kk in range(KC)],
                     reads=(wk,) + tuple(("aT", c) for c in range(KC)), writes=(pk,))
                k.op(k.dve, lambda: nc.vector.tensor_tensor(out=self.hT[:, dch, 0:NS], in0=ps[:, 0:NS],
                                                            in1=self.hT[:, dch, 0:NS], op=ALU.add),
                     reads=(pk, ("hT", dch)), writes=(("hT", dch),))

    def build(self):
        k, nc = self.k, self.nc
        NT = self.NT
        self.setup()
        for t in range(self.NTILES):
            own = t >= self.OWN0
            to = t - self.OWN0
            k.dma(k.sp, self.hT[:, :, :], self.xT.rearrange("(c p) n -> p c n", p=128)[:, :, t * NT:(t + 1) * NT],
                  (), HT_ALL, "x")
            self.barrier(self.ALIAS_KEYS)
            for layer in range(2):
                self.ffn(NT, layer, "a")
                k.op(k.dve, lambda: nc.vector.tensor_copy(out=self.uext[:, :, 0:2], in_=self.carry[:, layer, :, :]),
                     reads=("carry",), writes=("uext",))
                self.conv_mixer(NT, layer, 1, NT)
                k.op(k.dve, lambda: nc.vector.tensor_copy(out=self.carry[:, layer, :, :], in_=self.uext[:, :, NT:NT + 2]),
                     reads=("uext",), writes=("carry",))
                self.ffn(NT, layer, "b")
            self.kv_rows(NT, t, False)
            if own:
                k.dma(k.sp, self.o_kv[to * NT:(to + 1) * NT, :].rearrange("(s p) f -> p s f", p=128), self.kvrow[:, :, :],
                      ("rowdst",), (), "okv")
                if self.nown - (to + 1) * NT < 512:
                    r0 = 512 - (self.nown - to * NT)
                    k.dma(k.sp, self.o_win[r0:r0 + NT, :].rearrange("(s p) f -> p s f", p=128), self.winrow[:, :, :],
                          ("rowdst",), (), "owin")
            if self.cfg.get("stage", 9) >= 2:
                self.compress(t)
            if own and self.cfg.get("stage", 9) >= -1:
                k.dma(k.sp, self.cmq[:, :, :], self.d_cmq[to], (), ("cmq",), "m0")
                k.dma(k.sp, self.cmt[:, :, :], self.d_cmt[to], (), ("cmt",), "m1")
                k.dma(k.sp, self.mulc[:, :, :], self.d_mulc[to], (), ("mulc",), "m2")
                k.dma(k.sp, self.addc[:, :, :], self.d_addc[to], (), ("addc",), "m3")
                for lb in range(2 if self.cfg.get("stage", 9) >= -1 else 0):
                    self.barrier(self.ALIAS_KEYS)
                    self.ffn(NT, 2 + lb, "a")
                    self.barrier(self.ALIAS_KEYS)
                    if self.cfg.get("stage", 9) >= 3:
                        self.nsa(NT, t, lb)
                    self.ffn(NT, 2 + lb, "b")
                k.dma(k.sp, self.o_yT.rearrange("(c p) n -> p c n", p=128)[:, :, to * NT:(to + 1) * NT], self.hT[:, :, :],
                      HT_ALL, (), "oy")
        k.dma(k.sp, self.o_conv.rearrange("l p c j -> p l c j"), self.carry[:, :, :, :], ("carry",), (), "oconv")
        NS, NSEQ, TS = self.NS, self.NSEQ, self.TS
        k.dma(k.sp, self.hT[:, :, 0:NS], self.xsT.rearrange("(c p) n -> p c n", p=128), (), HT_ALL, "x")
        self.barrier(self.ALIAS_KEYS)
        for layer in range(2):
            self.ffn(NS, layer, "a")
            W = TS + 2
            ue4 = self.uext[:, :, 0:NSEQ * W].rearrange("p c (s w) -> p c s w", s=NSEQ)
            for s_ in range(NSEQ):
                k.dma(k.sp, ue4[:, :, s_, 0:2], self.convst[:, layer, :, s_, :], (), ("uext",), "cst")
            self.conv_mixer(NS, layer, NSEQ, TS)
            for s_ in range(NSEQ):
                k.dma(k.sp, self.o_convs[layer, :, :, s_, :], ue4[:, :, s_, TS:TS + 2], ("uext",), (), "oconvs")
            self.ffn(NS, layer, "b")
        self.kv_rows(NS, 0, True)
        k.dma(k.sp, self.o_kvs[:, :], self.kvrow[0:NS, 0, :], ("rowdst",), (), "okvs")
        k.dma(k.sp, self.o_wins[:, 0:512 - TS, :], self.cwin[:, TS:512, :], (), (), "owins")
        for s_ in range(NSEQ):
            k.dma(k.sp, self.o_wins[s_, 512 - TS:512, :], self.winrow[s_ * TS:(s_ + 1) * TS, 0, :], ("rowdst",), (), "owins")
        if self.cfg.get("stage", 9) >= 5:
            self.sample_prep()
            for lb in range(2):
                self.barrier(self.ALIAS_KEYS)
                self.ffn(NS, 2 + lb, "a")
                self.barrier(self.ALIAS_KEYS)
                self.nsa_sample(lb)
                self.ffn(NS, 2 + lb, "b")
        k.dma(k.sp, self.o_ysT.rearrange("(c p) n -> p c n", p=128), self.hT[:, :, 0:NS], HT_ALL, (), "oys")
        k.finish()
        return nc


FULL_CFG = dict(NT=512, NTILES=8, OWN0=4, NSEQ=4, TS=8, PAST=8192, NPOOL=2560)


def _rope_table(pos):
    half = 8
    inv = ROPE_THETA ** (-np.arange(half, dtype=np.float32) / half)
    ang = pos.astype(np.float32)[:, None] * inv[None, :].astype(np.float32)
    return np.concatenate([np.cos(ang), np.sin(ang)], axis=1).astype(np.float32)


def prepare_inputs(inp, cfg, ncores=8):
    NT, NTILES, OWN0, NSEQ, TS = cfg["NT"], cfg["NTILES"], cfg["OWN0"], cfg["NSEQ"], cfg["TS"]
    nslots = NT * NTILES
    nown = (NTILES - OWN0) * NT
    npart = OWN0 * NT
    xp = np.asarray(inp["x_prompt"], np.float32)
    xs = np.asarray(inp["x_sample"], np.float32)
    past = cfg["PAST"]

    def pk(a):
        a = np.asarray(a, np.float32)
        lead = a.shape[:-1]
        return np.ascontiguousarray(np.moveaxis(a.reshape(lead + (KC, 128)), -1, 0))

    gains = np.concatenate([inp["ffn_a_norm"], inp["mix_norm"], inp["ffn_b_norm"], np.asarray(inp["kv_norm"])[None]], 0)
    gains_p = pk(gains)
    convw_p = pk(inp["conv_w"])
    knorm_p = np.ascontiguousarray(np.broadcast_to(np.asarray(inp["k_norm"], np.float32)[None], (128, 3, HD)))
    shared = {kname: np.asarray(inp[kname], np.float32) for kname in
              ["ffn_a_w_in", "ffn_a_w_out", "ffn_b_w_in", "ffn_b_w_out", "conv_w_in", "conv_w_out", "w_kv"]}
    for kname in ["nsa_w_qg", "nsa_w_o", "cmp_w1", "cmp_w2"]:
        shared[kname] = np.asarray(inp[kname], np.float32)
    ident = np.eye(128, dtype=np.float32)
    kk_, qq_ = np.meshgrid(np.arange(128), np.arange(128), indexing="ij")
    tri = np.stack([(kk_ <= qq_), (kk_ >= qq_)], axis=1).astype(np.float32)
    etab = (np.arange(nslots)[None, :] // 64 == np.arange(64)[:, None]).astype(np.float32)
    qnorm_p = np.ascontiguousarray(np.broadcast_to(np.asarray(inp["nsa_q_norm"], np.float32)[None], (128, 2, HD)))
    kn0col = np.ascontiguousarray(np.asarray(inp["k_norm"], np.float32)[0][:, None])
    pe = np.asarray(inp["cmp_pe"], np.float32)
    peT = np.ascontiguousarray(np.tile(pe.transpose(2, 0, 1), (1, 1, 4)))
    npool = cfg["NPOOL"]
    cache2 = np.ascontiguousarray(np.asarray(inp["cache_kv"], np.float32)).reshape(npool * 256, 512)
    ptab = np.asarray(inp["page_table"], np.int32)
    npg = past // 128
    pidx = (2.0 * np.arange(128, dtype=np.float32))[:, None]
    ii = np.arange(TS)
    wmasks = np.zeros((128, 3, 32), np.float32)
    rr = np.arange(128)
    wmasks[:, 0, :] = np.tile((rr[:, None] >= ii[None, :]).astype(np.float32), (1, 4))
    wmasks[:, 1, :] = np.tile(((rr[:, None] <= ii[None, :]) & (rr[:, None] < TS)).astype(np.float32), (1, 4))
    wmasks[:, 2, :] = (rr[:, None] < TS)
    bmat = np.tile(np.eye(8, dtype=np.float32), (1, 4))
    nbs = past // 64 + 1
    spos = past + ii
    sb = np.arange(136)
    scur = (spos // 64)[:, None]
    svalid = (sb[None, :] < nbs) & (64 * sb[None, :] <= spos[:, None])
    sf = np.zeros((TS, 136), np.float32)
    sf = np.where(sb[None, :] == 0, 1e4, sf)
    sf = np.where(sb[None, :] == scur, 2e4, sf)
    sf = np.where(sb[None, :] == scur - 1, 3e4, sf)
    saddc = np.where(svalid, np.where(sf > 0, sf, 0.0), -1e30).astype(np.float32)
    smulc = (svalid & ~(sf > 0)).astype(np.float32)
    in_maps = []
    for c in range(ncores):
        seq, half = c // 2, c % 2
        own = xp[seq, half * nown:(half + 1) * nown]
        if half == 1:
            partner = xp[seq, 0:npart]
            pos0 = 0
        else:
            partner = np.zeros((npart, D), np.float32)
            pos0 = -npart
        xall = np.concatenate([partner, own], 0)
        pos = pos0 + np.arange(nslots)
        rope_p = _rope_table(np.maximum(pos, 0)).reshape(nslots // 128, 128, 16).transpose(1, 0, 2)
        xs_c = xs[c * NSEQ:(c + 1) * NSEQ].reshape(NSEQ * TS, D)
        rope_s = _rope_table(np.tile(past + np.arange(TS), NSEQ))
        cst = np.asarray(inp["state_conv"], np.float32)[:, c * NSEQ:(c + 1) * NSEQ]
        cst_p = pk(cst)
        cst_p = np.ascontiguousarray(cst_p.transpose(0, 1, 4, 2, 3))
        NCH, NCB, NBLK, NOT = nslots // 128, nslots // 32, nslots // 64, NTILES - OWN0
        flags = np.zeros((128, NCH + 1), np.float32)
        flags[:, :NCH] = (pos >= 0).reshape(NCH, 128).T
        creal = (32 * np.arange(NCB) + pos0 >= 0)
        flags[:NCB, NCH] = creal
        sq = (OWN0 * NT + np.arange(NOT * NT)).reshape(NOT, 4, 128)
        cend = 32 * np.arange(NCB) + 31
        cm = ((cend[None, None, None, :] <= sq[..., None]) & creal[None, None, None, :]).astype(np.float32)
        cmq = np.ascontiguousarray(cm.transpose(0, 2, 1, 3))
        cmt = np.zeros((NOT, 128, 4, 128), np.float32)
        cmt[:, :NCB] = cm.transpose(0, 3, 1, 2)
        bl = np.arange(NBLK)
        breal = (64 * bl + pos0 >= 0)
        b0 = (-pos0) // 64
        cur = (sq // 64)[..., None]
        valid = breal[None, None, None, :] & (64 * bl[None, None, None, :] <= sq[..., None])
        fval = np.zeros(valid.shape, np.float32)
        fval = np.where(bl[None, None, None, :] == b0, 1e4, fval)
        fval = np.where(bl[None, None, None, :] == cur, 2e4, fval)
        fval = np.where(bl[None, None, None, :] == cur - 1, 3e4, fval)
        forced = fval > 0
        addc = np.where(valid, np.where(forced, fval, 0.0), -1e30).astype(np.float32)
        mulc = (valid & ~forced).astype(np.float32)
        addc = np.ascontiguousarray(addc.transpose(0, 2, 1, 3))
        mulc = np.ascontiguousarray(mulc.transpose(0, 2, 1, 3))
        m = dict(shared)
        m.update(cache_kv=cache2, page_table=np.ascontiguousarray(ptab[c * NSEQ:(c + 1) * NSEQ].reshape(1, NSEQ * npg)),
                 pidx=pidx, wmasks=wmasks, bmat=bmat, smulc=smulc, saddc=saddc)
        m.update(ident=ident, tri=tri, etab=etab, flags=flags, qnorm=qnorm_p, kn0col=kn0col, peT=peT,
                 cmq=cmq, cmt=cmt, mulc=mulc, addc=addc)
        m.update(xT=np.ascontiguousarray(xall.T), xsT=np.ascontiguousarray(xs_c.T), convst=cst_p,
                 cwin=np.ascontiguousarray(np.asarray(inp["cache_win"], np.float32)[c * NSEQ:(c + 1) * NSEQ].reshape(NSEQ, 512, 512)),
                 gains=gains_p, convw=convw_p, knorm=knorm_p, rope_p=np.ascontiguousarray(rope_p), rope_s=rope_s)
        in_maps.append(m)
    return in_maps


def assemble(res, inp, cfg, ncores=8):
    NT, NTILES, OWN0, NSEQ, TS = cfg["NT"], cfg["NTILES"], cfg["OWN0"], cfg["NSEQ"], cfg["TS"]
    nown = (NTILES - OWN0) * NT
    B = ncores // 2
    SEQ = 2 * nown
    y_p = np.zeros((B, SEQ, D), np.float32)
    y_s = np.zeros((ncores * NSEQ, TS, D), np.float32)
    kv_p = np.zeros((B, SEQ, 4, NKV, HD), np.float32)
    kv_s = np.zeros((ncores * NSEQ, TS, 4, NKV, HD), np.float32)
    win_p = np.zeros((B, 512, 2, NKV, HD), np.float32)
    win_s = np.zeros((ncores * NSEQ, 512, 2, NKV, HD), np.float32)
    conv_p = np.zeros((2, B, 2, D), np.float32)
    conv_s = np.zeros((2, ncores * NSEQ, 2, D), np.float32)
    for c in range(ncores):
        r = res[c]
        seq, half = c // 2, c % 2
        y_p[seq, half * nown:(half + 1) * nown] = r["o_yT"].T
        y_s[c * NSEQ:(c + 1) * NSEQ] = r["o_ysT"].T.reshape(NSEQ, TS, D)
        kv_p[seq, half * nown:(half + 1) * nown] = r["o_kv"].reshape(nown, 4, NKV, HD)
        kv_s[c * NSEQ:(c + 1) * NSEQ] = r["o_kvs"].reshape(NSEQ, TS, 4, NKV, HD)
        if half == 1:
            win_p[seq] = r["o_win"].reshape(512, 2, NKV, HD)
            conv_p[:, seq] = r["o_conv"].transpose(0, 3, 2, 1).reshape(2, 2, D)
        conv_s[:, c * NSEQ:(c + 1) * NSEQ] = r["o_convs"].transpose(0, 3, 4, 2, 1).reshape(2, NSEQ, 2, D)
        win_s[c * NSEQ:(c + 1) * NSEQ] = r["o_wins"].reshape(NSEQ, 512, 2, NKV, HD)
    return (y_p, y_s, kv_p, kv_s, win_p, win_s, conv_p, conv_s)


def kernel(**inputs):
    cfg = FULL_CFG
    b = Builder(cfg)
    nc = b.build()
    in_maps = prepare_inputs(inputs, cfg)
    res = run_bass_kernel_spmd(nc, in_maps, core_ids=list(range(8)))
    return assemble(res.results, inputs, cfg)
```

```python
import numpy as np
import concourse.bass as bass
import concourse.mybir as mybir
from concourse.bass_utils import run_bass_kernel_spmd

F32, BF16, I32 = mybir.dt.float32, mybir.dt.bfloat16, mybir.dt.int32
AF = mybir.ActivationFunctionType
ALU = mybir.AluOpType
AX = mybir.AxisListType

D = 1024
KC = 8
DFF = 2816
FC = 22
HD = 64
NKV = 4
NH = 16
EPS = 1e-6
ROPE_THETA = 500000.0
WSLOT = 4096
NWSLOT = 3
HT_ALL = tuple(("hT", c) for c in range(KC))
HN_ALL = tuple(("hn", c) for c in range(KC))


class Eng:
    def __init__(self, nc, name, e):
        self.name = name
        self.e = e
        self.sem = nc.alloc_semaphore("s_" + name)
        self.cnt = 0
        self.seen = {}


class KB:
    def __init__(self, nc):
        self.nc = nc
        self.pe = Eng(nc, "pe", nc.tensor)
        self.act = Eng(nc, "act", nc.scalar)
        self.dve = Eng(nc, "dve", nc.vector)
        self.pool = Eng(nc, "pool", nc.gpsimd)
        self.sp = Eng(nc, "sp", nc.sync)
        self.lw = {}
        self.rd = {}
        self.dsems = {}
        self.sems = {}
        self.ps_i = 0
        self.w_i = 0
        self.n_inst = 0

    def _deps(self, reads, writes):
        evs = []
        for r in reads:
            if r in self.lw:
                evs.append(self.lw[r])
        for w in writes:
            if w in self.lw:
                evs.append(self.lw[w])
            evs.extend(self.rd.get(w, {}).values())
        return evs

    def _wait(self, E, evs, skip_self=False):
        best = {}
        for sem, val in evs:
            k = id(sem)
            if k not in best or best[k][1] < val:
                best[k] = (sem, val)
        for k, (sem, val) in best.items():
            if skip_self and sem is E.sem:
                continue
            if E.seen.get(k, 0) >= val:
                continue
            E.e.wait_ge(sem, val)
            self.n_inst += 1
            E.seen[k] = val

    def _commit(self, ev, reads, writes):
        for w in writes:
            self.lw[w] = ev
            self.rd[w] = {}
        for r in reads:
            d = self.rd.setdefault(r, {})
            k = id(ev[0])
            if k not in d or d[k][1] < ev[1]:
                d[k] = ev

    def op(self, E, fn, reads=(), writes=(), skip_self=False):
        self._wait(E, self._deps(reads, writes), skip_self)
        inst = fn()
        self.n_inst += 1
        E.cnt += 1
        inst.then_inc(E.sem, 1)
        self._commit((E.sem, E.cnt), reads, writes)
        return inst

    def mm(self, out, pairs, reads, writes, start=True, stop=True):
        E = self.pe
        self._wait(E, self._deps(reads, writes), skip_self=True)
        n = len(pairs)
        inst = None
        for i, (l, r) in enumerate(pairs):
            inst = self.nc.tensor.matmul(out, lhsT=l, rhs=r, start=(start and i == 0), stop=(stop and i == n - 1))
            self.n_inst += 1
        E.cnt += 1
        inst.then_inc(E.sem, 1)
        self._commit((E.sem, E.cnt), reads, writes)

    def tr(self, outs_ins, ident, reads, writes):
        E = self.pe
        self._wait(E, self._deps(reads, writes), skip_self=True)
        inst = None
        for o, i_ in outs_ins:
            inst = self.nc.tensor.transpose(o, i_, ident)
            self.n_inst += 1
        E.cnt += 1
        inst.then_inc(E.sem, 1)
        self._commit((E.sem, E.cnt), reads, writes)

    def dma(self, Q, out, in_, reads, writes, semname, **kw):
        self._wait(Q, self._deps(reads, writes))
        if semname not in self.dsems:
            self.dsems[semname] = [self.nc.alloc_semaphore("d_" + semname), 0]
        ent = self.dsems[semname]
        inst = Q.e.dma_start(out=out, in_=in_, **kw)
        self.n_inst += 1
        ent[1] += 16
        inst.then_inc(ent[0], 16)
        self._commit((ent[0], ent[1]), reads, writes)

    def dma_multi(self, Q, parts, reads, writes, semname):
        self._wait(Q, self._deps(reads, writes))
        if semname not in self.dsems:
            self.dsems[semname] = [self.nc.alloc_semaphore("d_" + semname), 0]
        ent = self.dsems[semname]
        for out, in_ in parts:
            inst = Q.e.dma_start(out=out, in_=in_)
            self.n_inst += 1
            ent[1] += 16
            inst.then_inc(ent[0], 16)
        self._commit((ent[0], ent[1]), reads, writes)

    def idma(self, out, in_, idx_ap, reads, writes, semname):
        Q = self.pool
        self._wait(Q, self._deps(reads, writes))
        if semname not in self.dsems:
            self.dsems[semname] = [self.nc.alloc_semaphore("d_" + semname), 0]
        ent = self.dsems[semname]
        inst = self.nc.gpsimd.indirect_dma_start(out=out, out_offset=None, in_=in_,
                                                 in_offset=bass.IndirectOffsetOnAxis(ap=idx_ap, axis=0))
        self.n_inst += 1
        ent[1] += 16
        inst.then_inc(ent[0], 16)
        self._commit((ent[0], ent[1]), reads, writes)

    def finish(self):
        for name, (sem, val) in self.dsems.items():
            if val:
                self.nc.sync.wait_ge(sem, val)


def sl(ap_t, *idx):
    return ap_t[idx]


class Builder:
    def __init__(self, cfg):
        self.cfg = cfg
        nc = bass.Bass("TRN2", target_bir_lowering=False)
        self.nc = nc
        self.k = KB(nc)
        self.NT = cfg["NT"]
        self.NTILES = cfg["NTILES"]
        self.OWN0 = cfg["OWN0"]
        self.NSEQ = cfg["NSEQ"]
        self.TS = cfg["TS"]
        self.NS = self.NSEQ * self.TS
        self.nslots = self.NTILES * self.NT
        self.nown = (self.NTILES - self.OWN0) * self.NT
        self.NCH = self.nslots // 128
        self.NCB = self.nslots // 32
        self.NBLK = self.nslots // 64
        self.NOT = self.NTILES - self.OWN0
        self.declare_io()
        self.alloc()

    def declare_io(self):
        nc = self.nc
        NT, NS = self.NT, self.NS

        def din(name, shape, dt=F32):
            return nc.dram_tensor(name, list(shape), dt, kind="ExternalInput").ap()

        def dout(name, shape, dt=F32):
            return nc.dram_tensor(name, list(shape), dt, kind="ExternalOutput").ap()

        self.xT = din("xT", [D, self.nslots])
        self.xsT = din("xsT", [D, NS])
        self.convst = din("convst", [128, 2, KC, self.NSEQ, 2])
        self.gains = din("gains", [128, 13, KC])
        self.convw = din("convw", [128, 2, 3, KC])
        self.knorm = din("knorm", [128, 3, HD])
        self.rope_p = din("rope_p", [128, self.nslots // 128, 16])
        self.rope_s = din("rope_s", [NS, 16])
        self.w = {}
        self.w["ffn_a_w_in"] = din("ffn_a_w_in", [4, D, 2 * DFF])
        self.w["ffn_a_w_out"] = din("ffn_a_w_out", [4, DFF, D])
        self.w["ffn_b_w_in"] = din("ffn_b_w_in", [4, D, 2 * DFF])
        self.w["ffn_b_w_out"] = din("ffn_b_w_out", [4, DFF, D])
        self.w["conv_w_in"] = din("conv_w_in", [2, D, 3 * D])
        self.w["conv_w_out"] = din("conv_w_out", [2, D, D])
        self.w["w_kv"] = din("w_kv", [D, 1536])
        if self.cfg.get("stage", 9) > -3:
            self.w["nsa_w_qg"] = din("nsa_w_qg", [2, D, 1072])
            self.w["nsa_w_o"] = din("nsa_w_o", [2, D, D])
            self.w["cmp_w1"] = din("cmp_w1", [2, 32, HD, 256])
            self.w["cmp_w2"] = din("cmp_w2", [2, 256, HD])
        if self.cfg.get("stage", 9) > -3:
            self.d_ident = din("ident", [128, 128])
            self.d_tri = din("tri", [128, 2, 128])
            self.d_etab = din("etab", [64, self.nslots])
            self.d_flags = din("flags", [128, self.NCH + 1])
            self.d_qnorm = din("qnorm", [128, 2, HD])
            self.d_kn0 = din("kn0col", [64, 1])
            self.d_pe = din("peT", [64, 2, 128])
            self.d_cmq = din("cmq", [self.NOT, 128, 4, self.NCB])
            self.d_cmt = din("cmt", [self.NOT, 128, 4, 128])
            self.d_mulc = din("mulc", [self.NOT, 128, 4, self.NBLK])
            self.d_addc = din("addc", [self.NOT, 128, 4, self.NBLK])
        self.PAST = self.cfg["PAST"]
        self.NPG = self.PAST // 128
        self.NCBS = self.PAST // 32
        self.NBS = self.PAST // 64 + 1
        self.d_cache = din("cache_kv", [self.cfg["NPOOL"] * 256, 512])
        self.d_pt = din("page_table", [1, self.NSEQ * self.NPG], I32)
        self.d_pidx = din("pidx", [128, 1])
        self.d_wm = din("wmasks", [128, 3, 32])
        self.d_bmat = din("bmat", [8, 32])
        self.d_smc = din("smulc", [8, 136])
        self.d_sac = din("saddc", [8, 136])
        self.o_yT = dout("o_yT", [D, self.nown])
        self.o_ysT = dout("o_ysT", [D, NS])
        self.o_kv = dout("o_kv", [self.nown, 1024])
        self.o_kvs = dout("o_kvs", [NS, 1024])
        self.o_win = dout("o_win", [512, 512])
        self.o_wins = dout("o_wins", [self.NSEQ, 512, 512])
        self.cwin = din("cwin", [self.NSEQ, 512, 512])
        self.o_conv = dout("o_conv", [2, 128, KC, 2])
        self.o_convs = dout("o_convs", [2, 128, KC, self.NSEQ, 2])

    def alloc(self):
        nc = self.nc
        NT = self.NT
        def A(name, shape, dt):
            return nc.alloc_sbuf_tensor("sb_" + name, shape, dt)
        self.hT = A("hT", [128, KC, NT], F32)
        self.hn = A("hn", [128, KC, NT], BF16)
        self.aTf = A("aTf", [128, FC * NT // 2], F32)
        a_aT = nc.lookup_mloc("sb_aTf").addr
        self.aT = nc.alloc_sbuf_tensor_at("sb_aT", [128, FC, NT], BF16, offset=a_aT)
        self.rinv = A("rinv", [128, NT], F32)
        self.tmpa = A("tmpa", [128, NT], F32)
        self.tmpb = A("tmpb", [128, NT], F32)
        self.sg = A("sg", [128, 2, NT], BF16)
        self.uext = A("uext", [128, KC, NT + 2 * max(1, self.NSEQ)], F32)
        a_ue = nc.lookup_mloc("sb_uext").addr
        self.qcb = nc.alloc_sbuf_tensor_at("sb_qcb", [128, 4, 16, HD], BF16, offset=a_ue)
        self.qrb = nc.alloc_sbuf_tensor_at("sb_qrb", [128, 4, 16, HD], BF16, offset=a_ue + 8192)
        self.carry = A("carry", [128, 2, KC, 2], F32)
        self.kvrow = self.aTf[:, 0:(NT // 128) * 1024].rearrange("p (s f) -> p s f", f=1024)
        uf = self.uext[:, :, :].rearrange("p c n -> p (c n)")
        self.winrow = uf[:, 0:(NT // 128) * 512].rearrange("p (s f) -> p s f", f=512)
        self.rawT = nc.alloc_sbuf_tensor_at("sb_rawT", [128, 8, NT], BF16, offset=a_ue + 8192)
        self.ones = A("ones", [128, 128], BF16)
        self.g32 = A("g32", [128, 13, KC], F32)
        self.cw = A("cw", [128, 2, 3, KC], F32)
        self.kn = A("kn", [128, 3, HD], F32)
        self.ropep = A("ropep", [128, self.nslots // 128, 16], F32)
        self.ropes = A("ropes", [128, 16], F32)
        self.epsb_t = A("epsb", [128, 2], F32)
        self.epsb = {float(D * EPS): self.epsb_t[:, 0:1], float(HD * EPS): self.epsb_t[:, 1:2]}
        self.wslots = [A(f"w{i}", [128, WSLOT], BF16) for i in range(NWSLOT)]
        NCH, NCB, NBLK = self.NCH, self.NCB, self.NBLK
        self.dummy = A("dummy_t", [128, 1], F32)
        self.ones32 = A("ones32", [128, 1], F32)
        self.identb = A("identb", [128, 128], BF16)
        self.identf = A("identf", [128, 128], F32)
        self.tri = A("tri", [128, 2, 128], F32)
        self.Kaug = A("Kaug", [128, 4, self.nslots], BF16)
        self.V1s = A("V1s", [128, NCH, 4, 66], BF16)
        self.kwT = A("kwT", [128, 4, 1024], BF16)
        self.V1w = A("V1w", [128, 8, 4, 66], BF16)
        self.kcT = A("kcT", [128, 4, NCB], BF16)
        self.vcT = A("vcT", [128, 4, NCB], BF16)
        self.V1c = A("V1c", [128, 4, 66], BF16)
        self.flags = A("flags", [128, NCH + 1], F32)
        self.rowb = A("rowb", [128, NT // 128, 512], BF16)
        self.kb2 = A("kb2", [128, 2, 256], BF16)
        self.hidT = A("hidT", [128, 256], BF16)
        self.gl = A("gl", [128, 3, 256], F32)
        self.w2sb = A("w2sb", [128, 2, 2, HD], BF16)
        self.peT = A("peT", [128, 2, 128], F32)
        self.kn0 = A("kn0", [128, 1], F32)
        self.qnrm = A("qnrm", [128, 2, HD], F32)
        self.qf = A("qf", [128, 512], F32)
        self.sq8 = A("sq8", [128, 512], F32)
        self.xn8 = A("xn8", [128, 512], F32)
        self.ss16 = A("ss16", [128, 16], F32)
        self.rt8 = A("rt8", [128, 4, 8, 8], F32)
        self.gates = A("gates", [128, 4, 48], F32)
        self.QcT = A("QcT", [128, 512], BF16)
        self.Qaug = A("Qaug", [128, 512], BF16)
        self.Pb = A("Pb", [128, 4, 512], BF16)
        self.ec = A("ec", [128, 4 * NCB], F32)
        self.ep = A("ep", [128, 4, NBLK], F32)
        self.sm = A("sm", [128, 64], F32)
        self.imp = A("imp", [128, 3, NBLK], F32)
        self.m8 = A("m8", [128, 16], F32)
        self.selpad = A("selpad", [128, 128], BF16)
        self.Oev = A("Oev", [128, 3, 512], F32)
        self.oacc = A("oacc", [128, 2, 256], F32)
        self.otok = A("otok", [128, 1024], BF16)
        self.cmq = A("cmq", [128, 4, NCB], F32)
        self.cmt = A("cmt", [128, 4, 128], F32)
        self.mulc = A("mulc", [128, 4, NBLK], F32)
        self.addc = A("addc", [128, 4, NBLK], F32)
        NSEQ, NPG, NCBS = self.NSEQ, self.NPG, self.NCBS
        CCH = (NCBS + 127) // 128
        self.CCH = CCH
        need = 60 * 1024
        a0 = nc.lookup_mloc("sb_Kaug").addr
        a1 = nc.lookup_mloc("sb_V1w").addr + 8 * 4 * 66 * 2
        self.sa_base = a0
        self.sa_off = 0
        use_alias = (a1 - a0 >= need)

        def SA(name, shape, dt):
            if not use_alias:
                return A(name, shape, dt)
            nbytes = int(np.prod(shape[1:])) * (2 if dt == BF16 else 4)
            t_ = nc.alloc_sbuf_tensor_at("sb_" + name, shape, dt, offset=self.sa_base + self.sa_off)
            self.sa_off += (nbytes + 31) // 32 * 32
            assert self.sa_off <= a1 - a0, self.sa_off
            return t_
        self.kcTs = SA("kcTs", [128, NSEQ, 4, NCBS], BF16)
        self.vcTs = SA("vcTs", [128, NSEQ, 4, NCBS], BF16)
        self.V1cs = SA("V1cs", [128, NSEQ, CCH, 4, 66], BF16)
        self.kvpage = SA("kvpage", [128, 4, 512], BF16)
        self.KTp = SA("KTp", [128, 3, 4, 128], BF16)
        self.V1p = SA("V1p", [128, 3, 4, 66], BF16)
        self.newsel = SA("newsel", [128, NSEQ, 512], BF16)
        self.newwin = SA("newwin", [128, NSEQ, 512], BF16)
        self.pt_i = SA("pt_i", [128, NSEQ * NPG], I32)
        self.idxf = SA("idxf", [128, NSEQ * NPG], F32)
        self.idx_i = SA("idx_i", [128, 2, NSEQ * NPG], I32)
        self.ec_s = SA("ec_s", [128, 4, NCBS], F32)
        self.ep_s = SA("ep_s", [128, 4, NCBS // 2], F32)
        self.imp_s = SA("imp_s", [128, 3, 136], F32)
        self.biasx = SA("biasx", [128, 4, 136], BF16)
        self.Oev_s = SA("Oev_s", [128, 3, 128], F32)
        self.Ps = SA("Ps", [128, 3, 128], BF16)
        self.QT_s = SA("QT_s", [128, 2, 16, 8], BF16)
        self.wm = SA("wm", [128, 3, 32], F32)
        self.bmat = SA("bmat", [128, 32], BF16)
        self.smc = SA("smc", [128, 136], F32)
        self.sac = SA("sac", [128, 136], F32)
        self.pidx = SA("pidx", [128, 1], F32)
        self.otok_s = SA("otok_s", [128, 1024], BF16)
        self.rawT2 = nc.alloc_sbuf_tensor_at("sb_rawT2", [128, 8, 2 * NT], BF16, offset=a_ue)
        aQ = nc.lookup_mloc("sb_QcT").addr
        aQe = nc.lookup_mloc("sb_ep").addr + 4 * NBLK * 4
        if aQe - aQ >= 7168:
            self.gl2 = nc.alloc_sbuf_tensor_at("sb_gl2", [128, 3, 512], F32, offset=aQ)
            self.hidT2 = nc.alloc_sbuf_tensor_at("sb_hidT2", [128, 512], BF16, offset=aQ + 6144)
        else:
            self.gl2 = A("gl2", [128, 3, 512], F32)
            self.hidT2 = A("hidT2", [128, 512], BF16)
        aO = nc.lookup_mloc("sb_Oev").addr
        aT_ = nc.lookup_mloc("sb_otok").addr
        aE = nc.lookup_mloc("sb_addc").addr + 4 * NBLK * 4
        self.rawTB = None
        if aT_ - aO >= 4096:
            self.rowbB = nc.alloc_sbuf_tensor_at("sb_rowbB", [128, NT // 128, 512], BF16, offset=aO)
        else:
            self.rowbB = A("rowbB", [128, NT // 128, 512], BF16)
        self.biasE = SA("biasE", [128, 3, 4, 128], BF16)
        print("sbuf bytes remaining", nc.sbuf_bytes_remaining() if callable(nc.sbuf_bytes_remaining) else nc.sbuf_bytes_remaining)
        self.ps = [nc.alloc_psum_tensor(f"ps{i}", [128, 512], F32) if i not in (4, 5) else
                   nc.alloc_psum_tensor(f"ps{i}", [128, 1024], BF16) for i in range(8)]
        self.psb_i = 0
        self.gen_list = [0, 1, 2, 3, 6, 7]

    def psum(self):
        lst = self.gen_list
        i = lst[self.k.ps_i % len(lst)]
        self.k.ps_i += 1
        return self.ps[i], ("ps", i)

    def psum_b(self):
        i = 4 + self.psb_i % 2
        self.psb_i += 1
        return self.ps[i], ("ps", i)

    def wload(self, src_ap, shape, nparts=128):
        k = self.k
        i = k.w_i % NWSLOT
        k.w_i += 1
        n = int(np.prod(shape))
        assert n <= WSLOT
        t = self.wslots[i]
        if len(shape) == 2:
            view = t[0:nparts, 0:n].rearrange("p (a b) -> p a b", a=shape[0])
        elif len(shape) == 3:
            view = t[0:nparts, 0:n].rearrange("p (a b c) -> p a b c", a=shape[0], b=shape[1])
        else:
            raise ValueError
        if isinstance(src_ap, (list, tuple)):
            k.dma_multi(k.pool, [(view[:, :, j, :], sa) for j, sa in enumerate(src_ap)], reads=(), writes=(("w", i),), semname=f"w{i}")
        else:
            k.dma(k.pool, view, src_ap, reads=(), writes=(("w", i),), semname=f"w{i}")
        return view, ("w", i)

    def setup(self):
        k, nc = self.k, self.nc
        k.op(k.dve, lambda: nc.vector.memset(self.ones[:], 1.0), writes=("ones",))
        k.dma(k.sp, self.g32[:], self.gains, (), ("g32",), "c0")
        k.dma(k.sp, self.cw[:], self.convw, (), ("cw",), "c1")
        k.dma(k.sp, self.kn[:], self.knorm, (), ("kn",), "c2")
        k.dma(k.sp, self.ropep[:], self.rope_p, (), ("ropep",), "c3")
        k.dma(k.sp, self.ropes[0:self.NS, :], self.rope_s, (), ("ropes",), "c4")
        k.op(k.dve, lambda: nc.vector.tensor_scalar(self.g32[:], self.g32[:], 32.0, None, ALU.mult),
             reads=("g32",), writes=("g32",))
        k.op(k.dve, lambda: nc.vector.memset(self.carry[:], 0.0), writes=("carry",))
        k.op(k.dve, lambda: nc.vector.memset(self.ones32[:], 1.0), writes=("ones32",))
        k.op(k.dve, lambda: nc.vector.memset(self.epsb_t[:, 0:1], float(D * EPS)), writes=("epsb",))
        k.op(k.dve, lambda: nc.vector.memset(self.epsb_t[:, 1:2], float(HD * EPS)), writes=("epsb",))
        if self.cfg.get('stage', 9) >= 0:
            self.setup2()

    def setup2(self):
        k, nc = self.k, self.nc
        k.dma(k.pool, self.identb[:, :], self.d_ident, (), ("identb",), "c5")
        k.dma(k.sp, self.identf[:, :], self.d_ident, (), ("identf",), "c6")
        k.dma(k.sp, self.tri[:, :, :], self.d_tri, (), ("tri",), "c7")
        k.dma(k.sp, self.flags[:, :], self.d_flags, (), ("flags",), "c8")
        k.dma(k.sp, self.qnrm[:, :, :], self.d_qnorm, (), ("qnrm",), "c9")
        k.dma(k.sp, self.kn0[0:HD, :], self.d_kn0, (), ("kn0",), "c10")
        k.dma(k.sp, self.peT[0:HD, :, :], self.d_pe, (), ("peT",), "c11")
        for e in range(2):
            k.dma(k.pool, self.w2sb[:, e, :, :], self.w["cmp_w2"][e].rearrange("(fc p) d -> p fc d", p=128), (), ("w2sb",), "c12")
        k.op(k.dve, lambda: nc.vector.memset(self.Kaug[:, :, :], 0.0), writes=("Kaug",))
        for g in range(4):
            k.dma(k.pool, self.Kaug[HD:128, g, :], self.d_etab, (), ("Kaug",), "c13")
        for t_, kname in ((self.V1s, "V1s"), (self.V1w, "V1w"), (self.V1c, "V1c"), (self.kcT, "kcT"), (self.vcT, "vcT"),
                          (self.kwT, "kwT"), (self.selpad, "selpad")):
            k.op(k.dve, lambda t_=t_: nc.vector.memset(t_[:], 0.0), writes=(kname,))
        k.op(k.dve, lambda: nc.vector.tensor_copy(out=self.V1c[:, :, HD:HD + 1], in_=self.bc_mid(self.flags[:, self.NCH:self.NCH + 1], 4)),
             reads=("flags",), writes=("V1c",))

    def rsqrt(self, out, in_, addc, rkeys, wkeys):
        k, nc = self.k, self.nc
        np_ = out.shape[0]
        k.op(k.act, lambda: nc.scalar.activation(out=out, in_=in_, func=AF.Sqrt, bias=self.epsb[addc][0:np_, :], scale=1.0),
             reads=tuple(rkeys) + ("epsb",), writes=tuple(wkeys))
        k.op(k.dve, lambda: nc.vector.reciprocal(out=out, in_=out), reads=tuple(wkeys), writes=tuple(wkeys))

    def rmsnorm_T(self, N, gi):
        k, nc = self.k, self.nc
        hT, hn = self.hT, self.hn
        k.op(k.act, lambda: nc.scalar.activation(out=hn[:, 0:5, 0:N], in_=hT[:, 0:5, 0:N], func=AF.Square),
             reads=HT_ALL[0:5], writes=HN_ALL[0:5])
        k.op(k.dve, lambda: nc.vector.tensor_tensor(out=hn[:, 5:KC, 0:N], in0=hT[:, 5:KC, 0:N], in1=hT[:, 5:KC, 0:N], op=ALU.mult),
             reads=HT_ALL[5:KC], writes=HN_ALL[5:KC])
        ps, pk = self.psum()
        k.mm(ps[:, 0:N], [(self.ones[:, :], hn[:, c, 0:N]) for c in range(KC)],
             reads=("ones",) + HN_ALL, writes=(pk,))
        self.rsqrt(self.rinv[:, 0:N], ps[:, 0:N], float(D * EPS), (pk,), ("rinv",))
        for c in range(KC):
            k.op(k.dve, lambda c=c: nc.vector.scalar_tensor_tensor(
                out=hn[:, c, 0:N], in0=hT[:, c, 0:N], scalar=self.g32[:, gi, c:c + 1], in1=self.rinv[:, 0:N],
                op0=ALU.mult, op1=ALU.mult),
                reads=(("hT", c), "rinv", "g32"), writes=(("hn", c),))

    def ffn(self, N, layer, which):
        k, nc = self.k, self.nc
        gi = (0 if which == "a" else 8) + layer
        w_in = self.w[f"ffn_{which}_w_in"]
        w_out = self.w[f"ffn_{which}_w_out"]
        self.rmsnorm_T(N, gi)
        hn, aT = self.hn, self.aT
        for b in range(FC // 2):
            src4 = w_in[layer].rearrange("(c p) (two f) -> p c two f", p=128, two=2)
            wv, wk = self.wload([src4[:, :, j, b * 256:(b + 1) * 256] for j in range(2)], (KC, 2, 256))
            for cc in range(2):
                c = 2 * b + cc
                pg, pgk = self.psum()
                pu, puk = self.psum()
                k.mm(pg[:, 0:N], [(wv[:, kk, 0, cc * 128:(cc + 1) * 128], hn[:, kk, 0:N]) for kk in range(KC)],
                     reads=(wk,) + HN_ALL, writes=(pgk,))
                k.mm(pu[:, 0:N], [(wv[:, kk, 1, cc * 128:(cc + 1) * 128], hn[:, kk, 0:N]) for kk in range(KC)],
                     reads=(wk,) + HN_ALL, writes=(puk,))
                sgk = ("sg", c % 2)
                k.op(k.act, lambda: nc.scalar.activation(out=self.sg[:, c % 2, 0:N], in_=pg[:, 0:N], func=AF.Silu),
                     reads=(pgk,), writes=(sgk,))
                k.op(k.dve, lambda: nc.vector.tensor_tensor(out=aT[:, c, 0:N], in0=pu[:, 0:N],
                                                            in1=self.sg[:, c % 2, 0:N], op=ALU.mult),
                     reads=(puk, sgk), writes=(("aT", c),))
        for dch in range(KC):
            src = w_out[layer].rearrange("(c p) d -> p c d", p=128)[:, :, dch * 128:(dch + 1) * 128]
            wv, wk = self.wload(src, (FC, 128))
            ps, pk = self.psum()
            k.mm(ps[:, 0:N], [(wv[:, c, :], aT[:, c, 0:N]) for c in range(FC)],
                 reads=(wk,) + tuple(("aT", c) for c in range(FC)), writes=(pk,))
            k.op(k.dve, lambda: nc.vector.scalar_tensor_tensor(
                out=self.hT[:, dch, 0:N], in0=ps[:, 0:N], scalar=0.5, in1=self.hT[:, dch, 0:N],
                op0=ALU.mult, op1=ALU.add),
                reads=(pk, ("hT", dch)), writes=(("hT", dch),))

    def conv_mixer(self, N, layer, nseq, T):
        k, nc = self.k, self.nc
        self.rmsnorm_T(N, 4 + layer)
        hn, uext = self.hn, self.uext
        W = T + 2
        ue4 = uext[:, :, 0:nseq * W].rearrange("p c (s w) -> p c s w", s=nseq)
        vT = self.aT
        w_in = self.w["conv_w_in"][layer].rearrange("(c p) (three f) -> p c three f", p=128, three=3)
        for m in range(KC):
            wv, wk = self.wload([w_in[:, :, j, m * 128:(m + 1) * 128] for j in range(3)], (KC, 3, 128))
            pb, pbk = self.psum()
            pc, pck = self.psum()
            px, pxk = self.psum()
            for j, (pp, ppk) in enumerate(((pb, pbk), (pc, pck), (px, pxk))):
                k.mm(pp[:, 0:N], [(wv[:, kk, j, :], hn[:, kk, 0:N]) for kk in range(KC)],
                     reads=(wk,) + HN_ALL, writes=(ppk,))
            k.op(k.act, lambda: nc.scalar.copy(out=self.tmpa[:, 0:N], in_=pc[:, 0:N]),
                 reads=(pck,), writes=("tmpa",))
            k.op(k.dve, lambda: nc.vector.tensor_tensor(
                out=ue4[:, m, :, 2:W], in0=px[:, 0:N].rearrange("p (s t) -> p s t", s=nseq),
                in1=self.tmpa[:, 0:N].rearrange("p (s t) -> p s t", s=nseq), op=ALU.mult),
                reads=(pxk, "tmpa"), writes=("uext",))
            tb = self.tmpb[:, 0:N].rearrange("p (s t) -> p s t", s=nseq)
            k.op(k.dve, lambda: nc.vector.tensor_scalar(tb, ue4[:, m, :, 0:T], self.cw[:, layer, 0, m:m + 1], None,
                                                        ALU.mult),
                 reads=("uext", "cw"), writes=("tmpb",))
            for j in (1, 2):
                k.op(k.dve, lambda j=j: nc.vector.scalar_tensor_tensor(
                    out=tb, in0=ue4[:, m, :, j:j + T], scalar=self.cw[:, layer, j, m:m + 1], in1=tb,
                    op0=ALU.mult, op1=ALU.add),
                    reads=("uext", "cw", "tmpb"), writes=("tmpb",))
            k.op(k.dve, lambda: nc.vector.tensor_tensor(out=vT[:, m, 0:N], in0=pb[:, 0:N], in1=self.tmpb[:, 0:N],
                                                        op=ALU.mult),
                 reads=(pbk, "tmpb"), writes=(("aT", m),))
        w_out = self.w["conv_w_out"][layer].rearrange("(c p) d -> p c d", p=128)
        for half in range(2):
            wv, wk = self.wload(w_out[:, :, half * 512:(half + 1) * 512], (KC, 512))
            for dd in range(4):
                dch = half * 4 + dd
                ps, pk = self.psum()
                k.mm(ps[:, 0:N], [(wv[:, kk, dd * 128:(dd + 1) * 128], vT[:, kk, 0:N]) for kk in range(KC)],
                     reads=(wk,) + tuple(("aT", c) for c in range(KC)), writes=(pk,))
                k.op(k.dve, lambda: nc.vector.tensor_tensor(out=self.hT[:, dch, 0:N], in0=ps[:, 0:N],
                                                            in1=self.hT[:, dch, 0:N], op=ALU.add),
                     reads=(pk, ("hT", dch)), writes=(("hT", dch),))
        return ue4

    def barrier(self, keys):
        k, nc = self.k, self.nc
        if self.cfg.get("stage", 9) <= -4:
            return
        k.op(k.dve, lambda: nc.vector.memset(self.dummy[:, :], 0.0), reads=(), writes=tuple(keys))

    ALIAS_KEYS = tuple(("aT", c) for c in range(FC)) + ("uext", "rowdst", "rawT", "rawT2", "qcb", "qrb")

    def bc_mid(self, ap2, n):
        return bass.AP(ap2.tensor, ap2.offset, [list(ap2.ap[0]), [0, n], list(ap2.ap[1])])

    def bc_last(self, ap2, n):
        return bass.AP(ap2.tensor, ap2.offset, [list(ap2.ap[0]), list(ap2.ap[1]), [0, n]])

    def headnorm_rope(self, src, pk, M, nh, gain_ap, scale, rope_ap, work, wkey, mid=None):
        k, nc = self.k, self.nc
        W = nh * HD
        sq, ss, xn, rt = self.sq8, self.ss16, self.xn8, self.rt8
        k.op(k.act, lambda: nc.scalar.activation(out=sq[0:M, 0:W], in_=src, func=AF.Square),
             reads=(pk,), writes=("sq8",))
        k.op(k.dve, lambda: nc.vector.tensor_reduce(out=ss[0:M, 0:nh], in_=sq[0:M, 0:W].rearrange("p (g d) -> p g d", g=nh),
                                                    axis=AX.X, op=ALU.add),
             reads=("sq8",), writes=("ss16",))
        self.rsqrt(ss[0:M, 8:8 + nh], ss[0:M, 0:nh], float(HD * EPS), ("ss16",), ("ss16",))
        rb = self.bc_last(ss[0:M, 8:8 + nh], HD)
        k.op(k.dve, lambda: nc.vector.tensor_tensor(out=xn[0:M, 0:W].rearrange("p (g d) -> p g d", g=nh),
                                                    in0=src.rearrange("p (g d) -> p g d", g=nh), in1=rb, op=ALU.mult),
             reads=("ss16", pk), writes=("xn8",))
        knb = self.bc_mid(gain_ap, nh)
        w3 = work.rearrange("p (g d) -> p g d", g=nh)
        k.op(k.dve, lambda: nc.vector.scalar_tensor_tensor(
            out=w3, in0=xn[0:M, 0:W].rearrange("p (g d) -> p g d", g=nh),
            scalar=float(scale), in1=knb, op0=ALU.mult, op1=ALU.mult),
            reads=("xn8", "kn", "qnrm"), writes=(wkey,))
        if mid is not None:
            mid()
        x1, x2 = w3[:, :, 0:8], w3[:, :, 8:16]
        cosb = self.bc_mid(rope_ap[:, 0:8], nh)
        sinb = self.bc_mid(rope_ap[:, 8:16], nh)
        T_ = rt[0:M, :, 0:nh, :]
        k.op(k.dve, lambda: nc.vector.tensor_tensor(out=T_[:, 0], in0=x1, in1=cosb, op=ALU.mult),
             reads=(wkey, "ropep", "ropes"), writes=("rt8",))
        k.op(k.dve, lambda: nc.vector.tensor_tensor(out=T_[:, 1], in0=x2, in1=sinb, op=ALU.mult),
             reads=(wkey,), writes=("rt8",))
        k.op(k.dve, lambda: nc.vector.tensor_tensor(out=T_[:, 2], in0=x2, in1=cosb, op=ALU.mult),
             reads=(wkey,), writes=("rt8",))
        k.op(k.dve, lambda: nc.vector.tensor_tensor(out=T_[:, 3], in0=x1, in1=sinb, op=ALU.mult),
             reads=(wkey,), writes=("rt8",))
        k.op(k.dve, lambda: nc.vector.tensor_tensor(out=x1, in0=T_[:, 0], in1=T_[:, 1], op=ALU.subtract),
             reads=("rt8",), writes=(wkey,))
        k.op(k.dve, lambda: nc.vector.tensor_tensor(out=x2, in0=T_[:, 2], in1=T_[:, 3], op=ALU.add),
             reads=("rt8",), writes=(wkey,))

    def psb(self, ps):
        return ps[:, :].bitcast(BF16)

    def kv_rows(self, N, tile_idx, sample):
        k, nc = self.k, self.nc
        self.rmsnorm_T(N, 12)
        self.barrier(self.ALIAS_KEYS)
        hn = self.hn
        wkv = self.w["w_kv"].rearrange("(c p) f -> p c f", p=128)
        nsub = max(1, N // 128)
        M = min(128, N)
        for blk in range(3):
            wv, wk = self.wload(wkv[:, :, blk * 512:(blk + 1) * 512], (KC, 512))
            for sub in range(nsub):
                ch = tile_idx * nsub + sub
                ps, pk = self.psum()
                k.mm(ps[0:M, :], [(hn[:, kk, sub * 128:sub * 128 + M], wv[:, kk, :]) for kk in range(KC)],
                     reads=(wk,) + HN_ALL, writes=(pk,))
                rope_ap = self.ropes[0:M, :] if sample else self.ropep[0:M, ch, :]
                if blk == 0:
                    k.op(k.act, lambda: nc.scalar.copy(out=self.kvrow[0:M, sub, 0:512], in_=ps[0:M, :]),
                         reads=(pk,), writes=("rowdst",))
                    if not sample:
                        k.op(k.dve, lambda: nc.vector.tensor_copy(out=self.rowb[0:M, sub, :], in_=self.kvrow[0:M, sub, 0:512]),
                             reads=("rowdst",), writes=("rowb",))
                    continue
                dstrow = self.kvrow[0:M, sub, 512:768] if blk == 1 else self.winrow[0:M, sub, 0:256]
                vdst = self.kvrow[0:M, sub, 768:1024] if blk == 1 else self.winrow[0:M, sub, 256:512]
                k.op(k.act, lambda: nc.scalar.copy(out=vdst, in_=ps[0:M, 256:512]), reads=(pk,), writes=("rowdst",))
                self.headnorm_rope(ps[0:M, 0:256], pk, M, 4, self.kn[0:M, blk, :], 8.0, rope_ap, dstrow, "rowdst")
                if sample or self.cfg.get("stage", 9) < 1:
                    continue
                kb = self.kb2[:, blk - 1, :]
                k.op(k.dve, lambda: nc.vector.tensor_copy(out=kb, in_=dstrow), reads=("rowdst",), writes=(("kb2", blk),))
                if blk == 1:
                    Vt, vkey, vi = self.V1s, "V1s", ch
                else:
                    Vt, vkey, vi = self.V1w, "V1w", ch % 8
                k.op(k.act, lambda: nc.scalar.copy(out=Vt[:, vi, :, 0:HD], in_=ps[:, 256:512].rearrange("p (g d) -> p g d", g=4)),
                     reads=(pk,), writes=(vkey,))
                fl = self.flags[:, ch:ch + 1]
                k.op(k.dve, lambda: nc.vector.tensor_copy(out=Vt[:, vi, :, HD:HD + 1], in_=self.bc_mid(fl, 4)),
                     reads=("flags",), writes=(vkey,))
                pt, ptk = self.psum_b()
                ptb = pt
                k.tr([(ptb[0:HD, g * 128:(g + 1) * 128], kb[:, g * HD:(g + 1) * HD]) for g in range(4)], self.identb[:, :],
                     reads=(("kb2", blk), "identb"), writes=(ptk,))
                if blk == 1:
                    dstk, kkey = self.Kaug[0:HD, :, ch * 128:(ch + 1) * 128], "Kaug"
                else:
                    dstk, kkey = self.kwT[0:HD, :, (ch % 8) * 128:(ch % 8 + 1) * 128], "kwT"
                k.op(k.act, lambda: nc.scalar.copy(out=dstk, in_=ptb[0:HD, 0:512].rearrange("p (g n) -> p g n", g=4)),
                     reads=(ptk,), writes=(kkey,))

    def compress(self, t, kc_dst=None, vc_dst=None, kkey="kcT", vkey="vcT", alt=False, dbl=False):
        k, nc = self.k, self.nc
        NT, NCB = self.NT, self.NCB
        nsub = NT // 128
        nb = NT // 32
        c0 = (t or 0) * nb
        if kc_dst is None:
            kc_dst = self.kcT[0:HD, :, c0:c0 + nb]
            vc_dst = self.vcT[0:HD, :, c0:c0 + nb]
        rawT = self.rawTB if alt else self.rawT
        rawk = "rawTB" if alt else "rawT"
        gl, hidT = self.gl, self.hidT
        if dbl:
            nsub, nb = 2 * nsub, 2 * nb
            rawT, rawk, gl, hidT = self.rawT2, "rawT2", self.gl2, self.hidT2
        for sub in range(nsub):
            pt, ptk = self.psum_b()
            ptb = pt
            useB = alt or (dbl and sub >= 4)
            rowb = self.rowbB if useB else self.rowb
            rsub = sub % 4
            k.tr([(ptb[0:HD, j * 128:(j + 1) * 128], rowb[:, rsub, j * HD:(j + 1) * HD]) for j in range(8)],
                 self.identb[:, :], reads=((("rowbB", rsub),) if useB else ("rowb", ("rowb", rsub))) + ("identb",), writes=(ptk,))
            for e in range(2):
                k.op(k.dve, lambda e=e: nc.vector.tensor_tensor(
                    out=rawT[0:HD, e * 4:(e + 1) * 4, sub * 128:(sub + 1) * 128],
                    in0=ptb[0:HD, e * 512:(e + 1) * 512].rearrange("p (g n) -> p g n", g=4),
                    in1=self.bc_mid(self.peT[0:HD, e, :], 4), op=ALU.add),
                    reads=(ptk, "peT"), writes=(rawk,))
        hps, hpk = self.psum()
        w1 = self.w["cmp_w1"]
        for e in range(2):
            wvs = []
            for lb in range(2):
                wv, wk = self.wload(w1[e, lb * 16:(lb + 1) * 16, :, :].rearrange("l d f -> d l f"), (16, 256), nparts=HD)
                wvs.append((wv, wk))
            for g in range(4):
                rv = rawT[0:HD, e * 4 + g, :].rearrange("p (c l) -> p c l", l=32)
                for fc in range(2):
                    col0 = ((e * 4 + g) * 2 + fc) * nb
                    k.mm(hps[:, col0:col0 + nb],
                         [(wvs[l // 16][0][0:HD, l % 16, fc * 128:(fc + 1) * 128], rv[:, :, l]) for l in range(32)],
                         reads=(wvs[0][1], wvs[1][1], rawk), writes=(hpk,))
        ncol = 16 * nb
        x_, a_, b_ = gl[:, 0, 0:ncol], gl[:, 1, 0:ncol], gl[:, 2, 0:ncol]
        k.op(k.act, lambda: nc.scalar.copy(out=x_, in_=hps[:, 0:ncol]), reads=(hpk,), writes=("gl0",))
        k.op(k.dve, lambda: nc.vector.tensor_tensor(out=a_, in0=x_, in1=x_, op=ALU.mult), reads=("gl0",), writes=("gl1",))
        k.op(k.dve, lambda: nc.vector.tensor_scalar(a_, a_, 0.044715, 1.0, ALU.mult, ALU.add), reads=("gl1",), writes=("gl1",))
        k.op(k.dve, lambda: nc.vector.tensor_tensor(out=a_, in0=a_, in1=x_, op=ALU.mult), reads=("gl0", "gl1"), writes=("gl1",))
        k.op(k.act, lambda: nc.scalar.activation(out=b_, in_=a_, func=AF.Sigmoid, scale=1.5957691216057308),
             reads=("gl1",), writes=("gl2",))
        k.op(k.dve, lambda: nc.vector.tensor_tensor(out=hidT[:, 0:ncol], in0=x_, in1=b_, op=ALU.mult),
             reads=("gl0", "gl2"), writes=("hidT",))
        po, pok = self.psum()
        for e in range(2):
            for g in range(4):
                cols = [((e * 4 + g) * 2 + fc) * nb for fc in range(2)]
                k.mm(po[0:HD, (e * 4 + g) * nb:(e * 4 + g + 1) * nb],
                     [(self.w2sb[:, e, fc, :], hidT[:, cols[fc]:cols[fc] + nb]) for fc in range(2)],
                     reads=("w2sb", "hidT"), writes=(pok,))
        nk = 4 * nb
        sqk = hidT[0:HD, 0:nk]
        k.op(k.act, lambda: nc.scalar.activation(out=sqk, in_=po[0:HD, 0:nk], func=AF.Square), reads=(pok,), writes=("hidT",))
        pss, pssk = self.psum()
        k.mm(pss[0:HD, 0:nk], [(self.ones[0:HD, 0:HD], sqk)], reads=("ones", "hidT"), writes=(pssk,))
        self.rsqrt(gl[0:HD, 0, 0:nk], pss[0:HD, 0:nk], float(HD * EPS), (pssk,), ("gl0",))
        k.op(k.dve, lambda: nc.vector.tensor_tensor(out=gl[0:HD, 1, 0:nk], in0=po[0:HD, 0:nk], in1=gl[0:HD, 0, 0:nk], op=ALU.mult),
             reads=(pok, "gl0"), writes=("gl1",))
        k.op(k.dve, lambda: nc.vector.tensor_scalar(kc_dst, gl[0:HD, 1, 0:nk].rearrange("p (g c) -> p g c", g=4),
                                                    self.kn0[0:HD, 0:1], 8.0, ALU.mult, ALU.mult),
             reads=("gl1", "kn0"), writes=(kkey,))
        k.op(k.act, lambda: nc.scalar.copy(out=vc_dst, in_=po[0:HD, nk:2 * nk].rearrange("p (g c) -> p g c", g=4)),
             reads=(pok,), writes=(vkey,))
        if t is None:
            return
        pv, pvk = self.psum_b()
        pvb = pv
        k.tr([(pvb[0:NCB, g * HD:(g + 1) * HD], self.vcT[0:HD, g, 0:NCB]) for g in range(4)], self.identb[0:HD, 0:HD],
             reads=("vcT", "identb"), writes=(pvk,))
        k.op(k.act, lambda: nc.scalar.copy(out=self.V1c[0:NCB, :, 0:HD], in_=pvb[0:NCB, 0:256].rearrange("p (g d) -> p g d", g=4)),
             reads=(pvk,), writes=("V1c",))

    def attn_group(self, t, sub, g, diag, lb):
        k, nc = self.k, self.nc
        NCB, NBLK = self.NCB, self.NBLK
        QcT, Qaug = self.QcT, self.Qaug
        pq, pqk = self.psum_b()
        pqb = pq
        k.tr([(pqb[0:HD, h * 128:(h + 1) * 128], self.qcb[:, sub, 4 * g + h, :]) for h in range(4)] +
             [(pqb[0:HD, 512 + h * 128:512 + (h + 1) * 128], self.qrb[:, sub, 4 * g + h, :]) for h in range(4)],
             self.identb[:, :], reads=("qcb", "qrb", "identb"), writes=(pqk,))
        k.op(k.act, lambda: nc.scalar.copy(out=QcT[0:HD, :], in_=pqb[0:HD, 0:512]), reads=(pqk,), writes=("QcT",))
        k.op(k.act, lambda: nc.scalar.copy(out=Qaug[0:HD, :], in_=pqb[0:HD, 512:1024]), reads=(pqk,), writes=("Qaug_q",))
        pc, pck = self.psum()
        for h in range(4):
            k.mm(pc[:, h * NCB:(h + 1) * NCB], [(QcT[0:HD, h * 128:(h + 1) * 128], self.kcT[0:HD, g, 0:NCB])],
                 reads=("QcT", "kcT"), writes=(pck,))
        ec, ep, sm, imp = self.ec, self.ep, self.sm, self.imp
        k.op(k.act, lambda: nc.scalar.activation(out=ec[:, :], in_=pc[:, 0:4 * NCB], func=AF.Exp), reads=(pck,), writes=("ec",))
        k.op(k.dve, lambda: nc.vector.tensor_tensor(out=ec[:, :].rearrange("p (h c) -> p h c", h=4),
                                                    in0=ec[:, :].rearrange("p (h c) -> p h c", h=4),
                                                    in1=self.bc_mid(self.cmq[:, sub, :], 4), op=ALU.mult),
             reads=("ec", "cmq"), writes=("ec",))
        k.op(k.dve, lambda: nc.vector.tensor_reduce(out=ep[:, :, :].rearrange("p h b -> p (h b)"),
                                                    in_=ec[:, :].rearrange("p (x two) -> p x two", two=2), axis=AX.X, op=ALU.add),
             reads=("ec",), writes=("ep",))
        k.op(k.dve, lambda: nc.vector.tensor_reduce(out=sm[:, 0:4], in_=ep[:, :, :], axis=AX.X, op=ALU.add),
             reads=("ep",), writes=("sm",))
        k.op(k.dve, lambda: nc.vector.tensor_scalar(sm[:, 0:4], sm[:, 0:4], 1e-30, None, ALU.max), reads=("sm",), writes=("sm",))
        k.op(k.dve, lambda: nc.vector.reciprocal(out=sm[:, 4:8], in_=sm[:, 0:4]), reads=("sm",), writes=("sm",))
        k.op(k.dve, lambda: nc.vector.tensor_scalar(imp[:, 0, :], ep[:, 0, :], sm[:, 4:5], None, ALU.mult),
             reads=("ep", "sm"), writes=("imp",))
        for h in range(1, 4):
            k.op(k.dve, lambda h=h: nc.vector.scalar_tensor_tensor(out=imp[:, 0, :], in0=ep[:, h, :], scalar=sm[:, 4 + h:5 + h],
                                                                   in1=imp[:, 0, :], op0=ALU.mult, op1=ALU.add),
                 reads=("ep", "sm", "imp"), writes=("imp",))
        k.op(k.dve, lambda: nc.vector.tensor_tensor(out=imp[:, 1, :], in0=imp[:, 0, :], in1=self.mulc[:, sub, :], op=ALU.mult),
             reads=("imp", "mulc"), writes=("imp",))
        k.op(k.dve, lambda: nc.vector.tensor_tensor(out=imp[:, 1, :], in0=imp[:, 1, :], in1=self.addc[:, sub, :], op=ALU.add),
             reads=("imp", "addc"), writes=("imp",))
        m8 = self.m8
        k.op(k.dve, lambda: nc.vector.max(out=m8[:, 0:8], in_=imp[:, 1, :]), reads=("imp",), writes=("m8",))
        k.op(k.dve, lambda: nc.vector.match_replace(out=imp[:, 2, :], in_to_replace=m8[:, 0:8], in_values=imp[:, 1, :], imm_value=-3e38),
             reads=("imp", "m8"), writes=("imp",))
        k.op(k.dve, lambda: nc.vector.max(out=m8[:, 8:16], in_=imp[:, 2, :]), reads=("imp",), writes=("m8",))
        k.op(k.dve, lambda: nc.vector.tensor_scalar(self.selpad[:, 64:64 + NBLK], imp[:, 1, :], m8[:, 15:16], 30000.0, ALU.is_ge, ALU.mult),
             reads=("imp", "m8"), writes=("selpad",))
        pt, ptk = self.psum_b()
        ptb = pt
        k.tr([(ptb[:, 0:128], self.selpad[:, :])], self.identb[:, :], reads=("selpad", "identb"), writes=(ptk,))
        src_b = ptb[64:128, 0:128]
        k.op(k.dve, lambda: nc.vector.tensor_scalar(Qaug[64:128, :].rearrange("p (h q) -> p h q", h=4), self.bc_mid(src_b, 4),
                                                    -30000.0, None, ALU.add),
             reads=(ptk,), writes=("Qaug_b",))
        self.pcount = getattr(self, "pcount", 0)

        def pbuf():
            i = self.pcount % 4
            self.pcount += 1
            return self.Pb[:, i, :], ("Pb", i)

        tri_lo = self.bc_mid(self.tri[:, 0, :], 4)
        tri_up = self.bc_mid(self.tri[:, 1, :], 4)
        Os, Osk = self.ps[6], ("ps", 6)
        Ow, Owk = self.ps[7], ("ps", 7)
        wj = [j for j in range(diag - 4, diag + 1) if j >= 0]
        jobs = [("s", j) for j in range(diag + 1)] + [("w", j) for j in wj] + [("c", 0)]
        oc_box = []

        def qk(job):
            kind, j = job
            ps, pk = self.psum()
            P, Pk = pbuf()
            if kind == "s":
                k.mm(ps[:, :], [(self.Kaug[:, g, j * 128:(j + 1) * 128], Qaug[:, :])], reads=("Kaug", "Qaug_q", "Qaug_b"), writes=(pk,))
                k.op(k.act, lambda: nc.scalar.activation(out=P, in_=ps[:, :], func=AF.Exp), reads=(pk,), writes=(Pk,))
                m_ = tri_lo if j == diag else None
            elif kind == "w":
                r = j % 8
                k.mm(ps[:, :], [(self.kwT[0:HD, g, r * 128:(r + 1) * 128], Qaug[0:HD, :])], reads=("kwT", "Qaug_q"), writes=(pk,))
                k.op(k.act, lambda: nc.scalar.activation(out=P, in_=ps[:, :], func=AF.Exp), reads=(pk,), writes=(Pk,))
                m_ = tri_lo if j == diag else (tri_up if j == diag - 4 else None)
            else:
                k.mm(ps[0:NCB, :], [(self.kcT[0:HD, g, 0:NCB], QcT[0:HD, :])], reads=("kcT", "QcT"), writes=(pk,))
                k.op(k.act, lambda: nc.scalar.activation(out=P[0:NCB, :], in_=ps[0:NCB, :], func=AF.Exp), reads=(pk,), writes=(Pk,))
                k.op(k.dve, lambda: nc.vector.tensor_tensor(out=P[0:NCB, :].rearrange("p (h q) -> p h q", h=4),
                                                            in0=P[0:NCB, :].rearrange("p (h q) -> p h q", h=4),
                                                            in1=self.bc_mid(self.cmt[0:NCB, sub, :], 4), op=ALU.mult),
                     reads=(Pk, "cmt"), writes=(Pk,))
                m_ = None
            if m_ is not None:
                k.op(k.dve, lambda: nc.vector.tensor_tensor(out=P.rearrange("p (h q) -> p h q", h=4),
                                                            in0=P.rearrange("p (h q) -> p h q", h=4), in1=m_, op=ALU.mult),
                     reads=(Pk, "tri"), writes=(Pk,))
            return P, Pk

        def pv(job, P, Pk):
            kind, j = job
            if kind == "s":
                k.mm(Os[0:HD + 2, :], [(self.V1s[:, j, g, :], P)], reads=("V1s", Pk), writes=(Osk,), start=(j == 0), stop=(j == diag))
            elif kind == "w":
                k.mm(Ow[0:HD + 2, :], [(self.V1w[:, j % 8, g, :], P)], reads=("V1w", Pk), writes=(Owk,), start=(j == wj[0]), stop=(j == wj[-1]))
            else:
                Oc_, Ock_ = self.psum()
                oc_box.append((Oc_, Ock_))
                k.mm(Oc_[0:HD + 2, :], [(self.V1c[0:NCB, g, :], P[0:NCB, :])], reads=("V1c", Pk), writes=(Ock_,))

        pend = []
        for job in jobs:
            pend.append((job,) + qk(job))
            if len(pend) > 2:
                pv(*pend.pop(0))
        while pend:
            pv(*pend.pop(0))
        Oc, Ock = oc_box[0]
        Oev = self.Oev
        for b, (O_, Ok_) in enumerate(((Oc, Ock), (Os, Osk), (Ow, Owk))):
            k.op(k.act, lambda: nc.scalar.copy(out=Oev[0:HD + 1, b, :], in_=O_[0:HD + 1, :]), reads=(Ok_,), writes=(("Oev", b),))
        self.combine_g(128, lambda b, h: Oev[0:HD + 1, b, h * 128:(h + 1) * 128], tuple(("Oev", b) for b in range(3)),
                       self.gates[:, sub, g * 12:(g + 1) * 12], self.otok[:, g * 256:(g + 1) * 256])

    def combine_g(self, M, src, srckeys, gates_ap, out_ap):
        k, nc = self.k, self.nc
        sm, oacc = self.sm, self.oacc
        gt = gates_ap.rearrange("p (h b) -> p h b", b=3)
        for b in range(3):
            po, pok = self.psum()
            k.tr([(po[0:M, h * 65:(h + 1) * 65], src(b, h)) for h in range(4)],
                 self.identf[0:HD + 1, 0:HD + 1], reads=tuple(srckeys) + ("identf",), writes=(pok,))
            po3 = po[0:M, 0:260].rearrange("p (h d) -> p h d", h=4)
            k.op(k.dve, lambda: nc.vector.tensor_scalar(sm[0:M, 8:12], po3[:, :, 64], 1e-30, None, ALU.max), reads=(pok,), writes=("sm2",))
            k.op(k.dve, lambda: nc.vector.reciprocal(out=sm[0:M, 8:12], in_=sm[0:M, 8:12]), reads=("sm2",), writes=("sm2",))
            k.op(k.dve, lambda: nc.vector.tensor_tensor(out=sm[0:M, 12:16], in0=sm[0:M, 8:12], in1=gt[:, :, b], op=ALU.mult),
                 reads=("sm2", "gates"), writes=("sm2",))
            fb = self.bc_last(sm[0:M, 12:16], HD)
            if b == 0:
                k.op(k.dve, lambda: nc.vector.tensor_tensor(out=oacc[0:M, 0, :].rearrange("p (h d) -> p h d", h=4), in0=po3[:, :, 0:HD],
                                                            in1=fb, op=ALU.mult), reads=(pok, "sm2"), writes=("oacc0",))
            else:
                k.op(k.dve, lambda: nc.vector.tensor_tensor(out=oacc[0:M, 1, :].rearrange("p (h d) -> p h d", h=4), in0=po3[:, :, 0:HD],
                                                            in1=fb, op=ALU.mult), reads=(pok, "sm2"), writes=("oacc1",))
                dst = oacc[0:M, 0, :] if b == 1 else out_ap
                k.op(k.dve, lambda: nc.vector.tensor_tensor(out=dst, in0=oacc[0:M, 0, :], in1=oacc[0:M, 1, :], op=ALU.add),
                     reads=("oacc0", "oacc1"), writes=(("oacc0",) if b == 1 else ("otok",)))

    def nsa(self, N, t, lb):
        k, nc = self.k, self.nc
        layer = 2 + lb
        nsub = N // 128
        self.rmsnorm_T(N, 4 + layer)
        hn = self.hn
        wqg = self.w["nsa_w_qg"][lb].rearrange("(c p) f -> p c f", p=128)
        for blk in range(3):
            ncol = 512 if blk < 2 else 48
            wv, wk = self.wload(wqg[:, :, blk * 512:blk * 512 + ncol], (KC, ncol))
            for sub in range(nsub):
                ch = t * nsub + sub
                ps, pk = self.psum()
                k.mm(ps[:, 0:ncol], [(hn[:, kk, sub * 128:(sub + 1) * 128], wv[:, kk, :]) for kk in range(KC)],
                     reads=(wk,) + HN_ALL, writes=(pk,))
                if blk < 2:
                    h0 = blk * 8

                    def mid(sub=sub, h0=h0):
                        k.op(k.act, lambda: nc.scalar.copy(out=self.qcb[:, sub, h0:h0 + 8, :],
                                                           in_=self.qf[:, :].rearrange("p (h d) -> p h d", h=8)),
                             reads=("qf",), writes=("qcb",))
                    self.headnorm_rope(ps[:, 0:512], pk, 128, 8, self.qnrm[:, lb, :], 1.0, self.ropep[:, ch, :], self.qf[:, :], "qf", mid=mid)
                    k.op(k.act, lambda: nc.scalar.copy(out=self.qrb[:, sub, h0:h0 + 8, :],
                                                       in_=self.qf[:, :].rearrange("p (h d) -> p h d", h=8)),
                         reads=("qf",), writes=("qrb",))
                else:
                    k.op(k.act, lambda: nc.scalar.activation(out=self.gates[:, sub, :], in_=ps[:, 0:48], func=AF.Sigmoid),
                         reads=(pk,), writes=("gates",))
        oT = self.aT
        self.gen_list = [0, 1, 2, 3]
        for sub in range(nsub):
            diag = t * nsub + sub
            for g in range(4):
                if self.cfg.get("stage", 9) >= 4:
                    self.attn_group(t, sub, g, diag, lb)
            pt, ptk = self.psum_b()
            ptb = pt
            k.tr([(ptb[:, kk * 128:(kk + 1) * 128], self.otok[:, kk * 128:(kk + 1) * 128]) for kk in range(KC)], self.identb[:, :],
                 reads=("otok", "identb"), writes=(ptk,))
            k.op(k.act, lambda: nc.scalar.copy(out=oT[:, 0:KC, sub * 128:(sub + 1) * 128],
                                               in_=ptb[:, 0:1024].rearrange("p (c n) -> p c n", c=KC)),
                 reads=(ptk,), writes=tuple(("aT", c) for c in range(KC)))
        self.gen_list = [0, 1, 2, 3, 6, 7]
        w_o = self.w["nsa_w_o"][lb].rearrange("(c p) d -> p c d", p=128)
        for half in range(2):
            wv, wk = self.wload(w_o[:, :, half * 512:(half + 1) * 512], (KC, 512))
            for dd in range(4):
                dch = half * 4 + dd
                ps, pk = self.psum()
                k.mm(ps[:, 0:N], [(wv[:, kk, dd * 128:(dd + 1) * 128], oT[:, kk, 0:N]) for kk in range(KC)],
                     reads=(wk,) + tuple(("aT", c) for c in range(KC)), writes=(pk,))
                k.op(k.dve, lambda: nc.vector.tensor_tensor(out=self.hT[:, dch, 0:N], in0=ps[:, 0:N],
                                                            in1=self.hT[:, dch, 0:N], op=ALU.add),
                     reads=(pk, ("hT", dch)), writes=(("hT", dch),))

    SKEYS = ("kcTs", "vcTs", "V1cs", "kvpage", "KTp", "V1p", "newsel", "newwin", "idx", "ec_s", "ep_s", "imp_s", "biasx",
             "Oev_s0", "Oev_s1", "Oev_s2", "Ps0", "Ps1", "QT_s", "sc_pidx", "sc_wm", "sc_smc", "sc_sac", "sc_bmat", "otok")

    def sample_prep(self):
        k, nc = self.k, self.nc
        NSEQ, NPG, NCBS, TS, CCH = self.NSEQ, self.NPG, self.NCBS, self.TS, self.CCH
        self.barrier(("Kaug", "V1s", "kwT", "V1w", "otok", "cmq", "cmt", "mulc", "addc", "rawTB", "QcT", "Qaug_q", "Qaug_b", "ec", "ep",
                      "gl0", "gl1", "gl2", "hidT", "Pb", ("Pb", 0), ("Pb", 1), ("Pb", 2), ("Pb", 3)) + tuple(("Oev", b) for b in range(3))
                     + tuple(("rowbB", i) for i in range(4)) + self.SKEYS + self.ALIAS_KEYS[-6:])
        n = NSEQ * NPG
        ptb = bass.AP(self.d_pt.tensor, 0, [[0, 128], [1, n]])
        k.dma(k.sp, self.pt_i[:, :], ptb, (), ("idx",), "s0")
        k.dma(k.sp, self.pidx[:, :], self.d_pidx, (), ("sc_pidx",), "s1a")
        k.dma(k.sp, self.wm[:, :, :], self.d_wm, (), ("sc_wm",), "s1b")
        k.dma(k.sp, self.smc[0:8, :], self.d_smc, (), ("sc_smc",), "s1c")
        k.dma(k.sp, self.sac[0:8, :], self.d_sac, (), ("sc_sac",), "s1d")
        k.dma(k.pool, self.bmat[0:8, :], self.d_bmat, (), ("sc_bmat",), "s2")
        k.op(k.dve, lambda: nc.vector.tensor_copy(out=self.idxf[:, :], in_=self.pt_i[:, :]), reads=("idx",), writes=("idxf",))
        k.op(k.dve, lambda: nc.vector.tensor_scalar(self.idxf[:, :], self.idxf[:, :], 256.0, self.pidx[:, 0:1], ALU.mult, ALU.add),
             reads=("idxf", "sc_pidx", "sc_wm", "sc_smc", "sc_sac", "sc_bmat"), writes=("idxf",))
        k.op(k.dve, lambda: nc.vector.tensor_copy(out=self.idx_i[:, 0, :], in_=self.idxf[:, :]), reads=("idxf",), writes=("idx",))
        k.op(k.dve, lambda: nc.vector.tensor_scalar(self.idxf[:, :], self.idxf[:, :], 1.0, None, ALU.add), reads=("idxf",), writes=("idxf",))
        k.op(k.dve, lambda: nc.vector.tensor_copy(out=self.idx_i[:, 1, :], in_=self.idxf[:, :]), reads=("idxf",), writes=("idx",))
        for t_, kn_ in ((self.newsel, tuple(("newsel", i) for i in range(NSEQ))), (self.newwin, tuple(("newwin", i) for i in range(NSEQ))),
                        (self.V1cs, ("V1cs",)), (self.imp_s, ("imp_s",)), (self.biasx, ("biasx",))):
            k.op(k.dve, lambda t_=t_: nc.vector.memset(t_[:], 0.0), writes=kn_)
        k.op(k.dve, lambda: nc.vector.memset(self.V1cs[:, :, :, :, HD:HD + 1], 1.0), writes=("V1cs",))
        k.op(k.dve, lambda: nc.vector.memset(self.V1p[:], 0.0), writes=(("V1p", 0), ("V1p", 1), ("V1p", 2)))
        k.op(k.dve, lambda: nc.vector.memset(self.Ps[:], 0.0), writes=(("Ps", 0), ("Ps", 1), ("Ps", 2)))
        for s_ in range(NSEQ):
            k.dma(k.pool, self.newsel[0:TS, s_, :], self.kvrow[s_ * TS:(s_ + 1) * TS, 0, 512:1024], ("rowdst",), (("newsel", s_),), f"s3a{s_}")
            k.dma(k.pool, self.newwin[0:TS, s_, :], self.winrow[s_ * TS:(s_ + 1) * TS, 0, :], ("rowdst",), (("newwin", s_),), f"s3b{s_}")
        self.barrier(("uext", "rowdst", "rawT", "rawT2"))
        for s_ in range(NSEQ):
            for pr in range(self.PAST // 1024):
                for sub in range(8):
                    j = s_ * NPG + pr * 8 + sub
                    if sub >= 4:
                        k.idma(self.rowbB[:, sub - 4, :], self.d_cache, self.idx_i[:, 0, j:j + 1], ("idx",), (("rowbB", sub - 4),), f"s5{sub - 4}")
                    else:
                        k.idma(self.rowb[:, sub, :], self.d_cache, self.idx_i[:, 0, j:j + 1], ("idx",), (("rowb", sub),), f"s4{sub}")
                self.compress(None, kc_dst=self.kcTs[0:HD, s_, :, pr * 32:(pr + 1) * 32],
                              vc_dst=self.vcTs[0:HD, s_, :, pr * 32:(pr + 1) * 32], kkey="kcTs", vkey="vcTs", dbl=True)
            for cch in range(CCH):
                w = min(128, NCBS - cch * 128)
                pv, pvk = self.psum_b()
                k.tr([(pv[0:w, g * HD:(g + 1) * HD], self.vcTs[0:HD, s_, g, cch * 128:cch * 128 + w]) for g in range(4)],
                     self.identb[0:HD, 0:HD], reads=("vcTs", "identb"), writes=(pvk,))
                k.op(k.act, lambda: nc.scalar.copy(out=self.V1cs[0:w, s_, cch, :, 0:HD],
                                                   in_=pv[0:w, 0:256].rearrange("p (g d) -> p g d", g=4)),
                     reads=(pvk,), writes=("V1cs",))

    def s_chunk(self, kv, kvkey, acc, first, last, QT, flag_ap, mask_ap=None, bias_j=None):
        k, nc = self.k, self.nc
        self.sc_i = getattr(self, "sc_i", 0) + 1
        par = self.sc_i % 3
        pt, ptk = self.psum_b()
        k.tr([(pt[0:HD, g * 128:(g + 1) * 128], kv[:, g * HD:(g + 1) * HD]) for g in range(4)], self.identb[:, :],
             reads=(kvkey, "identb"), writes=(ptk,))
        KT, KTk = self.KTp[0:HD, par, :, :], ("KTp", par)
        k.op(k.act, lambda: nc.scalar.copy(out=KT, in_=pt[0:HD, 0:512].rearrange("p (g n) -> p g n", g=4)), reads=(ptk,), writes=(KTk,))
        V1, V1k = self.V1p[:, par, :, :], ("V1p", par)
        k.op(k.dve, lambda: nc.vector.tensor_copy(out=V1[:, :, 0:HD], in_=kv[:, 256:512].rearrange("p (g d) -> p g d", g=4)),
             reads=(kvkey,), writes=(V1k,))
        k.op(k.dve, lambda: nc.vector.tensor_copy(out=V1[:, :, HD:HD + 1], in_=self.bc_mid(flag_ap, 4)), reads=("sc_pidx", "sc_wm", "sc_smc", "sc_sac", "sc_bmat", "ones32"), writes=(V1k,))
        if bias_j is not None:
            bx = self.biasx[0:8, 0, 2 * bias_j:2 * bias_j + 2]
            bsrc = bass.AP(bx.tensor, bx.offset, [list(bx.ap[0]), [136, 4], [1, 2], [0, 64]])
            k.op(k.dve, lambda: nc.vector.tensor_copy(out=self.biasE[0:8, par, :, :].rearrange("p g (r t) -> p g r t", r=2), in_=bsrc),
                 reads=("biasx",), writes=(("biasE", par),))
        ps, pk = self.psum()
        for g in range(4):
            pairs = [(KT[:, g, :], QT[0:HD, 4 * g:4 * g + 4, :])]
            if bias_j is not None:
                pairs.append((self.biasE[0:8, par, g, :], self.bmat[0:8, :]))
            k.mm(ps[:, g * 32:(g + 1) * 32], pairs, reads=(KTk, "QT_s", ("biasE", par), "sc_bmat"), writes=(pk,))
        P, Pk = self.Ps[:, par, :], ("Ps", par)
        k.op(k.act, lambda: nc.scalar.activation(out=P, in_=ps[:, 0:128], func=AF.Exp), reads=(pk,), writes=(Pk,))
        if mask_ap is not None:
            k.op(k.dve, lambda: nc.vector.tensor_tensor(out=P.rearrange("p (g n) -> p g n", g=4), in0=P.rearrange("p (g n) -> p g n", g=4),
                                                        in1=self.bc_mid(mask_ap, 4), op=ALU.mult), reads=(Pk, "sc_pidx", "sc_wm", "sc_smc", "sc_sac", "sc_bmat"), writes=(Pk,))
        return (acc, first, last, V1, V1k, P, Pk)

    def s_chunk_b(self, ctx):
        k = self.k
        acc, first, last, V1, V1k, P, Pk = ctx
        for g in range(4):
            a_, ak_ = acc[g]
            k.mm(a_[0:HD + 2, 0:32], [(V1[:, g, :], P[:, g * 32:(g + 1) * 32])], reads=(V1k, Pk), writes=(ak_,), start=first, stop=last)

    def nsa_sample(self, lb):
        k, nc = self.k, self.nc
        NSEQ, NPG, NCBS, TS, CCH, NBS = self.NSEQ, self.NPG, self.NCBS, self.TS, self.CCH, self.NBS
        NS = self.NS
        layer = 2 + lb
        self.rmsnorm_T(NS, 4 + layer)
        hn = self.hn
        wqg = self.w["nsa_w_qg"][lb].rearrange("(c p) f -> p c f", p=128)
        for blk in range(3):
            ncol = 512 if blk < 2 else 48
            wv, wk = self.wload(wqg[:, :, blk * 512:blk * 512 + ncol], (KC, ncol))
            for s_ in range(NSEQ):
                ps, pk = self.psum()
                k.mm(ps[0:TS, 0:ncol], [(hn[:, kk, s_ * TS:(s_ + 1) * TS], wv[:, kk, :]) for kk in range(KC)],
                     reads=(wk,) + HN_ALL, writes=(pk,))
                if blk < 2:
                    h0 = blk * 8

                    def mid(s_=s_, h0=h0):
                        k.op(k.act, lambda: nc.scalar.copy(out=self.qcb[0:TS, s_, h0:h0 + 8, :],
                                                           in_=self.qf[0:TS, :].rearrange("p (h d) -> p h d", h=8)),
                             reads=("qf",), writes=("qcb",))
                    self.headnorm_rope(ps[0:TS, 0:512], pk, TS, 8, self.qnrm[0:TS, lb, :], 1.0, self.ropes[0:TS, :], self.qf[0:TS, :], "qf", mid=mid)
                    k.op(k.act, lambda: nc.scalar.copy(out=self.qrb[0:TS, s_, h0:h0 + 8, :],
                                                       in_=self.qf[0:TS, :].rearrange("p (h d) -> p h d", h=8)),
                         reads=("qf",), writes=("qrb",))
                else:
                    k.op(k.act, lambda: nc.scalar.activation(out=self.gates[0:TS, s_, :], in_=ps[0:TS, 0:48], func=AF.Sigmoid),
                         reads=(pk,), writes=("gates",))
        oT = self.aT
        acc = [(self.ps[6], ("ps", 6)), (self.ps[7], ("ps", 7)), (self.ps[2], ("ps", 2)), (self.ps[3], ("ps", 3))]
        self.gen_list = [0, 1]
        ec, ep, imp, sm, m8 = self.ec_s, self.ep_s, self.imp_s, self.sm, self.m8
        nbp = NCBS // 2
        for s_ in range(NSEQ):
            pq, pqk = self.psum_b()
            k.tr([(pq[0:HD, (c * 16 + h) * 8:(c * 16 + h + 1) * 8], (self.qcb if c == 0 else self.qrb)[0:TS, s_, h, :])
                  for c in range(2) for h in range(16)], self.identb[0:TS, 0:TS], reads=("qcb", "qrb", "identb"), writes=(pqk,))
            k.op(k.act, lambda: nc.scalar.copy(out=self.QT_s[0:HD, :, :, :].rearrange("p c h q -> p (c h q)"), in_=pq[0:HD, 0:256]),
                 reads=(pqk,), writes=("QT_s",))
            QTc, QTr = self.QT_s[:, 0, :, :], self.QT_s[:, 1, :, :]
            for g in range(4):
                for hp in range(2):
                    pc, pck = self.psum()
                    for hh in range(2):
                        h = hp * 2 + hh
                        k.mm(pc[0:TS, hh * NCBS:(hh + 1) * NCBS], [(QTc[0:HD, 4 * g + h, :], self.kcTs[0:HD, s_, g, :])],
                             reads=("QT_s", "kcTs"), writes=(pck,))
                    k.op(k.act, lambda: nc.scalar.activation(out=ec[0:TS, hp * 2:hp * 2 + 2, :].rearrange("p h c -> p (h c)"),
                                                             in_=pc[0:TS, 0:2 * NCBS], func=AF.Exp), reads=(pck,), writes=("ec_s",))
                k.op(k.dve, lambda: nc.vector.tensor_reduce(out=ep[0:TS, :, :].rearrange("p h b -> p (h b)"),
                                                            in_=ec[0:TS, :, :].rearrange("p h (x two) -> p (h x) two", two=2),
                                                            axis=AX.X, op=ALU.add), reads=("ec_s",), writes=("ep_s",))
                k.op(k.dve, lambda: nc.vector.tensor_reduce(out=sm[0:TS, 0:4], in_=ep[0:TS, :, :], axis=AX.X, op=ALU.add),
                     reads=("ep_s",), writes=("sm",))
                k.op(k.dve, lambda: nc.vector.tensor_scalar(sm[0:TS, 0:4], sm[0:TS, 0:4], 1e-30, None, ALU.max), reads=("sm",), writes=("sm",))
                k.op(k.dve, lambda: nc.vector.reciprocal(out=sm[0:TS, 4:8], in_=sm[0:TS, 0:4]), reads=("sm",), writes=("sm",))
                k.op(k.dve, lambda: nc.vector.tensor_scalar(imp[0:TS, 0, 0:nbp], ep[0:TS, 0, :], sm[0:TS, 4:5], None, ALU.mult),
                     reads=("ep_s", "sm"), writes=("imp_s",))
                for h in range(1, 4):
                    k.op(k.dve, lambda h=h: nc.vector.scalar_tensor_tensor(out=imp[0:TS, 0, 0:nbp], in0=ep[0:TS, h, :], scalar=sm[0:TS, 4 + h:5 + h],
                                                                           in1=imp[0:TS, 0, 0:nbp], op0=ALU.mult, op1=ALU.add),
                         reads=("ep_s", "sm", "imp_s"), writes=("imp_s",))
                k.op(k.dve, lambda: nc.vector.tensor_tensor(out=imp[0:TS, 1, 0:NBS], in0=imp[0:TS, 0, 0:NBS], in1=self.smc[0:TS, 0:NBS], op=ALU.mult),
                     reads=("imp_s", "sc_pidx", "sc_wm", "sc_smc", "sc_sac", "sc_bmat"), writes=("imp_s",))
                k.op(k.dve, lambda: nc.vector.tensor_tensor(out=imp[0:TS, 1, 0:NBS], in0=imp[0:TS, 1, 0:NBS], in1=self.sac[0:TS, 0:NBS], op=ALU.add),
                     reads=("imp_s", "sc_pidx", "sc_wm", "sc_smc", "sc_sac", "sc_bmat"), writes=("imp_s",))
                k.op(k.dve, lambda: nc.vector.max(out=m8[0:TS, 0:8], in_=imp[0:TS, 1, 0:NBS]), reads=("imp_s",), writes=("m8",))
                k.op(k.dve, lambda: nc.vector.match_replace(out=imp[0:TS, 2, 0:NBS], in_to_replace=m8[0:TS, 0:8], in_values=imp[0:TS, 1, 0:NBS],
                                                            imm_value=-3e38), reads=("imp_s", "m8"), writes=("imp_s",))
                k.op(k.dve, lambda: nc.vector.max(out=m8[0:TS, 8:16], in_=imp[0:TS, 2, 0:NBS]), reads=("imp_s",), writes=("m8",))
                k.op(k.dve, lambda: nc.vector.tensor_scalar(imp[0:TS, 2, 0:NBS], imp[0:TS, 1, 0:NBS], m8[0:TS, 15:16], 30000.0, ALU.is_ge, ALU.mult),
                     reads=("imp_s", "m8"), writes=("imp_s",))
                k.op(k.dve, lambda: nc.vector.tensor_scalar(self.biasx[0:TS, g, 0:NBS], imp[0:TS, 2, 0:NBS], -30000.0, None, ALU.add),
                     reads=("imp_s",), writes=("biasx",))
            Oc, Ock = acc[0]
            for g in range(4):
                for cch in range(CCH):
                    w = min(128, NCBS - cch * 128)
                    ps, pk = self.psum()
                    k.mm(ps[0:w, 0:32], [(self.kcTs[0:HD, s_, g, cch * 128:cch * 128 + w], QTc[0:HD, 4 * g:4 * g + 4, :])],
                         reads=("kcTs", "QT_s"), writes=(pk,))
                    self.sc_i = getattr(self, "sc_i", 0) + 1
                    par = self.sc_i % 3
                    P, Pk = self.Ps[:, par, 0:32], ("Ps", par)
                    k.op(k.act, lambda: nc.scalar.activation(out=P[0:w, :], in_=ps[0:w, 0:32], func=AF.Exp), reads=(pk,), writes=(Pk,))
                    k.mm(Oc[0:HD + 2, g * 32:(g + 1) * 32], [(self.V1cs[0:w, s_, cch, g, :], P[0:w, :])], reads=("V1cs", Pk), writes=(Ock,),
                         start=(cch == 0), stop=(cch == CCH - 1))
            k.op(k.act, lambda: nc.scalar.copy(out=self.Oev_s[0:HD + 1, 0, :], in_=Oc[0:HD + 1, 0:128]), reads=(Ock,), writes=("Oev_s0",))
            pend = []
            for ch in range(5):
                if ch < 4:
                    self.sc_j = getattr(self, "sc_j", 0) + 1
                    pj = self.sc_j % 4
                    kv, kvk = self.kvpage[:, pj, :], ("kvpage", pj)
                    k.dma(k.pool, kv, self.cwin[s_, ch * 128:(ch + 1) * 128, :], (), (kvk,), f"pg{pj}")
                    ctx = self.s_chunk(kv, kvk, acc, ch == 0, False, QTr, self.ones32[:, 0:1], mask_ap=(self.wm[:, 0, :] if ch == 0 else None))
                else:
                    ctx = self.s_chunk(self.newwin[:, s_, :], ("newwin", s_), acc, False, True, QTr, self.wm[:, 2, 0:1], mask_ap=self.wm[:, 1, :])
                pend.append(ctx)
                if len(pend) > 2:
                    self.s_chunk_b(pend.pop(0))
            while pend:
                self.s_chunk_b(pend.pop(0))
            for g in range(4):
                k.op(k.act, lambda g=g: nc.scalar.copy(out=self.Oev_s[0:HD + 1, 2, g * 32:(g + 1) * 32], in_=acc[g][0][0:HD + 1, 0:32]),
                     reads=(acc[g][1],), writes=("Oev_s2",))
            pend = []
            for j in range(NPG + 1):
                if j < NPG:
                    self.sc_j = getattr(self, "sc_j", 0) + 1
                    pj = self.sc_j % 4
                    kv, kvk = self.kvpage[:, pj, :], ("kvpage", pj)
                    k.idma(kv, self.d_cache, self.idx_i[:, 1, s_ * NPG + j:s_ * NPG + j + 1], ("idx",), (kvk,), f"pg{pj}")
                    ctx = self.s_chunk(kv, kvk, acc, j == 0, False, QTr, self.ones32[:, 0:1], bias_j=j)
                else:
                    ctx = self.s_chunk(self.newsel[:, s_, :], ("newsel", s_), acc, False, True, QTr, self.wm[:, 2, 0:1], mask_ap=self.wm[:, 1, :])
                pend.append(ctx)
                if len(pend) > 2:
                    self.s_chunk_b(pend.pop(0))
            while pend:
                self.s_chunk_b(pend.pop(0))
            for g in range(4):
                k.op(k.act, lambda g=g: nc.scalar.copy(out=self.Oev_s[0:HD + 1, 1, g * 32:(g + 1) * 32], in_=acc[g][0][0:HD + 1, 0:32]),
                     reads=(acc[g][1],), writes=("Oev_s1",))
            for g in range(4):
                self.combine_g(TS, lambda b, h, g=g: self.Oev_s[0:HD + 1, b, (4 * g + h) * 8:(4 * g + h + 1) * 8],
                               ("Oev_s0", "Oev_s1", "Oev_s2"), self.gates[0:TS, s_, g * 12:(g + 1) * 12],
                               self.otok_s[0:TS, g * 256:(g + 1) * 256])
            pt, ptk = self.psum_b()
            k.tr([(pt[:, kk * 8:(kk + 1) * 8], self.otok_s[0:TS, kk * 128:(kk + 1) * 128]) for kk in range(KC)], self.identb[0:TS, 0:TS],
                 reads=("otok", "identb"), writes=(ptk,))
            k.op(k.act, lambda: nc.scalar.copy(out=oT[:, 0:KC, s_ * TS:(s_ + 1) * TS], in_=pt[:, 0:KC * 8].rearrange("p (c n) -> p c n", c=KC)),
                 reads=(ptk,), writes=tuple(("aT", c) for c in range(KC)))
        self.gen_list = [0, 1, 2, 3, 6, 7]
        w_o = self.w["nsa_w_o"][lb].rearrange("(c p) d -> p c d", p=128)
        for half in range(2):
            wv, wk = self.wload(w_o[:, :, half * 512:(half + 1) * 512], (KC, 512))
            for dd in range(4):
                dch = half * 4 + dd
                ps, pk = self.psum()
                k.mm(ps[:, 0:NS], [(wv[:, kk, dd * 128:(dd + 1) * 128], oT[:, kk, 0:NS]) for kk in range(KC)],
                     reads=(wk,) + tuple(("aT", c) for c in range(KC)), writes=(pk,))
                k.op(k.dve, lambda: nc.vector.tensor_tensor(out=self.hT[:, dch, 0:NS], in0=ps[:, 0:NS],
                                                            in1=self.hT[:, dch, 0:NS], op=ALU.add),
                     reads=(pk, ("hT", dch)), writes=(("hT", dch),))

    def build(self):
        k, nc = self.k, self.nc
        NT = self.NT
        self.setup()
        for t in range(self.NTILES):
            own = t >= self.OWN0
            to = t - self.OWN0
            k.dma(k.sp, self.hT[:, :, :], self.xT.rearrange("(c p) n -> p c n", p=128)[:, :, t * NT:(t + 1) * NT],
                  (), HT_ALL, "x")
            self.barrier(self.ALIAS_KEYS)
            for layer in range(2):
                self.ffn(NT, layer, "a")
                k.op(k.dve, lambda: nc.vector.tensor_copy(out=self.uext[:, :, 0:2], in_=self.carry[:, layer, :, :]),
                     reads=("carry",), writes=("uext",))
                self.conv_mixer(NT, layer, 1, NT)
                k.op(k.dve, lambda: nc.vector.tensor_copy(out=self.carry[:, layer, :, :], in_=self.uext[:, :, NT:NT + 2]),
                     reads=("uext",), writes=("carry",))
                self.ffn(NT, layer, "b")
            self.kv_rows(NT, t, False)
            if own:
                k.dma(k.sp, self.o_kv[to * NT:(to + 1) * NT, :].rearrange("(s p) f -> p s f", p=128), self.kvrow[:, :, :],
                      ("rowdst",), (), "okv")
                if self.nown - (to + 1) * NT < 512:
                    r0 = 512 - (self.nown - to * NT)
                    k.dma(k.sp, self.o_win[r0:r0 + NT, :].rearrange("(s p) f -> p s f", p=128), self.winrow[:, :, :],
                          ("rowdst",), (), "owin")
            if self.cfg.get("stage", 9) >= 2:
                self.compress(t)
            if own and self.cfg.get("stage", 9) >= -1:
                k.dma(k.sp, self.cmq[:, :, :], self.d_cmq[to], (), ("cmq",), "m0")
                k.dma(k.sp, self.cmt[:, :, :], self.d_cmt[to], (), ("cmt",), "m1")
                k.dma(k.sp, self.mulc[:, :, :], self.d_mulc[to], (), ("mulc",), "m2")
                k.dma(k.sp, self.addc[:, :, :], self.d_addc[to], (), ("addc",), "m3")
                for lb in range(2 if self.cfg.get("stage", 9) >= -1 else 0):
                    self.barrier(self.ALIAS_KEYS)
                    self.ffn(NT, 2 + lb, "a")
                    self.barrier(self.ALIAS_KEYS)
                    if self.cfg.get("stage", 9) >= 3:
                        self.nsa(NT, t, lb)
                    self.ffn(NT, 2 + lb, "b")
                k.dma(k.sp, self.o_yT.rearrange("(c p) n -> p c n", p=128)[:, :, to * NT:(to + 1) * NT], self.hT[:, :, :],
                      HT_ALL, (), "oy")
        k.dma(k.sp, self.o_conv.rearrange("l p c j -> p l c j"), self.carry[:, :, :, :], ("carry",), (), "oconv")
        NS, NSEQ, TS = self.NS, self.NSEQ, self.TS
        k.dma(k.sp, self.hT[:, :, 0:NS], self.xsT.rearrange("(c p) n -> p c n", p=128), (), HT_ALL, "x")
        self.barrier(self.ALIAS_KEYS)
        for layer in range(2):
            self.ffn(NS, layer, "a")
            W = TS + 2
            ue4 = self.uext[:, :, 0:NSEQ * W].rearrange("p c (s w) -> p c s w", s=NSEQ)
            for s_ in range(NSEQ):
                k.dma(k.sp, ue4[:, :, s_, 0:2], self.convst[:, layer, :, s_, :], (), ("uext",), "cst")
            self.conv_mixer(NS, layer, NSEQ, TS)
            for s_ in range(NSEQ):
                k.dma(k.sp, self.o_convs[layer, :, :, s_, :], ue4[:, :, s_, TS:TS + 2], ("uext",), (), "oconvs")
            self.ffn(NS, layer, "b")
        self.kv_rows(NS, 0, True)
        k.dma(k.sp, self.o_kvs[:, :], self.kvrow[0:NS, 0, :], ("rowdst",), (), "okvs")
        k.dma(k.sp, self.o_wins[:, 0:512 - TS, :], self.cwin[:, TS:512, :], (), (), "owins")
        for s_ in range(NSEQ):
            k.dma(k.sp, self.o_wins[s_, 512 - TS:512, :], self.winrow[s_ * TS:(s_ + 1) * TS, 0, :], ("rowdst",), (), "owins")
        if self.cfg.get("stage", 9) >= 5:
            self.sample_prep()
            for lb in range(2):
                self.barrier(self.ALIAS_KEYS)
                self.ffn(NS, 2 + lb, "a")
                self.barrier(self.ALIAS_KEYS)
                self.nsa_sample(lb)
                self.ffn(NS, 2 + lb, "b")
        k.dma(k.sp, self.o_ysT.rearrange("(c p) n -> p c n", p=128), self.hT[:, :, 0:NS], HT_ALL, (), "oys")
        k.finish()
        return nc


FULL_CFG = dict(NT=512, NTILES=8, OWN0=4, NSEQ=4, TS=8, PAST=8192, NPOOL=2560)


def _rope_table(pos):
    half = 8
    inv = ROPE_THETA ** (-np.arange(half, dtype=np.float32) / half)
    ang = pos.astype(np.float32)[:, None] * inv[None, :].astype(np.float32)
    return np.concatenate([np.cos(ang), np.sin(ang)], axis=1).astype(np.float32)


def prepare_inputs(inp, cfg, ncores=8):
    NT, NTILES, OWN0, NSEQ, TS = cfg["NT"], cfg["NTILES"], cfg["OWN0"], cfg["NSEQ"], cfg["TS"]
    nslots = NT * NTILES
    nown = (NTILES - OWN0) * NT
    npart = OWN0 * NT
    xp = np.asarray(inp["x_prompt"], np.float32)
    xs = np.asarray(inp["x_sample"], np.float32)
    past = cfg["PAST"]

    def pk(a):
        a = np.asarray(a, np.float32)
        lead = a.shape[:-1]
        return np.ascontiguousarray(np.moveaxis(a.reshape(lead + (KC, 128)), -1, 0))

    gains = np.concatenate([inp["ffn_a_norm"], inp["mix_norm"], inp["ffn_b_norm"], np.asarray(inp["kv_norm"])[None]], 0)
    gains_p = pk(gains)
    convw_p = pk(inp["conv_w"])
    knorm_p = np.ascontiguousarray(np.broadcast_to(np.asarray(inp["k_norm"], np.float32)[None], (128, 3, HD)))
    shared = {kname: np.asarray(inp[kname], np.float32) for kname in
              ["ffn_a_w_in", "ffn_a_w_out", "ffn_b_w_in", "ffn_b_w_out", "conv_w_in", "conv_w_out", "w_kv"]}
    for kname in ["nsa_w_qg", "nsa_w_o", "cmp_w1", "cmp_w2"]:
        shared[kname] = np.asarray(inp[kname], np.float32)
    ident = np.eye(128, dtype=np.float32)
    kk_, qq_ = np.meshgrid(np.arange(128), np.arange(128), indexing="ij")
    tri = np.stack([(kk_ <= qq_), (kk_ >= qq_)], axis=1).astype(np.float32)
    etab = (np.arange(nslots)[None, :] // 64 == np.arange(64)[:, None]).astype(np.float32)
    qnorm_p = np.ascontiguousarray(np.broadcast_to(np.asarray(inp["nsa_q_norm"], np.float32)[None], (128, 2, HD)))
    kn0col = np.ascontiguousarray(np.asarray(inp["k_norm"], np.float32)[0][:, None])
    pe = np.asarray(inp["cmp_pe"], np.float32)
    peT = np.ascontiguousarray(np.tile(pe.transpose(2, 0, 1), (1, 1, 4)))
    npool = cfg["NPOOL"]
    cache2 = np.ascontiguousarray(np.asarray(inp["cache_kv"], np.float32)).reshape(npool * 256, 512)
    ptab = np.asarray(inp["page_table"], np.int32)
    npg = past // 128
    pidx = (2.0 * np.arange(128, dtype=np.float32))[:, None]
    ii = np.arange(TS)
    wmasks = np.zeros((128, 3, 32), np.float32)
    rr = np.arange(128)
    wmasks[:, 0, :] = np.tile((rr[:, None] >= ii[None, :]).astype(np.float32), (1, 4))
    wmasks[:, 1, :] = np.tile(((rr[:, None] <= ii[None, :]) & (rr[:, None] < TS)).astype(np.float32), (1, 4))
    wmasks[:, 2, :] = (rr[:, None] < TS)
    bmat = np.tile(np.eye(8, dtype=np.float32), (1, 4))
    nbs = past // 64 + 1
    spos = past + ii
    sb = np.arange(136)
    scur = (spos // 64)[:, None]
    svalid = (sb[None, :] < nbs) & (64 * sb[None, :] <= spos[:, None])
    sf = np.zeros((TS, 136), np.float32)
    sf = np.where(sb[None, :] == 0, 1e4, sf)
    sf = np.where(sb[None, :] == scur, 2e4, sf)
    sf = np.where(sb[None, :] == scur - 1, 3e4, sf)
    saddc = np.where(svalid, np.where(sf > 0, sf, 0.0), -1e30).astype(np.float32)
    smulc = (svalid & ~(sf > 0)).astype(np.float32)
    in_maps = []
    for c in range(ncores):
        seq, half = c // 2, c % 2
        own = xp[seq, half * nown:(half + 1) * nown]
        if half == 1:
            partner = xp[seq, 0:npart]
            pos0 = 0
        else:
            partner = np.zeros((npart, D), np.float32)
            pos0 = -npart
        xall = np.concatenate([partner, own], 0)
        pos = pos0 + np.arange(nslots)
        rope_p = _rope_table(np.maximum(pos, 0)).reshape(nslots // 128, 128, 16).transpose(1, 0, 2)
        xs_c = xs[c * NSEQ:(c + 1) * NSEQ].reshape(NSEQ * TS, D)
        rope_s = _rope_table(np.tile(past + np.arange(TS), NSEQ))
        cst = np.asarray(inp["state_conv"], np.float32)[:, c * NSEQ:(c + 1) * NSEQ]
        cst_p = pk(cst)
        cst_p = np.ascontiguousarray(cst_p.transpose(0, 1, 4, 2, 3))
        NCH, NCB, NBLK, NOT = nslots // 128, nslots // 32, nslots // 64, NTILES - OWN0
        flags = np.zeros((128, NCH + 1), np.float32)
        flags[:, :NCH] = (pos >= 0).reshape(NCH, 128).T
        creal = (32 * np.arange(NCB) + pos0 >= 0)
        flags[:NCB, NCH] = creal
        sq = (OWN0 * NT + np.arange(NOT * NT)).reshape(NOT, 4, 128)
        cend = 32 * np.arange(NCB) + 31
        cm = ((cend[None, None, None, :] <= sq[..., None]) & creal[None, None, None, :]).astype(np.float32)
        cmq = np.ascontiguousarray(cm.transpose(0, 2, 1, 3))
        cmt = np.zeros((NOT, 128, 4, 128), np.float32)
        cmt[:, :NCB] = cm.transpose(0, 3, 1, 2)
        bl = np.arange(NBLK)
        breal = (64 * bl + pos0 >= 0)
        b0 = (-pos0) // 64
        cur = (sq // 64)[..., None]
        valid = breal[None, None, None, :] & (64 * bl[None, None, None, :] <= sq[..., None])
        fval = np.zeros(valid.shape, np.float32)
        fval = np.where(bl[None, None, None, :] == b0, 1e4, fval)
        fval = np.where(bl[None, None, None, :] == cur, 2e4, fval)
        fval = np.where(bl[None, None, None, :] == cur - 1, 3e4, fval)
        forced = fval > 0
        addc = np.where(valid, np.where(forced, fval, 0.0), -1e30).astype(np.float32)
        mulc = (valid & ~forced).astype(np.float32)
        addc = np.ascontiguousarray(addc.transpose(0, 2, 1, 3))
        mulc = np.ascontiguousarray(mulc.transpose(0, 2, 1, 3))
        m = dict(shared)
        m.update(cache_kv=cache2, page_table=np.ascontiguousarray(ptab[c * NSEQ:(c + 1) * NSEQ].reshape(1, NSEQ * npg)),
                 pidx=pidx, wmasks=wmasks, bmat=bmat, smulc=smulc, saddc=saddc)
        m.update(ident=ident, tri=tri, etab=etab, flags=flags, qnorm=qnorm_p, kn0col=kn0col, peT=peT,
                 cmq=cmq, cmt=cmt, mulc=mulc, addc=addc)
        m.update(xT=np.ascontiguousarray(xall.T), xsT=np.ascontiguousarray(xs_c.T), convst=cst_p,
                 cwin=np.ascontiguousarray(np.asarray(inp["cache_win"], np.float32)[c * NSEQ:(c + 1) * NSEQ].reshape(NSEQ, 512, 512)),
                 gains=gains_p, convw=convw_p, knorm=knorm_p, rope_p=np.ascontiguousarray(rope_p), rope_s=rope_s)
        in_maps.append(m)
    return in_maps


def assemble(res, inp, cfg, ncores=8):
    NT, NTILES, OWN0, NSEQ, TS = cfg["NT"], cfg["NTILES"], cfg["OWN0"], cfg["NSEQ"], cfg["TS"]
    nown = (NTILES - OWN0) * NT
    B = ncores // 2
    SEQ = 2 * nown
    y_p = np.zeros((B, SEQ, D), np.float32)
    y_s = np.zeros((ncores * NSEQ, TS, D), np.float32)
    kv_p = np.zeros((B, SEQ, 4, NKV, HD), np.float32)
    kv_s = np.zeros((ncores * NSEQ, TS, 4, NKV, HD), np.float32)
    win_p = np.zeros((B, 512, 2, NKV, HD), np.float32)
    win_s = np.zeros((ncores * NSEQ, 512, 2, NKV, HD), np.float32)
    conv_p = np.zeros((2, B, 2, D), np.float32)
    conv_s = np.zeros((2, ncores * NSEQ, 2, D), np.float32)
    for c in range(ncores):
        r = res[c]
        seq, half = c // 2, c % 2
        y_p[seq, half * nown:(half + 1) * nown] = r["o_yT"].T
        y_s[c * NSEQ:(c + 1) * NSEQ] = r["o_ysT"].T.reshape(NSEQ, TS, D)
        kv_p[seq, half * nown:(half + 1) * nown] = r["o_kv"].reshape(nown, 4, NKV, HD)
        kv_s[c * NSEQ:(c + 1) * NSEQ] = r["o_kvs"].reshape(NSEQ, TS, 4, NKV, HD)
        if half == 1:
            win_p[seq] = r["o_win"].reshape(512, 2, NKV, HD)
            conv_p[:, seq] = r["o_conv"].transpose(0, 3, 2, 1).reshape(2, 2, D)
        conv_s[:, c * NSEQ:(c + 1) * NSEQ] = r["o_convs"].transpose(0, 3, 4, 2, 1).reshape(2, NSEQ, 2, D)
        win_s[c * NSEQ:(c + 1) * NSEQ] = r["o_wins"].reshape(NSEQ, 512, 2, NKV, HD)
    return (y_p, y_s, kv_p, kv_s, win_p, win_s, conv_p, conv_s)


def kernel(**inputs):
    cfg = FULL_CFG
    b = Builder(cfg)
    nc = b.build()
    in_maps = prepare_inputs(inputs, cfg)
    res = run_bass_kernel_spmd(nc, in_maps, core_ids=list(range(8)))
    return assemble(res.results, inputs, cfg)
```
